# Optimizing a Trainium2 kernel written in Bass

```python
import math
import jax
import jax.numpy as jnp
from jax import lax
import numpy as np

D_MODEL = 1024
BATCH = 8
SEQ = 2048
DEPTH = 4
DEC_BATCH = 128
DEC_SEQ = 8
PAST_LEN = 16384
PAGE_SIZE = 128

W_BR = D_MODEL // 2
N_BRANCH = 4
CONV_W = 3
S5_GROUP = 16
S5_GROUPS = W_BR // S5_GROUP
S5_STATE = 64
HG_HEADS = 4
HG_DK = W_BR // HG_HEADS
HG_DV = W_BR // HG_HEADS
GLA_HEADS = 4
GLA_DK = W_BR // (2 * GLA_HEADS)
GLA_DV = W_BR // GLA_HEADS
GLA_RANK = 16
GLA_GATE_NORM = 16.0
CHUNK = 64
EPS = 1e-6
SIZES = (W_BR, W_BR, W_BR, W_BR,
         W_BR, W_BR,
         HG_HEADS * HG_DK, HG_HEADS * HG_DK, HG_HEADS * HG_DV, W_BR,
         GLA_HEADS * GLA_DK, GLA_HEADS * GLA_DK, GLA_HEADS * GLA_DV, W_BR, GLA_RANK,
         N_BRANCH * D_MODEL)
D_IN = sum(SIZES)

kernel_name = 'parallel_gated_hybrid_conv_s5_hgrn2_gla_step'


def rmsnorm(x, g):
    xf = x.astype(jnp.float32)
    xf = xf * lax.rsqrt(jnp.mean(xf * xf, axis=-1, keepdims=True) + EPS)
    return xf * g.astype(jnp.float32)


def split_points():
    pts, s = [], 0
    for n in SIZES[:-1]:
        s += n
        pts.append(s)
    return pts


def short_conv(u, buf, w):
    L = u.shape[1]
    ext = jnp.concatenate([buf.astype(u.dtype), u], axis=1)
    y = ext[:, 0:L] * w[0]
    for j in range(1, CONV_W):
        y = y + ext[:, j:j + L] * w[j]
    return y, ext[:, L:]


def cmul(ar, ai, br, bi):
    return ar * br - ai * bi, ar * bi + ai * br


def s5_ssm(u, h0_re, h0_im, a_re, a_im, log_dt, b_re, b_im, c_re, c_im, d):
    f32 = jnp.float32
    bsz, L, _ = u.shape
    uf = u.astype(f32).reshape(bsz, L, S5_GROUPS, S5_GROUP)
    ar = a_re.astype(f32)
    ai = a_im.astype(f32)
    dt = jnp.exp(log_dt.astype(f32))[:, None]
    mag = jnp.exp(dt * ar)
    abar_r = mag * jnp.cos(dt * ai)
    abar_i = mag * jnp.sin(dt * ai)
    den = ar * ar + ai * ai
    zr = ((abar_r - 1.0) * ar + abar_i * ai) / den
    zi = (abar_i * ar - (abar_r - 1.0) * ai) / den
    bbar_r, bbar_i = cmul(zr[..., None], zi[..., None], b_re.astype(f32), b_im.astype(f32))
    bu_r = jnp.einsum('gnp,blgp->blgn', bbar_r, uf)
    bu_i = jnp.einsum('gnp,blgp->blgn', bbar_i, uf)
    i_r, i_i = cmul(abar_r, abar_i, h0_re.astype(f32), h0_im.astype(f32))
    bu_r = bu_r.at[:, 0].add(i_r)
    bu_i = bu_i.at[:, 0].add(i_i)
    a_r = jnp.broadcast_to(abar_r, (1, L, S5_GROUPS, S5_STATE))
    a_i = jnp.broadcast_to(abar_i, (1, L, S5_GROUPS, S5_STATE))

    def combine(e1, e2):
        a1r, a1i, b1r, b1i = e1
        a2r, a2i, b2r, b2i = e2
        nar, nai = cmul(a2r, a2i, a1r, a1i)
        nbr, nbi = cmul(a2r, a2i, b1r, b1i)
        return nar, nai, nbr + b2r, nbi + b2i

    _, _, h_r, h_i = lax.associative_scan(combine, (a_r, a_i, bu_r, bu_i), axis=1)
    y = (jnp.einsum('gpn,blgn->blgp', c_re.astype(f32), h_r)
         - jnp.einsum('gpn,blgn->blgp', c_im.astype(f32), h_i))
    y = y + d.astype(f32).reshape(S5_GROUPS, S5_GROUP) * uf
    return y.reshape(bsz, L, W_BR), h_r[:, -1], h_i[:, -1]


def gated_recurrence(q, k, v, logf, s0):
    f32 = jnp.float32
    bsz, L, H, K = q.shape
    V = v.shape[-1]
    c = math.gcd(L, CHUNK)
    n = L // c

    def to_chunks(t):
        return jnp.moveaxis(t.astype(f32).reshape(bsz, n, c, H, t.shape[-1]), 1, 0)

    mask = jnp.tril(jnp.ones((c, c), dtype=bool))[None, :, :, None, None]

    def step(S, inp):
        qc, kc, vc, gc = inp
        b = jnp.cumsum(gc, axis=1)
        o_inter = jnp.einsum('bthk,bhkv->bthv', qc * jnp.exp(b), S)
        diff = b[:, :, None] - b[:, None, :]
        decay = jnp.exp(jnp.where(mask, diff, -jnp.inf))
        att = jnp.einsum('bthk,bshk,btshk->bhts', qc, kc, decay)
        o_intra = jnp.einsum('bhts,bshv->bthv', att, vc)
        b_last = b[:, -1]
        k_dec = kc * jnp.exp(b_last[:, None] - b)
        S = jnp.exp(b_last)[..., None] * S + jnp.einsum('bshk,bshv->bhkv', k_dec, vc)
        return S, o_inter + o_intra

    S, o = lax.scan(step, s0.astype(f32), (to_chunks(q), to_chunks(k), to_chunks(v), to_chunks(logf)))
    o = jnp.moveaxis(o, 0, 1).reshape(bsz, L, H, V)
    return o, S


def trunk(x, st_conv, st_re, st_im, st_hg, st_gla, norm_w, w_in, conv_w, s5_a_re, s5_a_im,
          s5_log_dt, s5_b_re, s5_b_im, s5_c_re, s5_c_im, s5_d, w_glu, b_glu, hgrn_lb_raw,
          hgrn_norm, w_gk, b_gk, gla_norm, w_branch, w_out, final_norm):
    f32 = jnp.float32
    bsz, L, _ = x.shape
    dt = x.dtype
    lb_cum = jnp.cumsum(jax.nn.softmax(hgrn_lb_raw.astype(f32), axis=0), axis=0)
    lb_all = lb_cum - lb_cum[0:1]
    pts = split_points()
    new_conv, new_re, new_im, new_hg, new_gla = [], [], [], [], []
    for l in range(DEPTH):
        h = rmsnorm(x, norm_w[l]).astype(dt)
        proj = h @ w_in[l]
        (a_x, a_b, a_c, a_z, s_u, s_z, c_q, c_f, c_i, c_z,
         d_q, d_k, d_v, d_z, d_r, m_logit) = jnp.split(proj, pts, axis=-1)
        conv_out, cbuf = short_conv(a_c * a_x, st_conv[l], conv_w[l])
        y_a = a_b * conv_out * jax.nn.silu(a_z)
        y_s, h_r, h_i = s5_ssm(s_u, st_re[l], st_im[l], s5_a_re[l], s5_a_im[l], s5_log_dt[l],
                               s5_b_re[l], s5_b_im[l], s5_c_re[l], s5_c_im[l], s5_d[l])
        sg = jax.nn.gelu(y_s)
        glu = sg * jax.nn.sigmoid(sg @ w_glu[l].astype(f32) + b_glu[l].astype(f32))
        y_b = glu.astype(dt) * jax.nn.silu(s_z)
        lb = lb_all[l].reshape(HG_HEADS, HG_DK)
        fr = c_f.astype(f32).reshape(bsz, L, HG_HEADS, HG_DK)
        logf_c = jnp.logaddexp(jnp.log(lb), jnp.log1p(-lb) + jax.nn.log_sigmoid(fr))
        k_c = (1.0 - lb) * jax.nn.sigmoid(-fr)
        q_c = jax.nn.silu(c_q.astype(f32)).reshape(bsz, L, HG_HEADS, HG_DK) * (HG_DK ** -0.5)
        v_c = c_i.reshape(bsz, L, HG_HEADS, HG_DV)
        o_c, S_c = gated_recurrence(q_c, k_c, v_c, logf_c, st_hg[l])
        y_c = (rmsnorm(o_c, hgrn_norm[l]).reshape(bsz, L, W_BR) * jax.nn.silu(c_z.astype(f32))).astype(dt)
        gk = jax.nn.log_sigmoid((d_r @ w_gk[l] + b_gk[l]).astype(f32)) / GLA_GATE_NORM
        gk = gk.reshape(bsz, L, GLA_HEADS, GLA_DK)
        q_d = d_q.astype(f32).reshape(bsz, L, GLA_HEADS, GLA_DK) * (GLA_DK ** -0.5)
        k_d = d_k.reshape(bsz, L, GLA_HEADS, GLA_DK)
        v_d = d_v.reshape(bsz, L, GLA_HEADS, GLA_DV)
        o_d, S_d = gated_recurrence(q_d, k_d, v_d, gk, st_gla[l])
        y_d = (rmsnorm(o_d, gla_norm[l]).reshape(bsz, L, W_BR) * jax.nn.silu(d_z.astype(f32))).astype(dt)
        branches = jnp.stack([y_a, y_b, y_c, y_d], axis=2)
        p = jnp.einsum('blnw,nwd->blnd', branches, w_branch[l])
        gates = jax.nn.sigmoid(m_logit.reshape(bsz, L, N_BRANCH, D_MODEL))
        merged = jnp.sum(gates * p, axis=2)
        x = x + merged @ w_out[l]
        new_conv.append(cbuf.astype(dt))
        new_re.append(h_r.astype(dt))
        new_im.append(h_i.astype(dt))
        new_hg.append(S_c.astype(dt))
        new_gla.append(S_d.astype(dt))
    y = rmsnorm(x, final_norm).astype(dt)
    return (y, jnp.stack(new_conv, 0), jnp.stack(new_re, 0), jnp.stack(new_im, 0),
            jnp.stack(new_hg, 0), jnp.stack(new_gla, 0))


def setup_inputs(seed: int = 0) -> dict:
    key = jax.random.key(seed)
    ks = jax.random.split(key, 32)
    nrm = jax.random.normal
    f32 = jnp.float32
    n_idx = jnp.arange(S5_STATE, dtype=f32)
    return {
        'x_prompt': nrm(ks[0], (BATCH, SEQ, D_MODEL), f32),
        'x_sample': nrm(ks[1], (DEC_BATCH, DEC_SEQ, D_MODEL), f32),
        'state_conv': nrm(ks[2], (DEPTH, DEC_BATCH, CONV_W - 1, W_BR), f32),
        'state_ssm_re': 0.5 * nrm(ks[3], (DEPTH, DEC_BATCH, S5_GROUPS, S5_STATE), f32),
        'state_ssm_im': 0.5 * nrm(ks[4], (DEPTH, DEC_BATCH, S5_GROUPS, S5_STATE), f32),
        'state_hgrn': 0.5 * nrm(ks[5], (DEPTH, DEC_BATCH, HG_HEADS, HG_DK, HG_DV), f32),
        'state_gla': nrm(ks[6], (DEPTH, DEC_BATCH, GLA_HEADS, GLA_DK, GLA_DV), f32),
        'norm_w': 1.0 + 0.02 * nrm(ks[7], (DEPTH, D_MODEL), f32),
        'w_in': nrm(ks[8], (DEPTH, D_MODEL, D_IN), f32) * D_MODEL ** -0.5,
        'conv_w': nrm(ks[9], (DEPTH, CONV_W, W_BR), f32) * CONV_W ** -0.5,
        's5_a_re': -0.5 + 0.01 * nrm(ks[10], (DEPTH, S5_GROUPS, S5_STATE), f32),
        's5_a_im': math.pi * n_idx + 0.01 * nrm(ks[11], (DEPTH, S5_GROUPS, S5_STATE), f32),
        's5_log_dt': jax.random.uniform(ks[12], (DEPTH, S5_GROUPS), f32, math.log(1e-3), math.log(1e-1)),
        's5_b_re': nrm(ks[13], (DEPTH, S5_GROUPS, S5_STATE, S5_GROUP), f32) * (2 * S5_GROUP) ** -0.5,
        's5_b_im': nrm(ks[14], (DEPTH, S5_GROUPS, S5_STATE, S5_GROUP), f32) * (2 * S5_GROUP) ** -0.5,
        's5_c_re': nrm(ks[15], (DEPTH, S5_GROUPS, S5_GROUP, S5_STATE), f32) * (2 * S5_STATE) ** -0.5,
        's5_c_im': nrm(ks[16], (DEPTH, S5_GROUPS, S5_GROUP, S5_STATE), f32) * (2 * S5_STATE) ** -0.5,
        's5_d': nrm(ks[17], (DEPTH, W_BR), f32),
        'w_glu': nrm(ks[18], (DEPTH, W_BR, W_BR), f32) * W_BR ** -0.5,
        'b_glu': 0.01 * nrm(ks[19], (DEPTH, W_BR), f32),
        'hgrn_lb_raw': 0.1 * nrm(ks[20], (DEPTH, HG_HEADS * HG_DK), f32),
        'hgrn_norm': 1.0 + 0.02 * nrm(ks[21], (DEPTH, HG_DV), f32),
        'w_gk': nrm(ks[22], (DEPTH, GLA_RANK, GLA_HEADS * GLA_DK), f32) * GLA_RANK ** -0.5,
        'b_gk': 0.01 * nrm(ks[23], (DEPTH, GLA_HEADS * GLA_DK), f32),
        'gla_norm': 1.0 + 0.02 * nrm(ks[24], (DEPTH, GLA_DV), f32),
        'w_branch': nrm(ks[25], (DEPTH, N_BRANCH, W_BR, D_MODEL), f32) * W_BR ** -0.5,
        'w_out': nrm(ks[26], (DEPTH, D_MODEL, D_MODEL), f32) * D_MODEL ** -0.5,
        'final_norm': 1.0 + 0.02 * nrm(ks[27], (D_MODEL,), f32),
    }


def reference(x_prompt, x_sample, state_conv, state_ssm_re, state_ssm_im, state_hgrn, state_gla,
              norm_w, w_in, conv_w, s5_a_re, s5_a_im, s5_log_dt, s5_b_re, s5_b_im, s5_c_re, s5_c_im,
              s5_d, w_glu, b_glu, hgrn_lb_raw, hgrn_norm, w_gk, b_gk, gla_norm, w_branch, w_out,
              final_norm):
    dt = x_prompt.dtype
    bp = x_prompt.shape[0]
    z_conv = jnp.zeros((DEPTH, bp, CONV_W - 1, W_BR), dt)
    z_ssm = jnp.zeros((DEPTH, bp, S5_GROUPS, S5_STATE), dt)
    z_hg = jnp.zeros((DEPTH, bp, HG_HEADS, HG_DK, HG_DV), dt)
    z_gla = jnp.zeros((DEPTH, bp, GLA_HEADS, GLA_DK, GLA_DV), dt)
    y_prompt, conv_p, re_p, im_p, hg_p, gla_p = trunk(
        x_prompt, z_conv, z_ssm, z_ssm, z_hg, z_gla, norm_w, w_in, conv_w, s5_a_re, s5_a_im,
        s5_log_dt, s5_b_re, s5_b_im, s5_c_re, s5_c_im, s5_d, w_glu, b_glu, hgrn_lb_raw,
        hgrn_norm, w_gk, b_gk, gla_norm, w_branch, w_out, final_norm)
    y_sample, conv_s, re_s, im_s, hg_s, gla_s = trunk(
        x_sample, state_conv, state_ssm_re, state_ssm_im, state_hgrn, state_gla, norm_w, w_in,
        conv_w, s5_a_re, s5_a_im, s5_log_dt, s5_b_re, s5_b_im, s5_c_re, s5_c_im, s5_d, w_glu,
        b_glu, hgrn_lb_raw, hgrn_norm, w_gk, b_gk, gla_norm, w_branch, w_out, final_norm)
    return (y_prompt, y_sample, conv_p, conv_s, re_p, re_s, im_p, im_s, hg_p, hg_s, gla_p, gla_s)
```

```python
import math
import os
from contextlib import ExitStack

import numpy as np
import concourse.bass as bass
import concourse.mybir as mybir
from concourse.bass_utils import run_bass_kernel_spmd

F32 = mybir.dt.float32
BF16 = mybir.dt.bfloat16
I32 = mybir.dt.int32
ALU = mybir.AluOpType
AF = mybir.ActivationFunctionType

D = 1024
NP_ = 2048
NS_ = 128
NT = NP_ + NS_
DEPTH = 4
WBR = 512
DIN = 10768
NSEQ = 16
EPS = 1e-6
TC = [(0, 512), (512, 512), (1024, 512), (1536, 512), (2048, 128)]
A_X, A_B, A_C, A_Z = 0, 512, 1024, 1536
S_U, S_Z = 2048, 2560
C_Q, C_F, C_I, C_Z = 3072, 3584, 4096, 4608
G_Q, G_K, G_V, G_Z, G_R = 5120, 5376, 5632, 6144, 6656
M_G = 6672
TWO_PI = 2.0 * math.pi

ENGS = ("pe", "act", "dve", "pool", "sp")


class Res:
    __slots__ = ("name", "w", "r")

    def __init__(self, name):
        self.name = name
        self.w = None
        self.r = {}


class Prog:
    def __init__(self, nc, stack):
        self.nc = nc
        self.stack = stack
        self.streams = {e: [] for e in ENGS}
        self.sems = {}
        self.count = {}
        self.known = {e: {} for e in ENGS}
        for e in ("pe", "act", "dve", "pool"):
            self._newsem("eng_" + e)
        self.final_waits = {}
        self.dry = False
        self.marks = []

    def _newsem(self, key):
        s = self.stack.enter_context(self.nc.semaphore(key))
        self.sems[key] = s
        self.count[key] = 0
        return s

    def _deps(self, eng, reads, writes, own=None):
        need = {}

        def add(tok):
            if tok is None:
                return
            k, v = tok
            if need.get(k, 0) < v:
                need[k] = v
        for r in reads:
            add(r.w)
        for w in writes:
            if not (own is not None and w.w is not None and w.w[0] == own):
                add(w.w)
            for k, v in w.r.items():
                add((k, v))
        waits = []
        kn = self.known[eng]
        for k, v in need.items():
            if k == "eng_pe" and eng == "pe":
                continue
            if kn.get(k, 0) >= v:
                continue
            kn[k] = v
            waits.append((k, v))
        return waits

    def _commit(self, tok, reads, writes):
        k, v = tok
        for r in reads:
            if r.r.get(k, 0) < v:
                r.r[k] = v
        for w in writes:
            w.w = tok
            w.r = {}

    def op(self, eng, fn, reads=(), writes=()):
        if self.dry:
            return
        waits = self._deps(eng, reads, writes)
        k = "eng_" + eng
        self.count[k] += 1
        tok = (k, self.count[k])
        self.streams[eng].append((waits, fn, (k, 1)))
        self._commit(tok, reads, writes)

    def mark(self, name):
        if not self.dry:
            self.marks.append((name, {e: len(st_) for e, st_ in self.streams.items()}))

    def I(self, eng, name, *args, reads=(), writes=(), **kw):
        self.op(eng, (name, args, kw), reads, writes)

    def dma(self, q, out, in_, reads=(), writes=(), sem="g", final=False, **kw):
        if self.dry:
            return
        k = "dma_" + sem
        waits = self._deps(q, reads, writes, own=k)
        if k not in self.sems:
            self._newsem(k)
        self.count[k] += 16
        tok = (k, self.count[k])
        self.streams[q].append(
            (waits, lambda e, o=out, i=in_, kw=kw: e.dma_start(out=o, in_=i, **kw), (k, 16)))
        self._commit(tok, reads, writes)
        if final:
            self.final_waits[k] = self.count[k]

    def emit(self):
        nc = self.nc
        fw = list(self.final_waits.items())
        with nc.Block() as block:
            def replay(e, name, extra=()):
                for waits, fn, inc in self.streams[name]:
                    for k, v in waits:
                        e.wait_ge(self.sems[k], v)
                    if isinstance(fn, tuple):
                        ins = getattr(e, fn[0])(*fn[1], **fn[2])
                    else:
                        ins = fn(e)
                    ins.then_inc(self.sems[inc[0]], inc[1])
                for k, v in extra:
                    e.wait_ge(self.sems[k], v)

            @block.tensor
            def _(e):
                replay(e, "pe")

            @block.scalar
            def _(e):
                replay(e, "act")

            @block.vector
            def _(e):
                replay(e, "dve")

            @block.gpsimd
            def _(e):
                replay(e, "pool")

            @block.sync
            def _(e):
                replay(e, "sp", fw)


class WS:
    AHEAD = 2

    def __init__(self, P, slots, slot_elems):
        self.P = P
        self.slots = slots
        self.res = [Res("wslot%d" % i) for i in range(len(slots))]
        self.slot_elems = slot_elems
        self.plan = []
        self.reset()

    def reset(self):
        self.free = list(range(len(self.slots)))
        self.issued = 0
        self.taken = 0
        self.slot_of = {}

    def _views(self, slot, parts):
        t = self.slots[slot]
        off = 0
        views = []
        for ap in parts:
            shp = list(ap.shape)
            n = 1
            for s_ in shp[1:]:
                n *= s_
            v = t[0:shp[0], off:off + n]
            if len(shp) == 3:
                v = v.rearrange("p (a n) -> p a n", a=shp[1])
            views.append(v)
            off += n
        assert off <= self.slot_elems, off
        return views

    def _pump(self):
        while (self.issued < len(self.plan) and self.free
               and self.issued < self.taken + self.AHEAD):
            i = self.issued
            slot = self.free.pop(0)
            self.slot_of[i] = slot
            parts = self.plan[i]
            for ap, v in zip(parts, self._views(slot, parts)):
                self.P.dma("pool", v, ap, writes=[self.res[slot]], sem="w%d" % slot)
            self.issued += 1

    def take(self, parts):
        if self.P.dry:
            self.plan.append(parts)
            return self._views(0, parts), self.res[0], -1
        i = self.taken
        self.taken += 1
        self._pump()
        if i not in self.slot_of:
            raise RuntimeError("weight stream: no free slot for block %d" % i)
        slot = self.slot_of[i]
        return self._views(slot, self.plan[i]), self.res[slot], i

    def release(self, i):
        if self.P.dry:
            return
        self.free.append(self.slot_of[i])
        self._pump()


DEBUG = bool(int(os.environ.get("MK_DEBUG", "0")))
LAST_RESULTS = None
NWK = 22
SLOT_E = 4224
NSLOT = 4
GELU_C = 1.5957691216057308
SIN_SCALE = TWO_PI * (1.0 - 2e-6)
TW = 256


def build_program(stage=99):
    nc = bass.Bass("TRN2", target_bir_lowering=False)

    def din(name, shape, dt=F32):
        return nc.dram_tensor(name, list(shape), dt, kind="ExternalInput").ap()

    def dout(name, shape):
        return nc.dram_tensor(name, list(shape), F32, kind="ExternalOutput").ap()

    xp = din("xp", [NP_, D])
    xs = din("xs", [NS_, D])
    st_conv = din("st_conv", [DEPTH, NSEQ, 2, WBR])
    st_re = din("st_re", [DEPTH, NSEQ, 32, 64])
    st_im = din("st_im", [DEPTH, NSEQ, 32, 64])
    st_hg = din("st_hg", [DEPTH, NSEQ, 4, 128, 128])
    st_gla = din("st_gla", [DEPTH, NSEQ, 4, 64, 128])
    norm_w = din("norm_w", [DEPTH, D])
    w_in = din("w_in", [DEPTH, D, DIN])
    conv_w = din("conv_w", [DEPTH, 3, WBR])
    s5_a_re = din("s5_a_re", [DEPTH, 32, 64])
    s5_a_im = din("s5_a_im", [DEPTH, 32, 64])
    s5_log_dt = din("s5_log_dt", [DEPTH, 32])
    s5_b_re = din("s5_b_re", [DEPTH, 32, 64, 16])
    s5_b_im = din("s5_b_im", [DEPTH, 32, 64, 16])
    s5_c_re = din("s5_c_re", [DEPTH, 32, 16, 64])
    s5_c_im = din("s5_c_im", [DEPTH, 32, 16, 64])
    s5_d = din("s5_d", [DEPTH, WBR])
    w_glu = din("w_glu", [DEPTH, WBR, WBR])
    b_glu = din("b_glu", [DEPTH, WBR])
    hgrn_lb_raw = din("hgrn_lb_raw", [DEPTH, WBR])
    hgrn_norm = din("hgrn_norm", [DEPTH, 128])
    w_gk = din("w_gk", [DEPTH, 16, 256])
    b_gk = din("b_gk", [DEPTH, 256])
    gla_norm = din("gla_norm", [DEPTH, 128])
    w_branch = din("w_branch", [DEPTH, 4, WBR, D])
    w_out = din("w_out", [DEPTH, D, D])
    final_norm = din("final_norm", [D])
    c_ident = din("c_ident", [128, 128])
    c_ones = din("c_ones", [128, 128])
    c_mask64 = din("c_mask64", [64, 64])
    c_cm = din("c_cm", [128, 640])
    c_tp1 = din("c_tp1", [128, 256])

    o_yp = dout("o_yp", [NP_, D])
    o_ys = dout("o_ys", [NS_, D])
    o_conv_p = dout("o_conv_p", [DEPTH, 2, WBR])
    o_conv_s = dout("o_conv_s", [DEPTH, NSEQ, 2, WBR])
    o_re_p = dout("o_re_p", [DEPTH, 2048])
    o_re_s = dout("o_re_s", [DEPTH, NSEQ, 2048])
    o_im_p = dout("o_im_p", [DEPTH, 2048])
    o_im_s = dout("o_im_s", [DEPTH, NSEQ, 2048])
    o_hg_p = dout("o_hg_p", [DEPTH, 4, 128, 128])
    o_hg_s = dout("o_hg_s", [DEPTH, NSEQ, 4, 128, 128])
    o_gla_p = dout("o_gla_p", [DEPTH, 4, 64, 128])
    o_gla_s = dout("o_gla_s", [DEPTH, NSEQ, 4, 64, 128])

    st = ExitStack()
    with st:
        P = Prog(nc, st)
        I = P.I

        def sb(name, shape, dt=F32):
            return st.enter_context(nc.sbuf_tensor(name, list(shape), dt))

        xT = sb("xT", [128, 8, NT]); r_xT = [Res("xT%d" % i) for i in range(len(TC))]
        hT = sb("hT", [128, 8, NT], BF16); r_hT = [Res("hT%d" % i) for i in range(len(TC))]
        yT = sb("yT", [128, 4, NT], BF16); r_yT = [Res("yT%d" % i) for i in range(len(TC))]
        wslots = [sb("wslot%d" % i, [128, SLOT_E], BF16) for i in range(NSLOT)]
        ws = WS(P, wslots, SLOT_E)
        banks = [st.enter_context(nc.psum_tensor("bank%d" % i, [128, 512], F32)) for i in range(8)]
        r_breg = [[Res("bank%d_%d" % (i, q)) for q in range(8)] for i in range(8)]

        def rb(i, c0=0, n=512):
            return r_breg[i][c0 // 64:(c0 + n + 63) // 64]

        rr = {"i": 0}

        def bank():
            i = rr["i"] % 8
            rr["i"] += 1
            return i

        AR = sb("AR", [128, NWK * 512])
        r_ar = [Res("ar%d" % i) for i in range(NWK)]

        def T(i, n=512, off=0):
            return AR[:, i * 512 + off:i * 512 + off + n]

        def TB(i, n=1024, off=0, nt=1):
            return AR[:, i * 512:(i + nt) * 512].bitcast(BF16)[:, off:off + n]

        def RT(i, nt=1):
            return r_ar[i:i + nt]

        ident_f = sb("ident_f", [128, 128]); r_identf = Res("identf")
        ident_b = sb("ident_b", [128, 128], BF16); r_identb = Res("identb")
        ones_b = sb("ones_b", [128, 128], BF16); r_ones = Res("ones")
        mask64 = sb("mask64", [64, 64]); r_mask = Res("mask64")
        cm = sb("cm", [128, 640]); r_cm = Res("cm")
        tp1 = sb("tp1", [128, 256]); r_tp1 = Res("tp1")
        normw = sb("normw", [128, DEPTH, 8]); r_par = Res("params")
        fnorm = sb("fnorm", [128, 8])
        cw = sb("cw", [128, DEPTH, 3, 4])
        s5r = sb("s5r", [128, 64]); s5thn = sb("s5thn", [128, 64]); s5zr = sb("s5zr", [128, 64]); s5zi = sb("s5zi", [128, 64])
        s5dd = sb("s5dd", [128, DEPTH, 4])
        bglu = sb("bglu", [128, DEPTH, 4])
        lbp = sb("lbp", [128, DEPTH, 4]); oml = sb("oml", [128, DEPTH, 4])
        gn_h = sb("gn_h", [128, DEPTH]); gn_g = sb("gn_g", [128, DEPTH])
        nbgk = sb("nbgk", [64, DEPTH, 4])
        wgk = sb("wgk", [16, DEPTH, 256], BF16); r_wgk = Res("wgk")
        Hst = sb("Hst", [128, 16, 2]); r_H = Res("H")
        epsb = sb("epsb", [128, 1])

        NC_DMA = dict(allow_slow_non_contiguous=True)
        DBG = {}

        def dbg(name, ap, res):
            if P.dry or name in DBG or not DEBUG:
                return
            shp = list(ap.shape)
            DBG[name] = shp
            o = nc.dram_tensor("dbg_" + name, shp, F32, kind="ExternalOutput").ap()
            P.dma("sp", o, ap, reads=list(res), sem="dbg", final=True)

        def interleave(gens):
            gens = list(gens)
            while gens:
                for g in list(gens):
                    try:
                        next(g)
                    except StopIteration:
                        gens.remove(g)

        def setup():
            P.dma("sp", ident_f[:], c_ident, writes=[r_identf], sem="c8")
            P.dma("pool", ident_b[:], c_ident, writes=[r_identb], sem="c9")
            P.dma("pool", ones_b[:], c_ones, writes=[r_ones], sem="c10")
            P.dma("sp", mask64[:], c_mask64, writes=[r_mask], sem="c5")
            P.dma("sp", cm[:], c_cm, writes=[r_cm], sem="c6")
            P.dma("sp", tp1[:], c_tp1, writes=[r_tp1], sem="c7")
            I("pool", "memset", epsb[:], EPS, writes=[r_par])
            pd = lambda o, i_: P.dma("sp", o, i_, writes=[r_par], sem="c0", **NC_DMA)
            pd(normw[:], norm_w.rearrange("l (k p) -> p l k", p=128))
            pd(fnorm[:], final_norm.rearrange("(k p) -> p k", p=128))
            pd(s5dd[:], s5_d.rearrange("l (k p) -> p l k", p=128))
            pd(bglu[:], b_glu.rearrange("l (k p) -> p l k", p=128))
            pd(gn_h[:], hgrn_norm.rearrange("l p -> p l"))
            pd(gn_g[:], gla_norm.rearrange("l p -> p l"))
            pd(lbp[:], hgrn_lb_raw.rearrange("l (k p) -> p l k", p=128))
            pd(nbgk[:], b_gk.rearrange("l (k p) -> p l k", p=64))
            for l in range(DEPTH):
                pd(cw[:, l], conv_w[l].rearrange("j (t p) -> p j t", p=128))
                P.dma("pool", wgk[:, l, :], w_gk[l], writes=[r_wgk], sem="c1")
            ex = T(0, 16).rearrange("p (l k) -> p l k", k=4)
            I("act", "activation", ex, lbp[:], AF.Exp, reads=[r_par], writes=RT(0))
            sm = T(1, 4)
            I("dve", "tensor_tensor", sm, ex[:, 0, :], ex[:, 1, :], ALU.add, reads=RT(0), writes=RT(1))
            I("dve", "tensor_tensor", sm, sm, ex[:, 2, :], ALU.add, reads=RT(0) + RT(1), writes=RT(1))
            I("dve", "tensor_tensor", sm, sm, ex[:, 3, :], ALU.add, reads=RT(0) + RT(1), writes=RT(1))
            I("dve", "reciprocal", sm, sm, reads=RT(1), writes=RT(1))
            for l in range(1, DEPTH):
                I("dve", "tensor_tensor", ex[:, l, :], ex[:, l, :], sm, ALU.mult, reads=RT(0) + RT(1), writes=RT(0))
            I("dve", "memset", lbp[:, 0, :], 0.0, reads=RT(0), writes=[r_par])
            I("dve", "tensor_copy", lbp[:, 1, :], ex[:, 1, :], reads=RT(0), writes=[r_par])
            I("dve", "tensor_tensor", lbp[:, 2, :], lbp[:, 1, :], ex[:, 2, :], ALU.add, reads=RT(0) + [r_par], writes=[r_par])
            I("dve", "tensor_tensor", lbp[:, 3, :], lbp[:, 2, :], ex[:, 3, :], ALU.add, reads=RT(0) + [r_par], writes=[r_par])
            I("dve", "tensor_scalar", oml[:], lbp[:], -1.0, 1.0, ALU.mult, ALU.add, reads=[r_par], writes=[r_par])
            I("dve", "tensor_scalar", nbgk[:], nbgk[:], -1.0, None, ALU.mult, reads=[r_par], writes=[r_par])
            ar_, ai_, ldt = T(2, 64), T(3, 64), T(4, 64)
            for l in range(DEPTH):
                for g in range(2):
                    ps_ = slice(g * 64, (g + 1) * 64)
                    cs_ = slice(l * 16, (l + 1) * 16)
                    P.dma("sp", AR[ps_, 2 * 512 + l * 16:2 * 512 + (l + 1) * 16],
                          s5_a_re[l].rearrange("(j g) n -> g n j", g=2)[g], writes=RT(2), sem="c2", **NC_DMA)
                    P.dma("sp", AR[ps_, 3 * 512 + l * 16:3 * 512 + (l + 1) * 16],
                          s5_a_im[l].rearrange("(j g) n -> g n j", g=2)[g], writes=RT(3), sem="c3", **NC_DMA)
                    P.dma("sp", AR[ps_, 4 * 512 + l * 16:4 * 512 + (l + 1) * 16],
                          s5_log_dt[l].rearrange("(j g) -> g j", g=2)[g:g + 1, :].broadcast_to([64, 16]),
                          writes=RT(4), sem="c4", **NC_DMA)
            dt_, th, us, kf, fs, cs_t, sn_t = T(5, 64), T(6, 64), T(7, 64), T(8, 64), T(9, 64), T(10, 64), T(11, 64)
            ki = T(12, 64).bitcast(I32)
            I("act", "activation", dt_, ldt, AF.Exp, reads=RT(4), writes=RT(5))
            I("dve", "tensor_tensor", th, dt_, ai_, ALU.mult, reads=RT(5) + RT(3), writes=RT(6))
            I("dve", "tensor_scalar", s5thn[:], th, 1.0 / TWO_PI, None, ALU.mult, reads=RT(6), writes=[r_par])
            for (shift, dst, dt_i) in ((64.0, sn_t, 11), (64.25, cs_t, 10)):
                I("dve", "tensor_scalar", us, s5thn[:], shift, None, ALU.add, reads=[r_par], writes=RT(7))
                I("dve", "tensor_copy", ki, us, reads=RT(7), writes=RT(12))
                I("dve", "tensor_copy", kf, ki, reads=RT(12), writes=RT(8))
                I("dve", "tensor_tensor", fs, us, kf, ALU.subtract, reads=RT(7) + RT(8), writes=RT(9))
                I("act", "activation", dst, fs, AF.Sin, scale=SIN_SCALE, reads=RT(9), writes=RT(dt_i))
            mg = s5r[:]
            I("dve", "tensor_tensor", us, dt_, ar_, ALU.mult, reads=RT(5) + RT(2), writes=RT(7))
            I("act", "activation", mg, us, AF.Exp, reads=RT(7), writes=[r_par])
            abr, abi = T(13, 64), T(14, 64)
            I("dve", "tensor_tensor", abr, mg, cs_t, ALU.mult, reads=[r_par] + RT(10), writes=RT(13))
            I("dve", "tensor_tensor", abi, mg, sn_t, ALU.mult, reads=[r_par] + RT(11), writes=RT(14))
            den, t1_, t2_ = T(15, 64), T(16, 64), T(17, 64)
            I("dve", "tensor_tensor", den, ar_, ar_, ALU.mult, reads=RT(2), writes=RT(15))
            I("dve", "tensor_tensor", t1_, ai_, ai_, ALU.mult, reads=RT(3), writes=RT(16))
            I("dve", "tensor_tensor", den, den, t1_, ALU.add, reads=RT(15) + RT(16), writes=RT(15))
            I("dve", "reciprocal", den, den, reads=RT(15), writes=RT(15))
            I("dve", "tensor_scalar", abr, abr, -1.0, None, ALU.add, reads=RT(13), writes=RT(13))
            I("dve", "tensor_tensor", t1_, abr, ar_, ALU.mult, reads=RT(13) + RT(2), writes=RT(16))
            I("dve", "tensor_tensor", t2_, abi, ai_, ALU.mult, reads=RT(14) + RT(3), writes=RT(17))
            I("dve", "tensor_tensor", t1_, t1_, t2_, ALU.add, reads=RT(16) + RT(17), writes=RT(16))
            I("dve", "tensor_tensor", s5zr[:], t1_, den, ALU.mult, reads=RT(16) + RT(15), writes=[r_par])
            I("dve", "tensor_tensor", t1_, abi, ar_, ALU.mult, reads=RT(14) + RT(2), writes=RT(16))
            I("dve", "tensor_tensor", t2_, abr, ai_, ALU.mult, reads=RT(13) + RT(3), writes=RT(17))
            I("dve", "tensor_tensor", t1_, t1_, t2_, ALU.subtract, reads=RT(16) + RT(17), writes=RT(16))
            I("dve", "tensor_tensor", s5zi[:], t1_, den, ALU.mult, reads=RT(16) + RT(15), writes=[r_par])

            for tt in range(17):
                src = xp[tt * 128:(tt + 1) * 128, :] if tt < 16 else xs
                a0 = 18 + (tt % 2) * 2
                xb_ = AR[:, a0 * 512:(a0 + 2) * 512]
                rxb = RT(a0, 2)
                P.dma("sp", xb_, src, writes=rxb, sem="xin%d" % (tt % 2))
                ci = tt // 4
                for h in range(2):
                    bi = bank()
                    for q in range(4):
                        ft = h * 4 + q
                        I("pe", "transpose", banks[bi][:, q * 128:(q + 1) * 128], xb_[:, ft * 128:(ft + 1) * 128],
                          ident_f[:], reads=rxb + [r_identf], writes=rb(bi))
                    dst = xT[:, h * 4:h * 4 + 4, tt * 128:(tt + 1) * 128]
                    srcv = banks[bi][:, :].rearrange("p (a n) -> p a n", a=4)
                    if h == 0:
                        I("act", "activation", dst, srcv, AF.Copy, reads=rb(bi), writes=[r_xT[ci]])
                    else:
                        I("dve", "tensor_copy", dst, srcv, reads=rb(bi), writes=[r_xT[ci]])

        def rstd_tile(src_list, src_res, n, scale, sq_tiles, out_tile, ones_k=128):
            bi = bank()
            nk = len(src_list)
            for kt, src in enumerate(src_list):
                w_ = sq_tiles[kt % len(sq_tiles)]
                I("act", "activation", TB(w_, n), src, AF.Square, reads=src_res, writes=RT(w_))
                I("pe", "matmul", banks[bi][:, 0:n], ones_b[0:ones_k, :], TB(w_, n)[0:ones_k, :], start=(kt == 0), stop=(kt == nk - 1),
                  reads=RT(w_) + [r_ones], writes=rb(bi, 0, n))
            o = T(out_tile, n)
            I("act", "activation", o, banks[bi][:, 0:n], AF.Ln, scale=scale, bias=epsb[:, 0:1], reads=rb(bi, 0, n) + [r_par], writes=RT(out_tile))
            I("act", "activation", o, o, AF.Exp, scale=-0.5, reads=RT(out_tile), writes=RT(out_tile))

        def rmsnorm_h(l):
            for ci, (c0, n) in enumerate(TC):
                rstd_tile([xT[:, kt, c0:c0 + n] for kt in range(8)], [r_xT[ci]], n, 1.0 / D, (0, 1), 2)
                for kt in range(8):
                    I("dve", "scalar_tensor_tensor", hT[:, kt, c0:c0 + n], xT[:, kt, c0:c0 + n], normw[:, l, kt:kt + 1],
                      T(2, n), ALU.mult, ALU.mult, reads=[r_xT[ci], r_par] + RT(2), writes=[r_hT[ci]])

        def proj(wv, wres, col0, m, ci, bi, boff=0):
            c0, n = TC[ci]
            for kt in range(8):
                I("pe", "matmul", banks[bi][0:m, boff:boff + n], wv[:, kt, col0:col0 + m], hT[:, kt, c0:c0 + n],
                  start=(kt == 0), stop=(kt == 7), reads=[wres, r_hT[ci]], writes=rb(bi, boff, n))

        def branch_a(l):
            wl = w_in[l].rearrange("(k p) c -> p k c", p=128)
            EXT0 = 0
            ext = AR[:, 0:2 + NP_]
            r_ext = RT(0, 5)
            ext_s = T(5, 160).rearrange("p (s t) -> p s t", t=10)
            r_exts = RT(5)
            I("pool", "memset", ext[:, 0:2], 0.0, writes=r_ext)
            for jt in range(4):
                parts = [wl[:, :, base + jt * 128: base + (jt + 1) * 128] for base in (A_X, A_B, A_C, A_Z)]
                (vx, vb, vc, vz), wres, wi = ws.take(parts)
                for j in range(2):
                    P.dma("sp", ext_s[:, :, j], st_conv[l, :, j, jt * 128:(jt + 1) * 128].rearrange("s c -> c s"),
                          writes=r_exts, sem="cst", **NC_DMA)
                for ci, (c0, n) in enumerate(TC):
                    smp = ci == 4
                    bx, bc, bz, bb = bank(), bank(), bank(), bank()
                    proj(vx, wres, 0, 128, ci, bx)
                    proj(vc, wres, 0, 128, ci, bc)
                    proj(vz, wres, 0, 128, ci, bz)
                    proj(vb, wres, 0, 128, ci, bb)
                    I("act", "activation", T(6, n), banks[bx][:, 0:n], AF.Copy, reads=rb(bx, 0, n), writes=RT(6))
                    v3 = lambda ap: ap.rearrange("p (s t) -> p s t", t=8)
                    if not smp:
                        I("dve", "tensor_tensor", ext[:, 2 + c0:2 + c0 + n], banks[bc][:, 0:n], T(6, n), ALU.mult,
                          reads=rb(bc, 0, n) + RT(6), writes=r_ext)
                    else:
                        I("dve", "tensor_tensor", ext_s[:, :, 2:10], v3(banks[bc][:, 0:128]), v3(T(6, 128)), ALU.mult,
                          reads=rb(bc, 0, n) + RT(6), writes=r_exts)
                    I("act", "activation", T(7, n), banks[bz][:, 0:n], AF.Silu, reads=rb(bz, 0, n), writes=RT(7))
                    I("dve", "tensor_tensor", T(8, n), banks[bb][:, 0:n], T(7, n), ALU.mult, reads=rb(bb, 0, n) + RT(7), writes=RT(8))
                    if not smp:
                        e0 = lambda k: ext[:, c0 + k:c0 + k + n]
                        acc = T(9, n)
                        rsrc = r_ext
                    else:
                        e0 = lambda k: ext_s[:, :, k:k + 8]
                        acc = v3(T(9, 128))
                        rsrc = r_exts
                    I("act", "activation", acc, e0(0), AF.Copy, scale=cw[:, l, 0, jt:jt + 1], reads=rsrc + [r_par], writes=RT(9))
                    for k in (1, 2):
                        I("dve", "scalar_tensor_tensor", acc, e0(k), cw[:, l, k, jt:jt + 1], acc, ALU.mult, ALU.add,
                          reads=rsrc + [r_par] + RT(9), writes=RT(9))
                    I("dve", "tensor_tensor", yT[:, jt, c0:c0 + n], T(9, n), T(8, n), ALU.mult, reads=RT(9) + RT(8), writes=[r_yT[ci]])
                    if ci == 3:
                        P.dma("sp", o_conv_p[l].rearrange("j c -> c j")[jt * 128:(jt + 1) * 128],
                              ext[:, NP_:NP_ + 2], reads=r_ext, sem="o_ext", final=True, **NC_DMA)
                    if ci == 4:
                        for j in range(2):
                            P.dma("sp", o_conv_s[l, :, j, jt * 128:(jt + 1) * 128].rearrange("s c -> c s"),
                                  ext_s[:, :, 8 + j], reads=r_exts, sem="o_exts", final=True, **NC_DMA)
                ws.release(wi)

        def branch_s5(l):
            wl = w_in[l].rearrange("(k p) c -> p k c", p=128)
            (vu,), wres, wi = ws.take([wl[:, :, S_U:S_U + 512]])
            for ci, (c0, n) in enumerate(TC):
                for ut in range(4):
                    bi = bank()
                    proj(vu, wres, ut * 128, 128, ci, bi)
                    if ut % 2 == 0:
                        I("act", "activation", yT[:, ut, c0:c0 + n], banks[bi][:, 0:n], AF.Copy, reads=rb(bi, 0, n), writes=[r_yT[ci]])
                    else:
                        I("dve", "tensor_copy", yT[:, ut, c0:c0 + n], banks[bi][:, 0:n], reads=rb(bi, 0, n), writes=[r_yT[ci]])
            ws.release(wi)
            stgB = TB(0).rearrange("p (r j n) -> p r j n", r=2, j=4)
            stgC = TB(1).rearrange("p (r j n) -> p r j n", r=2, j=4)
            Bt = TB(18).rearrange("p (r j n) -> p r j n", r=2, j=4)
            Ct = TB(19).rearrange("p (r j n) -> p r j n", r=2, j=4)
            h0re = T(20, 256).rearrange("p (j s) -> p j s", s=16)
            h0im = T(20, 256, 256).rearrange("p (j s) -> p j s", s=16)
            Hsre = T(21, 256).rearrange("p (j s) -> p j s", s=16)
            Hsim = T(21, 256, 256).rearrange("p (j s) -> p j s", s=16)
            for s_ in range(NSEQ):
                P.dma("sp", h0re[:, :, s_], st_re[l, s_].rearrange("(j g) n -> (g n) j", g=2), writes=RT(20), sem="s5st", **NC_DMA)
                P.dma("sp", h0im[:, :, s_], st_im[l, s_].rearrange("(j g) n -> (g n) j", g=2), writes=RT(20), sem="s5st", **NC_DMA)
            YB = [0, 1, 2, 3, 4]
            h2 = lambda ap: ap.rearrange("p (h x) -> p h x", h=2)

            def gen_tables(cb, lj):
                thn_c, r_c, zr_c, zi_c = s5thn[:, lj:lj + 1], s5r[:, lj:lj + 1], s5zr[:, lj:lj + 1], s5zi[:, lj:lj + 1]
                us, kf = T(cb + 0, TW), T(cb + 0, TW, TW)
                ki, fs = T(cb + 1, TW).bitcast(I32), T(cb + 1, TW, TW)
                TR, EC = T(cb + 5, TW), T(cb + 5, TW, TW)
                TI2, ES2 = T(cb + 6), T(cb + 7)
                rt, rt_s = T(cb + 8, TW), T(cb + 8, 128, TW)
                Es = ES2[:, TW:2 * TW]
                for (shift, dst, dtile) in ((64.0, Es, cb + 7), (64.25, EC, cb + 5)):
                    I("dve", "tensor_scalar", us, tp1[:, 0:TW], thn_c, shift, ALU.mult, ALU.add, reads=[r_tp1, r_par], writes=RT(cb + 0))
                    I("dve", "tensor_copy", ki, us, reads=RT(cb + 0), writes=RT(cb + 1))
                    I("dve", "tensor_copy", kf, ki, reads=RT(cb + 1), writes=RT(cb + 0))
                    I("dve", "tensor_tensor", fs, us, kf, ALU.subtract, reads=RT(cb + 0), writes=RT(cb + 1))
                    I("act", "activation", dst, fs, AF.Sin, scale=SIN_SCALE, reads=RT(cb + 1), writes=RT(dtile))
                tA, tB_ = T(cb + 3, TW), T(cb + 3, TW, TW)
                I("act", "activation", tA, Es, AF.Copy, scale=zi_c, reads=RT(cb + 7) + [r_par], writes=RT(cb + 3))
                I("dve", "scalar_tensor_tensor", TR, EC, zr_c, tA, ALU.mult, ALU.add, reads=RT(cb + 5) + RT(cb + 3) + [r_par], writes=RT(cb + 5))
                I("act", "activation", tB_, Es, AF.Copy, scale=zr_c, reads=RT(cb + 7) + [r_par], writes=RT(cb + 3))
                I("dve", "scalar_tensor_tensor", TI2[:, TW:2 * TW], EC, zi_c, tB_, ALU.mult, ALU.subtract,
                  reads=RT(cb + 5) + RT(cb + 3) + [r_par], writes=RT(cb + 6))
                I("act", "activation", TI2[:, 0:TW], TI2[:, TW:2 * TW], AF.Copy, scale=-1.0, reads=RT(cb + 6), writes=RT(cb + 6))
                I("act", "activation", ES2[:, 0:TW], Es, AF.Copy, scale=-1.0, reads=RT(cb + 7), writes=RT(cb + 7))
                I("dve", "tensor_scalar", rt, tp1[:, 0:TW], 0.0, r_c, ALU.mult, ALU.add, reads=[r_tp1, r_par], writes=RT(cb + 8))
                I("dve", "tensor_scalar", rt_s, cm[:, 512:640], r_c, None, ALU.mult, reads=[r_cm, r_par], writes=RT(cb + 8))

            def window(cb, bk, ut, jp, j, lj, wi_, c0, n, smp, nwin):
                r_c = s5r[:, lj:lj + 1]
                ci = c0 // 512
                boff = c0 % 512
                TR, EC = T(cb + 5, TW), T(cb + 5, TW, TW)
                TI2, ES2 = h2(T(cb + 6)), h2(T(cb + 7))
                rt, rt_s = T(cb + 8, TW), T(cb + 8, 128, TW)
                Xp = h2(banks[bk][:, :])[:, :, 0:n]
                Xs = h2(T(cb + 0))[:, :, 0:n]
                A = h2(T(cb + 1))[:, :, 0:n]
                Bm = h2(T(cb + 2))[:, :, 0:n]
                G = h2(T(cb + 3))[:, :, 0:n]
                A2 = Xs
                Hb = h2(TB(cb + 4, 512))[:, :, 0:n]
                u_rhs = yT[:, ut, c0:c0 + n]
                I("pe", "matmul", banks[bk][:, 0:n], Bt[:, 0, jp, :], u_rhs, start=True, stop=True,
                  reads=RT(18) + [r_yT[ci]], writes=rb(bk, 0, n))
                I("pe", "matmul", banks[bk][:, 256:256 + n], Bt[:, 1, jp, :], u_rhs, start=True, stop=True,
                  reads=RT(18) + [r_yT[ci]], writes=rb(bk, 256, n))
                yield
                rbk = rb(bk, 0, n) + rb(bk, 256, n)
                I("act", "activation", Xs, Xp, AF.Copy, reads=rbk, writes=RT(cb + 0))
                yield
                f3 = lambda ap: ap.rearrange("p h (s t) -> p h s t", t=8)
                if not smp:
                    bc2 = lambda tab: tab[:, 0:n].unsqueeze(1).broadcast_to([128, 2, n])
                    tv = lambda ap: ap
                    sv = lambda ap: ap
                    rtab = rt[:, 0:n]
                else:
                    bc2 = lambda tab: tab[:, 0:8].unsqueeze(1).unsqueeze(1).broadcast_to([128, 2, NSEQ, 8])
                    tv = lambda ap: ap[:, 0:8].unsqueeze(1).broadcast_to([128, NSEQ, 8])
                    sv = lambda ap: ap.rearrange("p (s t) -> p s t", t=8)
                    rtab = rt_s
                if not smp:
                    I("dve", "tensor_tensor", A, Xs, bc2(TR), ALU.mult, reads=RT(cb + 0) + RT(cb + 5), writes=RT(cb + 1))
                else:
                    I("dve", "tensor_tensor", f3(A), f3(Xs), bc2(TR), ALU.mult, reads=RT(cb + 0) + RT(cb + 5), writes=RT(cb + 1))
                yield
                I("pool", "tensor_tensor", sv(Bm[:, 0, :]), sv(Xs[:, 1, :]), tv(TI2[:, 0, :]), ALU.mult, reads=RT(cb + 0) + RT(cb + 6), writes=RT(cb + 2))
                I("pool", "tensor_tensor", sv(Bm[:, 1, :]), sv(Xs[:, 0, :]), tv(TI2[:, 1, :]), ALU.mult, reads=RT(cb + 0) + RT(cb + 6), writes=RT(cb + 2))
                yield
                I("dve", "tensor_tensor", A, A, Bm, ALU.add, reads=RT(cb + 1) + RT(cb + 2), writes=RT(cb + 1))
                yield
                if smp:
                    w0r = sv(A[:, 0, :])[:, :, 0]
                    w0i = sv(A[:, 1, :])[:, :, 0]
                    I("dve", "scalar_tensor_tensor", w0r, h0re[:, j, :], r_c, w0r, ALU.mult, ALU.add,
                      reads=RT(20) + RT(cb + 1) + [r_par], writes=RT(cb + 1))
                    I("dve", "scalar_tensor_tensor", w0i, h0im[:, j, :], r_c, w0i, ALU.mult, ALU.add,
                      reads=RT(20) + RT(cb + 1) + [r_par], writes=RT(cb + 1))
                init_re = 0.0 if (wi_ == 0 or smp) else Hst[:, j, 0:1]
                init_im = 0.0 if (wi_ == 0 or smp) else Hst[:, j, 1:2]
                I("dve", "tensor_tensor_scan", G[:, 0, :], rtab, A[:, 0, :], init_re, ALU.mult, ALU.add,
                  reads=RT(cb + 8) + RT(cb + 1) + [r_H], writes=RT(cb + 3))
                I("dve", "tensor_tensor_scan", G[:, 1, :], rtab, A[:, 1, :], init_im, ALU.mult, ALU.add,
                  reads=RT(cb + 8) + RT(cb + 1) + [r_H], writes=RT(cb + 3))
                yield
                if not smp:
                    I("dve", "tensor_tensor", A2, G, bc2(EC), ALU.mult, reads=RT(cb + 3) + RT(cb + 5), writes=RT(cb + 0))
                else:
                    I("dve", "tensor_tensor", f3(A2), f3(G), bc2(EC), ALU.mult, reads=RT(cb + 3) + RT(cb + 5), writes=RT(cb + 0))
                I("pool", "tensor_tensor", sv(Bm[:, 0, :]), sv(G[:, 1, :]), tv(ES2[:, 0, :]), ALU.mult, reads=RT(cb + 3) + RT(cb + 7), writes=RT(cb + 2))
                I("pool", "tensor_tensor", sv(Bm[:, 1, :]), sv(G[:, 0, :]), tv(ES2[:, 1, :]), ALU.mult, reads=RT(cb + 3) + RT(cb + 7), writes=RT(cb + 2))
                yield
                I("dve", "tensor_tensor", Hb, A2, Bm, ALU.add, reads=RT(cb + 0) + RT(cb + 2), writes=RT(cb + 4))
                yield
                if not smp:
                    I("dve", "tensor_tensor", Hst[:, j, :], A2[:, :, n - 1], Bm[:, :, n - 1], ALU.add, reads=RT(cb + 0) + RT(cb + 2), writes=[r_H])
                else:
                    I("dve", "tensor_tensor", Hsre[:, j, :], sv(A2[:, 0, :])[:, :, 7], sv(Bm[:, 0, :])[:, :, 7], ALU.add,
                      reads=RT(cb + 0) + RT(cb + 2), writes=RT(21))
                    I("dve", "tensor_tensor", Hsim[:, j, :], sv(A2[:, 1, :])[:, :, 7], sv(Bm[:, 1, :])[:, :, 7], ALU.add,
                      reads=RT(cb + 0) + RT(cb + 2), writes=RT(21))
                yb = YB[ci]
                I("pe", "matmul", banks[yb][:, boff:boff + n], Ct[:, 0, jp, :], Hb[:, 0, :], start=(jp == 0 and boff == 0), stop=False,
                  skip_group_check=True, reads=RT(19) + RT(cb + 4), writes=rb(yb, boff, n))
                I("pe", "matmul", banks[yb][:, boff:boff + n], Ct[:, 1, jp, :], Hb[:, 1, :], start=False, stop=(jp == 3 and (boff != 0 or ci == 4)),
                  skip_group_check=True, reads=RT(19) + RT(cb + 4), writes=rb(yb, boff, n))

            wins = [(w * TW, TW, False) for w in range(NP_ // TW)] + [(NP_, NS_, True)]
            for ut in range(4):
                I("pool", "memset", TB(0), 0.0, writes=RT(0))
                I("pool", "memset", TB(1), 0.0, writes=RT(1))
                for jp in range(4):
                    j = 4 * ut + jp
                    for g in range(2):
                        grp = 2 * j + g
                        for ri, (bsrc, csrc) in enumerate(((s5_b_re, s5_c_re), (s5_b_im, s5_c_im))):
                            P.dma("pool", stgB[g * 64:(g + 1) * 64, ri, jp, 32 * jp + 16 * g:32 * jp + 16 * g + 16],
                                  bsrc[l, grp], writes=RT(0), sem="s5wB")
                            P.dma("pool", stgC[32 * jp + 16 * g:32 * jp + 16 * g + 16, ri, jp, g * 64:(g + 1) * 64],
                                  csrc[l, grp], writes=RT(1), sem="s5wC")
                for (stg, dstT, rs_, rd_, negim) in ((stgB, Bt, RT(0), RT(18), False), (stgC, Ct, RT(1), RT(19), True)):
                    for ri in range(2):
                        bi = 7
                        pb = banks[bi][:, :].bitcast(BF16)
                        for jp in range(4):
                            I("pe", "transpose", pb[:, jp * 128:(jp + 1) * 128], stg[:, ri, jp, :], ident_b[:],
                              reads=rs_ + [r_identb], writes=rb(bi))
                        dst = dstT[:, ri].rearrange("p j n -> p (j n)")
                        if negim and ri == 1:
                            I("act", "activation", dst, pb[:, 0:512], AF.Copy, scale=-1.0, reads=rb(bi), writes=rd_)
                        else:
                            I("act", "activation", dst, pb[:, 0:512], AF.Copy, reads=rb(bi), writes=rd_)
                for pair in range(2):
                    jps = (2 * pair, 2 * pair + 1)
                    for c, jp in enumerate(jps):
                        gen_tables(9 * c, l * 16 + 4 * ut + jp)
                    for wi_, (c0, n, smp) in enumerate(wins):
                        interleave([window(9 * c, 5 + c, ut, jp, 4 * ut + jp, l * 16 + 4 * ut + jp, wi_, c0, n, smp, len(wins))
                                    for c, jp in enumerate(jps)])
                if ut == 3:
                    P.dma("sp", o_re_p[l].rearrange("(j p) -> p j", p=128), Hst[:, :, 0], reads=[r_H], sem="o_H", final=True, **NC_DMA)
                    P.dma("sp", o_im_p[l].rearrange("(j p) -> p j", p=128), Hst[:, :, 1], reads=[r_H], sem="o_H", final=True, **NC_DMA)
                for ci, (c0, n) in enumerate(TC):
                    ys, t_, t2 = T(9, n), T(10, n), T(11, n)
                    I("dve", "scalar_tensor_tensor", ys, yT[:, ut, c0:c0 + n], s5dd[:, l, ut:ut + 1], banks[YB[ci]][:, 0:n],
                      ALU.mult, ALU.add, reads=[r_yT[ci], r_par] + rb(YB[ci], 0, n), writes=RT(9))
                    I("act", "activation", t_, ys, AF.Square, reads=RT(9), writes=RT(10))
                    I("dve", "tensor_scalar", t_, t_, 0.044715, 1.0, ALU.mult, ALU.add, reads=RT(10), writes=RT(10))
                    I("pool", "tensor_tensor", t_, t_, ys, ALU.mult, reads=RT(10) + RT(9), writes=RT(10))
                    I("act", "activation", t2, t_, AF.Sigmoid, scale=GELU_C, reads=RT(10), writes=RT(11))
                    I("dve", "tensor_tensor", yT[:, ut, c0:c0 + n], ys, t2, ALU.mult, reads=RT(9) + RT(11), writes=[r_yT[ci]])
            for s_ in range(NSEQ):
                P.dma("sp", o_re_s[l, s_].rearrange("(j p) -> p j", p=128), Hsre[:, :, s_], reads=RT(21), sem="o_Hs", final=True, **NC_DMA)
                P.dma("sp", o_im_s[l, s_].rearrange("(j p) -> p j", p=128), Hsim[:, :, s_], reads=RT(21), sem="o_Hs", final=True, **NC_DMA)
            (vg,), gres, gi = ws.take([w_glu[l].rearrange("(k p) c -> p k c", p=128)])
            (vz,), zres, zi_ = ws.take([wl[:, :, S_Z:S_Z + 512]])
            resb = TB(0, 2048, 0, 2).rearrange("p (k n) -> p k n", k=4)
            for ci, (c0, n) in enumerate(TC):
                for ot in range(4):
                    b1, b2 = bank(), bank()
                    for kt in range(4):
                        I("pe", "matmul", banks[b1][:, 0:n], vg[:, kt, ot * 128:(ot + 1) * 128], yT[:, kt, c0:c0 + n],
                          start=(kt == 0), stop=(kt == 3), reads=[gres, r_yT[ci]], writes=rb(b1, 0, n))
                    proj(vz, zres, ot * 128, 128, ci, b2)
                    I("act", "activation", T(2, n), banks[b1][:, 0:n], AF.Sigmoid, bias=bglu[:, l, ot:ot + 1], reads=rb(b1, 0, n) + [r_par], writes=RT(2))
                    I("act", "activation", T(3, n), banks[b2][:, 0:n], AF.Silu, reads=rb(b2, 0, n), writes=RT(3))
                    I("dve", "tensor_tensor", T(4, n), yT[:, ot, c0:c0 + n], T(2, n), ALU.mult, reads=[r_yT[ci]] + RT(2), writes=RT(4))
                    I("dve", "tensor_tensor", resb[:, ot, 0:n], T(4, n), T(3, n), ALU.mult, reads=RT(4) + RT(3), writes=RT(0, 2))
                I("pool", "tensor_copy", yT[:, :, c0:c0 + n], resb[:, :, 0:n], reads=RT(0, 2), writes=[r_yT[ci]])
            ws.release(gi)
            ws.release(zi_)

        def branch_gated(l, kind):
            wl = w_in[l].rearrange("(k p) c -> p k c", p=128)
            hg = kind == "hgrn"
            K = 128 if hg else 64
            if hg:
                (vq,), rq, iq = ws.take([wl[:, :, C_Q:C_Q + 512]])
                (vf,), rf, if_ = ws.take([wl[:, :, C_F:C_F + 512]])
                (vv,), rv, iv = ws.take([wl[:, :, C_I:C_I + 512]])
                (vz,), rz, iz = ws.take([wl[:, :, C_Z:C_Z + 512]])
                handles = [iq, if_, iv, iz]
                gn = gn_h
                st_in, o_p, o_s = st_hg, o_hg_p, o_hg_s
            else:
                (vqk,), rq, iq = ws.take([wl[:, :, G_Q:G_Q + 512]])
                (vv,), rv, iv = ws.take([wl[:, :, G_V:G_V + 512]])
                (vz,), rz, iz = ws.take([wl[:, :, G_Z:G_Z + 528]])
                handles = [iq, iv, iz]
                gn = gn_g
                st_in, o_p, o_s = st_gla, o_gla_p, o_gla_s
            qb = TB(0, 2048, 0, 2).rearrange("p (h n) -> p h n", h=4)
            kb = TB(2, 2048, 0, 2).rearrange("p (h n) -> p h n", h=4)
            oT = AR[:, 4 * 512:8 * 512].rearrange("p (h n) -> p h n", h=4)
            vtok = [TB(8, 512, 0), TB(8, 512, 512)]
            attm = [TB(9, 64, 0), TB(9, 64, 128)]
            kbTs = [TB(9, 128, 256), TB(9, 128, 512)]
            Sf = T(10).rearrange("p (h n) -> p h n", h=4)
            Sb = TB(11, 512, 0).rearrange("p (h n) -> p h n", h=4)
            elast = T(11, 64, 256).rearrange("p (h n) -> p h n", h=4)
            Ss = [T(12, 128, 128 * i) for i in range(4)]
            Ssb = [TB(13, 128, 128 * i) for i in range(4)]
            tmpS = [T(18, 128, 0), T(18, 128, 128)]
            rT = TB(19, 512, 0)
            r_vtok = [Res("vtok%d" % i) for i in range(2)]
            r_attm = [Res("attm%d" % i) for i in range(2)]
            r_kbT = [Res("kbT%d" % i) for i in range(2)]
            r_Ss = [Res("Ss%d" % i) for i in range(4)]
            r_Ssb = [Res("Ssb%d" % i) for i in range(4)]
            r_tmp = [Res("tmpS%d" % i) for i in range(2)]
            fine = {8: r_vtok, 9: r_attm + r_kbT, 12: r_Ss, 13: r_Ssb, 18: r_tmp}

            def touch():
                for tile_, toks in fine.items():
                    I("pool", "memset", T(tile_, 2), 0.0, writes=RT(tile_) + toks)
            touch()
            I("pool", "memset", T(10), 0.0, writes=RT(10))
            I("pool", "memset", TB(11, 512, 0), 0.0, writes=RT(11))
            BQ, BF_, BV = (0, 1), (0, 1), (2, 3)
            vcount = {"i": 0}
            ucount = {"i": 0}
            for ci, (c0, n) in enumerate(TC):
                smp = ci == 4
                csz = 8 if smp else 64
                nch = n // csz
                cmt = cm[:, 512:640] if smp else cm[:, 0:512]
                if not hg:
                    bi = bank() % 2
                    proj(vz, rz, 512, 16, ci, bi)
                    I("act", "activation", rT[0:16, 0:n], banks[bi][0:16, 0:n], AF.Copy, reads=rb(bi, 0, n), writes=RT(19))
                for hd in range(4):
                    b1, b2 = 0, 1
                    t0_, t1_, t2_, t3_ = T(14, n), T(15, n), T(16, n), T(17, n)
                    if hg:
                        proj(vq, rq, hd * 128, 128, ci, b1)
                        proj(vf, rf, hd * 128, 128, ci, b2)
                        I("act", "activation", t0_, banks[b2][:, 0:n], AF.Sigmoid, reads=rb(b2, 0, n), writes=RT(14))
                        I("dve", "tensor_scalar", t0_, t0_, oml[:, l, hd:hd + 1], lbp[:, l, hd:hd + 1], ALU.mult, ALU.add,
                          reads=RT(14) + [r_par], writes=RT(14))
                        I("act", "activation", t1_, t0_, AF.Ln, reads=RT(14), writes=RT(15))
                        I("dve", "tensor_tensor_scan", t2_, cmt, t1_, 0.0, ALU.mult, ALU.add, reads=[r_cm] + RT(15), writes=RT(16))
                        I("act", "activation", t1_, t2_, AF.Exp, reads=RT(16), writes=RT(15))
                        I("act", "activation", t3_, t2_, AF.Exp, scale=-1.0, reads=RT(16), writes=RT(17))
                        I("dve", "tensor_scalar", t0_, t0_, -1.0, 1.0, ALU.mult, ALU.add, reads=RT(14), writes=RT(14))
                        I("dve", "tensor_tensor", kb[:, hd, 0:n], t0_, t3_, ALU.mult, reads=RT(14) + RT(17), writes=RT(2, 2))
                        I("act", "activation", t2_, banks[b1][:, 0:n], AF.Silu, reads=rb(b1, 0, n), writes=RT(16))
                        I("dve", "scalar_tensor_tensor", qb[:, hd, 0:n], t2_, float(K) ** -0.5, t1_, ALU.mult, ALU.mult,
                          reads=RT(16) + RT(15), writes=RT(0, 2))
                    else:
                        proj(vqk, rq, hd * 64, 64, ci, b1)
                        proj(vqk, rq, 256 + hd * 64, 64, ci, b2)
                        b3 = 6
                        I("pe", "matmul", banks[b3][0:64, 0:n], wgk[0:16, l, hd * 64:(hd + 1) * 64], rT[0:16, 0:n], start=True, stop=True,
                          reads=[r_wgk] + RT(19), writes=rb(b3, 0, n))
                        I("act", "activation", t0_[0:64], banks[b3][0:64, 0:n], AF.Exp, scale=-1.0, bias=nbgk[:, l, hd:hd + 1],
                          reads=rb(b3, 0, n) + [r_par], writes=RT(14))
                        I("act", "activation", t1_[0:64], t0_[0:64], AF.Ln, bias=1.0, reads=RT(14), writes=RT(15))
                        I("dve", "tensor_tensor_scan", t2_[0:64], cmt[0:64], t1_[0:64], 0.0, ALU.mult, ALU.add, reads=[r_cm] + RT(15), writes=RT(16))
                        I("act", "activation", t1_[0:64], t2_[0:64], AF.Exp, scale=-1.0 / 16.0, reads=RT(16), writes=RT(15))
                        I("act", "activation", t3_[0:64], t2_[0:64], AF.Exp, scale=1.0 / 16.0, reads=RT(16), writes=RT(17))
                        I("dve", "tensor_tensor", kb[0:64, hd, 0:n], banks[b2][0:64, 0:n], t3_[0:64], ALU.mult, reads=rb(b2, 0, n) + RT(17), writes=RT(2, 2))
                        I("dve", "scalar_tensor_tensor", qb[0:64, hd, 0:n], banks[b1][0:64, 0:n], float(K) ** -0.5, t1_[0:64], ALU.mult, ALU.mult,
                          reads=rb(b1, 0, n) + RT(15), writes=RT(0, 2))
                    ev = t1_[0:K].rearrange("p (c t) -> p c t", t=csz)[:, :, csz - 1]
                    I("pool", "tensor_copy", elast[0:K, hd, 0:nch], ev, reads=RT(15), writes=RT(11))
                for ch in range(nch):
                    tl = ch * csz
                    t0g = c0 + tl
                    vi = vcount["i"] % 2
                    vcount["i"] += 1
                    bv = BV[vi]
                    for kt in range(8):
                        I("pe", "matmul", banks[bv][0:csz, 0:512], hT[:, kt, t0g:t0g + csz], vv[:, kt, :], start=(kt == 0), stop=(kt == 7),
                          reads=[rv, r_hT[ci]], writes=rb(bv))
                    I("act", "activation", vtok[vi][0:csz, :], banks[bv][0:csz, 0:512], AF.Copy, reads=rb(bv), writes=[r_vtok[vi]])
                    for hd in range(4):
                        u_ = ucount["i"]
                        ucount["i"] += 1
                        pa = u_ % 2
                        if smp:
                            su = ch * 4 + hd
                            p4 = su % 4
                            S_f, S_b = Ss[p4][0:K, :], Ssb[p4][0:K, :]
                            for ahead in ((0, 1, 2) if su == 0 else (2,)):
                                sn = su + ahead
                                if sn < 4 * nch:
                                    P.dma("sp", Ss[sn % 4][0:K, :], st_in[l, sn // 4, sn % 4], writes=[r_Ss[sn % 4]], sem="sst%d" % (sn % 4))
                            I("act", "activation", S_b, S_f, AF.Copy, reads=[r_Ss[p4]], writes=[r_Ssb[p4]])
                            rSf, rSb = [r_Ss[p4]], [r_Ssb[p4]]
                        else:
                            S_f, S_b = Sf[0:K, hd, :], Sb[0:K, hd, :]
                            rSf, rSb = RT(10), RT(11)
                        q_c = qb[0:K, hd, tl:tl + csz]
                        k_c = kb[0:K, hd, tl:tl + csz]
                        ao = (u_ % 4) * 128
                        I("pe", "matmul", banks[4][0:csz, ao:ao + csz], k_c, q_c, start=True, stop=True,
                          reads=RT(0, 4), writes=rb(4, ao, csz))
                        pbf = banks[6][:, :].bitcast(BF16)
                        to = (u_ % 4) * 256
                        I("pe", "transpose", pbf[0:csz, to:to + K], k_c, ident_b[0:K, 0:K], reads=RT(2, 2) + [r_identb], writes=rb(6, to // 2, 128))
                        I("dve", "tensor_tensor", attm[pa][0:csz, 0:csz], banks[4][0:csz, ao:ao + csz], mask64[0:csz, 0:csz], ALU.mult,
                          reads=rb(4, ao, csz) + [r_mask], writes=[r_attm[pa]])
                        I("act", "activation", kbTs[pa][0:csz, 0:K], pbf[0:csz, to:to + K], AF.Copy, reads=rb(6, to // 2, 128), writes=[r_kbT[pa]])
                        oo = (u_ % 8) * 64
                        I("pe", "matmul", banks[5][:, oo:oo + csz], S_b, q_c, start=True, stop=False,
                          reads=rSb + RT(0, 2), writes=rb(5, oo, csz))
                        I("pe", "matmul", banks[5][:, oo:oo + csz], vtok[vi][0:csz, hd * 128:(hd + 1) * 128], attm[pa][0:csz, 0:csz],
                          start=False, stop=True, reads=[r_vtok[vi], r_attm[pa]], writes=rb(5, oo, csz))
                        I("act", "activation", oT[:, hd, tl:tl + csz], banks[5][:, oo:oo + csz], AF.Copy, reads=rb(5, oo, csz), writes=RT(4, 4))
                        po = (u_ % 4) * 128
                        I("pe", "matmul", banks[7][0:K, po:po + 128], kbTs[pa][0:csz, 0:K], vtok[vi][0:csz, hd * 128:(hd + 1) * 128],
                          start=True, stop=True, reads=[r_kbT[pa], r_vtok[vi]], writes=rb(7, po, 128))
                        tm = tmpS[pa][0:K, :]
                        I("dve", "tensor_tensor", tm, banks[7][0:K, po:po + 128], S_f, ALU.add, reads=rb(7, po, 128) + rSf, writes=[r_tmp[pa]])
                        I("act", "activation", S_f, tm, AF.Copy, scale=elast[0:K, hd, ch:ch + 1], reads=[r_tmp[pa]] + RT(11), writes=rSf)
                        if not smp:
                            I("dve", "tensor_scalar", S_b, tm, elast[0:K, hd, ch:ch + 1], None, ALU.mult, reads=[r_tmp[pa]] + RT(11), writes=rSb)
                        if smp:
                            P.dma("sp", o_s[l, ch, hd], S_f, reads=rSf, sem="o_ss%d" % p4, final=True)
                        elif ci == 3 and ch == nch - 1:
                            P.dma("sp", o_p[l, hd], S_f, reads=rSf, sem="o_sp", final=True)
                for hd in range(4):
                    rstd_tile([oT[:, hd, 0:n]], RT(4, 4), n, 1.0 / 128.0, (20,), 21)
                    bz = bank() % 2 + 2
                    proj(vz, rz, hd * 128, 128, ci, bz)
                    I("act", "activation", T(20, n), banks[bz][:, 0:n], AF.Silu, reads=rb(bz, 0, n), writes=RT(20))
                    I("dve", "scalar_tensor_tensor", T(19, n), oT[:, hd, 0:n], gn[:, l:l + 1], T(21, n), ALU.mult, ALU.mult,
                      reads=RT(4, 4) + RT(21) + [r_par], writes=RT(19))
                    I("dve", "tensor_tensor", yT[:, hd, c0:c0 + n], T(19, n), T(20, n), ALU.mult, reads=RT(19) + RT(20), writes=[r_yT[ci]])
            touch()
            for h_ in handles:
                ws.release(h_)

        def merge_branch(l, b):
            wl = w_in[l].rearrange("(k p) c -> p k c", p=128)
            wbl = w_branch[l, b].rearrange("(k p) c -> p k c", p=128)
            wol = w_out[l].rearrange("(k p) c -> p k c", p=128)
            mb = TB(0, 2048, 0, 2).rearrange("p (k n) -> p k n", k=4)
            for hf in range(2):
                g0 = M_G + b * 1024 + hf * 512
                (vg,), rg, ig = ws.take([wl[:, :, g0:g0 + 512]])
                (vb,), rbw, ib = ws.take([wbl[:, :, hf * 512:(hf + 1) * 512]])
                (vo,), ro, io = ws.take([wol[:, hf * 4:(hf + 1) * 4, :]])
                for ci, (c0, n) in enumerate(TC):
                    for dt in range(4):
                        b1, b2 = bank(), bank()
                        for kt in range(4):
                            I("pe", "matmul", banks[b1][:, 0:n], vb[:, kt, dt * 128:(dt + 1) * 128], yT[:, kt, c0:c0 + n],
                              start=(kt == 0), stop=(kt == 3), reads=[rbw, r_yT[ci]], writes=rb(b1, 0, n))
                        proj(vg, rg, dt * 128, 128, ci, b2)
                        I("act", "activation", T(2 + dt % 2, n), banks[b2][:, 0:n], AF.Sigmoid, reads=rb(b2, 0, n), writes=RT(2 + dt % 2))
                        I("dve", "tensor_tensor", mb[:, dt, 0:n], banks[b1][:, 0:n], T(2 + dt % 2, n), ALU.mult,
                          reads=rb(b1, 0, n) + RT(2 + dt % 2), writes=RT(0, 2))
                    for od in range(8):
                        b3 = bank()
                        for kt in range(4):
                            I("pe", "matmul", banks[b3][:, 0:n], vo[:, kt, od * 128:(od + 1) * 128], mb[:, kt, 0:n],
                              start=(kt == 0), stop=(kt == 3), reads=[ro] + RT(0, 2), writes=rb(b3, 0, n))
                        I("dve", "tensor_tensor", xT[:, od, c0:c0 + n], banks[b3][:, 0:n], xT[:, od, c0:c0 + n], ALU.add,
                          reads=rb(b3, 0, n) + [r_xT[ci]], writes=[r_xT[ci]])
                ws.release(ig)
                ws.release(ib)
                ws.release(io)

        def final_out():
            for ci, (c0, n) in enumerate(TC):
                rstd_tile([xT[:, kt, c0:c0 + n] for kt in range(8)], [r_xT[ci]], n, 1.0 / D, (0, 1), 2)
                for sub in range(n // 128):
                    t0 = c0 + sub * 128
                    a0 = 18 + (sub % 2) * 2
                    ob = AR[:, a0 * 512:(a0 + 2) * 512]
                    rob = RT(a0, 2)
                    for h in range(2):
                        bi = bank()
                        for q in range(4):
                            kt = h * 4 + q
                            w_ = 3 + (kt % 2)
                            I("dve", "scalar_tensor_tensor", T(w_, 128), xT[:, kt, t0:t0 + 128], fnorm[:, kt:kt + 1],
                              T(2, 128, sub * 128), ALU.mult, ALU.mult, reads=[r_xT[ci], r_par] + RT(2), writes=RT(w_))
                            I("pe", "transpose", banks[bi][:, q * 128:(q + 1) * 128], T(w_, 128), ident_f[:],
                              reads=RT(w_) + [r_identf], writes=rb(bi))
                        if h == 0:
                            I("act", "activation", ob[:, 0:512], banks[bi][:, :], AF.Copy, reads=rb(bi), writes=rob)
                        else:
                            I("dve", "tensor_copy", ob[:, 512:1024], banks[bi][:, :], reads=rb(bi), writes=rob)
                    dst = o_yp[t0:t0 + 128, :] if ci < 4 else o_ys
                    P.dma("sp", dst, ob, reads=rob, sem="o_y%d" % (sub % 2), final=True)

        def emit_all():
            rr["i"] = 0
            P.mark("setup")
            setup()
            for l in range(DEPTH):
                P.mark("L%d norm" % l)
                rmsnorm_h(l)
                if stage in (1, 99):
                    P.mark("L%d A" % l)
                    branch_a(l)
                    if stage == 99:
                        P.mark("L%d mergeA" % l)
                        merge_branch(l, 0)
                if stage in (2, 99):
                    P.mark("L%d S5" % l)
                    branch_s5(l)
                    if stage == 99:
                        P.mark("L%d mergeS5" % l)
                        merge_branch(l, 1)
                if stage in (3, 99):
                    P.mark("L%d hgrn" % l)
                    branch_gated(l, "hgrn")
                    if stage == 99:
                        P.mark("L%d mergeH" % l)
                        merge_branch(l, 2)
                if stage in (4, 99):
                    P.mark("L%d gla" % l)
                    branch_gated(l, "gla")
                    if stage == 99:
                        P.mark("L%d mergeG" % l)
                        merge_branch(l, 3)
                if stage != 99:
                    break
            P.mark("final")
            final_out()
            P.mark("end")

        P.dry = True
        emit_all()
        P.dry = False
        ws.reset()
        emit_all()
        P.emit()
        if os.environ.get("MK_MARKS"):
            import json
            json.dump(P.marks, open(os.environ["MK_MARKS"], "w"))
        print("[kernel] instr per engine:", {e: len(s) for e, s in P.streams.items()},
              "sbuf left:", nc.sbuf_bytes_remaining, "sems:", len(P.sems), flush=True)
    return nc


_CONSTS = None


def _consts():
    global _CONSTS
    if _CONSTS is None:
        cm = np.ones((128, 640), np.float32)
        cm[:, 0:512:64] = 0.0
        cm[:, 512:640:8] = 0.0
        s = np.arange(64)
        _CONSTS = {
            "c_ident": np.eye(128, dtype=np.float32),
            "c_ones": np.ones((128, 128), np.float32),
            "c_mask64": (s[:, None] <= s[None, :]).astype(np.float32),
            "c_cm": cm,
            "c_tp1": np.broadcast_to(np.arange(1, 257, dtype=np.float32), (128, 256)).copy(),
        }
    return _CONSTS


_WNAMES = ["norm_w", "w_in", "conv_w", "s5_a_re", "s5_a_im", "s5_log_dt", "s5_b_re", "s5_b_im",
           "s5_c_re", "s5_c_im", "s5_d", "w_glu", "b_glu", "hgrn_lb_raw", "hgrn_norm", "w_gk", "b_gk",
           "gla_norm", "w_branch", "w_out", "final_norm"]


def kernel(**inputs):
    global LAST_RESULTS
    stage = int(os.environ.get("MK_STAGE", "99"))
    nc = build_program(stage)
    f = lambda a: np.ascontiguousarray(np.asarray(a, dtype=np.float32))
    shared = {k: f(inputs[k]) for k in _WNAMES}
    shared.update(_consts())
    x_prompt = f(inputs["x_prompt"]); x_sample = f(inputs["x_sample"])
    in_maps = []
    for c in range(8):
        sq = slice(c * NSEQ, (c + 1) * NSEQ)
        m = dict(shared)
        m["xp"] = x_prompt[c]
        m["xs"] = x_sample[sq].reshape(NS_, D)
        m["st_conv"] = f(inputs["state_conv"][:, sq])
        m["st_re"] = f(inputs["state_ssm_re"][:, sq])
        m["st_im"] = f(inputs["state_ssm_im"][:, sq])
        m["st_hg"] = f(inputs["state_hgrn"][:, sq])
        m["st_gla"] = f(inputs["state_gla"][:, sq])
        in_maps.append(m)
    res = run_bass_kernel_spmd(nc, in_maps, core_ids=list(range(8)))
    R = res.results
    LAST_RESULTS = R
    cat_p = lambda k, shp: np.stack([R[c][k].reshape(shp) for c in range(8)], axis=1)
    cat_s = lambda k, shp: np.concatenate([R[c][k].reshape(shp) for c in range(8)], axis=1)
    y_prompt = np.stack([R[c]["o_yp"] for c in range(8)], axis=0)
    y_sample = np.concatenate([R[c]["o_ys"].reshape(NSEQ, 8, D) for c in range(8)], axis=0)
    return (y_prompt, y_sample,
            cat_p("o_conv_p", (DEPTH, 2, WBR)), cat_s("o_conv_s", (DEPTH, NSEQ, 2, WBR)),
            cat_p("o_re_p", (DEPTH, 32, 64)), cat_s("o_re_s", (DEPTH, NSEQ, 32, 64)),
            cat_p("o_im_p", (DEPTH, 32, 64)), cat_s("o_im_s", (DEPTH, NSEQ, 32, 64)),
            cat_p("o_hg_p", (DEPTH, 4, 128, 128)), cat_s("o_hg_s", (DEPTH, NSEQ, 4, 128, 128)),
            cat_p("o_gla_p", (DEPTH, 4, 64, 128)), cat_s("o_gla_s", (DEPTH, NSEQ, 4, 64, 128)))
```

```python
import math
import os
from contextlib import ExitStack

import numpy as np
import concourse.bass as bass
import concourse.mybir as mybir
from concourse.bass_utils import run_bass_kernel_spmd

F32 = mybir.dt.float32
BF16 = mybir.dt.bfloat16
I32 = mybir.dt.int32
ALU = mybir.AluOpType
AF = mybir.ActivationFunctionType

D = 1024
NP_ = 2048
NS_ = 128
NT = NP_ + NS_
DEPTH = 4
WBR = 512
DIN = 10768
NSEQ = 16
EPS = 1e-6
TC = [(0, 512), (512, 512), (1024, 512), (1536, 512), (2048, 128)]
A_X, A_B, A_C, A_Z = 0, 512, 1024, 1536
S_U, S_Z = 2048, 2560
C_Q, C_F, C_I, C_Z = 3072, 3584, 4096, 4608
G_Q, G_K, G_V, G_Z, G_R = 5120, 5376, 5632, 6144, 6656
M_G = 6672
TWO_PI = 2.0 * math.pi

ENGS = ("pe", "act", "dve", "pool", "sp")


class Res:
    __slots__ = ("name", "w", "r")

    def __init__(self, name):
        self.name = name
        self.w = None
        self.r = {}


class Prog:
    def __init__(self, nc, stack):
        self.nc = nc
        self.stack = stack
        self.streams = {e: [] for e in ENGS}
        self.sems = {}
        self.count = {}
        self.known = {e: {} for e in ENGS}
        for e in ("pe", "act", "dve", "pool"):
            self._newsem("eng_" + e)
        self.final_waits = {}
        self.dry = False
        self.marks = []

    def _newsem(self, key):
        s = self.stack.enter_context(self.nc.semaphore(key))
        self.sems[key] = s
        self.count[key] = 0
        return s

    def _deps(self, eng, reads, writes, own=None):
        need = {}

        def add(tok):
            if tok is None:
                return
            k, v = tok
            if need.get(k, 0) < v:
                need[k] = v
        for r in reads:
            add(r.w)
        for w in writes:
            if not (own is not None and w.w is not None and w.w[0] == own):
                add(w.w)
            for k, v in w.r.items():
                add((k, v))
        waits = []
        kn = self.known[eng]
        for k, v in need.items():
            if k == "eng_pe" and eng == "pe":
                continue
            if kn.get(k, 0) >= v:
                continue
            kn[k] = v
            waits.append((k, v))
        return waits

    def _commit(self, tok, reads, writes):
        k, v = tok
        for r in reads:
            if r.r.get(k, 0) < v:
                r.r[k] = v
        for w in writes:
            w.w = tok
            w.r = {}

    def op(self, eng, fn, reads=(), writes=()):
        if self.dry:
            return
        waits = self._deps(eng, reads, writes)
        k = "eng_" + eng
        self.count[k] += 1
        tok = (k, self.count[k])
        self.streams[eng].append((waits, fn, (k, 1)))
        self._commit(tok, reads, writes)

    def mark(self, name):
        if not self.dry:
            self.marks.append((name, {e: len(st_) for e, st_ in self.streams.items()}))

    def I(self, eng, name, *args, reads=(), writes=(), **kw):
        self.op(eng, (name, args, kw), reads, writes)

    def dma(self, q, out, in_, reads=(), writes=(), sem="g", final=False, **kw):
        if self.dry:
            return
        k = "dma_" + sem
        waits = self._deps(q, reads, writes, own=k)
        if k not in self.sems:
            self._newsem(k)
        self.count[k] += 16
        tok = (k, self.count[k])
        self.streams[q].append(
            (waits, lambda e, o=out, i=in_, kw=kw: e.dma_start(out=o, in_=i, **kw), (k, 16)))
        self._commit(tok, reads, writes)
        if final:
            self.final_waits[k] = self.count[k]

    def emit(self):
        nc = self.nc
        fw = list(self.final_waits.items())
        with nc.Block() as block:
            def replay(e, name, extra=()):
                for waits, fn, inc in self.streams[name]:
                    for k, v in waits:
                        e.wait_ge(self.sems[k], v)
                    if isinstance(fn, tuple):
                        ins = getattr(e, fn[0])(*fn[1], **fn[2])
                    else:
                        ins = fn(e)
                    ins.then_inc(self.sems[inc[0]], inc[1])
                for k, v in extra:
                    e.wait_ge(self.sems[k], v)

            @block.tensor
            def _(e):
                replay(e, "pe")

            @block.scalar
            def _(e):
                replay(e, "act")

            @block.vector
            def _(e):
                replay(e, "dve")

            @block.gpsimd
            def _(e):
                replay(e, "pool")

            @block.sync
            def _(e):
                replay(e, "sp", fw)


class WS:
    AHEAD = 2

    def __init__(self, P, slots, slot_elems):
        self.P = P
        self.slots = slots
        self.res = [Res("wslot%d" % i) for i in range(len(slots))]
        self.slot_elems = slot_elems
        self.plan = []
        self.reset()

    def reset(self):
        self.free = list(range(len(self.slots)))
        self.issued = 0
        self.taken = 0
        self.slot_of = {}

    def _views(self, slot, parts):
        t = self.slots[slot]
        off = 0
        views = []
        for ap in parts:
            shp = list(ap.shape)
            n = 1
            for s_ in shp[1:]:
                n *= s_
            v = t[0:shp[0], off:off + n]
            if len(shp) == 3:
                v = v.rearrange("p (a n) -> p a n", a=shp[1])
            views.append(v)
            off += n
        assert off <= self.slot_elems, off
        return views

    def _pump(self):
        while (self.issued < len(self.plan) and self.free
               and self.issued < self.taken + self.AHEAD):
            i = self.issued
            slot = self.free.pop(0)
            self.slot_of[i] = slot
            parts = self.plan[i]
            for ap, v in zip(parts, self._views(slot, parts)):
                self.P.dma("pool", v, ap, writes=[self.res[slot]], sem="w%d" % slot)
            self.issued += 1

    def take(self, parts):
        if self.P.dry:
            self.plan.append(parts)
            return self._views(0, parts), self.res[0], -1
        i = self.taken
        self.taken += 1
        self._pump()
        if i not in self.slot_of:
            raise RuntimeError("weight stream: no free slot for block %d" % i)
        slot = self.slot_of[i]
        return self._views(slot, self.plan[i]), self.res[slot], i

    def release(self, i):
        if self.P.dry:
            return
        self.free.append(self.slot_of[i])
        self._pump()


DEBUG = bool(int(os.environ.get("MK_DEBUG", "0")))
LAST_RESULTS = None
NWK = 22
SLOT_E = 4224
NSLOT = 4
GELU_C = 1.5957691216057308
SIN_SCALE = TWO_PI * (1.0 - 2e-6)
TW = 256


def build_program(stage=99):
    nc = bass.Bass("TRN2", target_bir_lowering=False)

    def din(name, shape, dt=F32):
        return nc.dram_tensor(name, list(shape), dt, kind="ExternalInput").ap()

    def dout(name, shape):
        return nc.dram_tensor(name, list(shape), F32, kind="ExternalOutput").ap()

    xp = din("xp", [NP_, D])
    xs = din("xs", [NS_, D])
    st_conv = din("st_conv", [DEPTH, NSEQ, 2, WBR])
    st_re = din("st_re", [DEPTH, NSEQ, 32, 64])
    st_im = din("st_im", [DEPTH, NSEQ, 32, 64])
    st_hg = din("st_hg", [DEPTH, NSEQ, 4, 128, 128])
    st_gla = din("st_gla", [DEPTH, NSEQ, 4, 64, 128])
    norm_w = din("norm_w", [DEPTH, D])
    w_in = din("w_in", [DEPTH, D, DIN])
    conv_w = din("conv_w", [DEPTH, 3, WBR])
    s5_a_re = din("s5_a_re", [DEPTH, 32, 64])
    s5_a_im = din("s5_a_im", [DEPTH, 32, 64])
    s5_log_dt = din("s5_log_dt", [DEPTH, 32])
    s5_b_re = din("s5_b_re", [DEPTH, 32, 64, 16])
    s5_b_im = din("s5_b_im", [DEPTH, 32, 64, 16])
    s5_c_re = din("s5_c_re", [DEPTH, 32, 16, 64])
    s5_c_im = din("s5_c_im", [DEPTH, 32, 16, 64])
    s5_d = din("s5_d", [DEPTH, WBR])
    w_glu = din("w_glu", [DEPTH, WBR, WBR])
    b_glu = din("b_glu", [DEPTH, WBR])
    hgrn_lb_raw = din("hgrn_lb_raw", [DEPTH, WBR])
    hgrn_norm = din("hgrn_norm", [DEPTH, 128])
    w_gk = din("w_gk", [DEPTH, 16, 256])
    b_gk = din("b_gk", [DEPTH, 256])
    gla_norm = din("gla_norm", [DEPTH, 128])
    w_branch = din("w_branch", [DEPTH, 4, WBR, D])
    w_out = din("w_out", [DEPTH, D, D])
    final_norm = din("final_norm", [D])
    c_ident = din("c_ident", [128, 128])
    c_ones = din("c_ones", [128, 128])
    c_mask64 = din("c_mask64", [64, 64])
    c_cm = din("c_cm", [128, 640])
    c_tp1 = din("c_tp1", [128, 256])

    o_yp = dout("o_yp", [NP_, D])
    o_ys = dout("o_ys", [NS_, D])
    o_conv_p = dout("o_conv_p", [DEPTH, 2, WBR])
    o_conv_s = dout("o_conv_s", [DEPTH, NSEQ, 2, WBR])
    o_re_p = dout("o_re_p", [DEPTH, 2048])
    o_re_s = dout("o_re_s", [DEPTH, NSEQ, 2048])
    o_im_p = dout("o_im_p", [DEPTH, 2048])
    o_im_s = dout("o_im_s", [DEPTH, NSEQ, 2048])
    o_hg_p = dout("o_hg_p", [DEPTH, 4, 128, 128])
    o_hg_s = dout("o_hg_s", [DEPTH, NSEQ, 4, 128, 128])
    o_gla_p = dout("o_gla_p", [DEPTH, 4, 64, 128])
    o_gla_s = dout("o_gla_s", [DEPTH, NSEQ, 4, 64, 128])

    st = ExitStack()
    with st:
        P = Prog(nc, st)
        I = P.I

        def sb(name, shape, dt=F32):
            return st.enter_context(nc.sbuf_tensor(name, list(shape), dt))

        xT = sb("xT", [128, 8, NT]); r_xT = [Res("xT%d" % i) for i in range(len(TC))]
        hT = sb("hT", [128, 8, NT], BF16); r_hT = [Res("hT%d" % i) for i in range(len(TC))]
        yT = sb("yT", [128, 4, NT], BF16); r_yT = [Res("yT%d" % i) for i in range(len(TC))]
        wslots = [sb("wslot%d" % i, [128, SLOT_E], BF16) for i in range(NSLOT)]
        ws = WS(P, wslots, SLOT_E)
        banks = [st.enter_context(nc.psum_tensor("bank%d" % i, [128, 512], F32)) for i in range(8)]
        r_breg = [[Res("bank%d_%d" % (i, q)) for q in range(8)] for i in range(8)]

        def rb(i, c0=0, n=512):
            return r_breg[i][c0 // 64:(c0 + n + 63) // 64]

        rr = {"i": 0}

        def bank():
            i = rr["i"] % 8
            rr["i"] += 1
            return i

        AR = sb("AR", [128, NWK * 512])
        r_ar = [Res("ar%d" % i) for i in range(NWK)]

        def T(i, n=512, off=0):
            return AR[:, i * 512 + off:i * 512 + off + n]

        def TB(i, n=1024, off=0, nt=1):
            return AR[:, i * 512:(i + nt) * 512].bitcast(BF16)[:, off:off + n]

        def RT(i, nt=1):
            return r_ar[i:i + nt]

        ident_f = sb("ident_f", [128, 128]); r_identf = Res("identf")
        ident_b = sb("ident_b", [128, 128], BF16); r_identb = Res("identb")
        ones_b = sb("ones_b", [128, 128], BF16); r_ones = Res("ones")
        mask64 = sb("mask64", [64, 64]); r_mask = Res("mask64")
        cm = sb("cm", [128, 640]); r_cm = Res("cm")
        tp1 = sb("tp1", [128, 256]); r_tp1 = Res("tp1")
        normw = sb("normw", [128, DEPTH, 8]); r_par = Res("params")
        fnorm = sb("fnorm", [128, 8])
        cw = sb("cw", [128, DEPTH, 3, 4])
        s5r = sb("s5r", [128, 64]); s5thn = sb("s5thn", [128, 64]); s5zr = sb("s5zr", [128, 64]); s5zi = sb("s5zi", [128, 64])
        s5dd = sb("s5dd", [128, DEPTH, 4])
        bglu = sb("bglu", [128, DEPTH, 4])
        lbp = sb("lbp", [128, DEPTH, 4]); oml = sb("oml", [128, DEPTH, 4])
        gn_h = sb("gn_h", [128, DEPTH]); gn_g = sb("gn_g", [128, DEPTH])
        nbgk = sb("nbgk", [64, DEPTH, 4])
        wgk = sb("wgk", [16, DEPTH, 256], BF16); r_wgk = Res("wgk")
        Hst = sb("Hst", [128, 16, 2]); r_H = Res("H")
        epsb = sb("epsb", [128, 1])

        NC_DMA = dict(allow_slow_non_contiguous=True)
        DBG = {}

        def dbg(name, ap, res):
            if P.dry or name in DBG or not DEBUG:
                return
            shp = list(ap.shape)
            DBG[name] = shp
            o = nc.dram_tensor("dbg_" + name, shp, F32, kind="ExternalOutput").ap()
            P.dma("sp", o, ap, reads=list(res), sem="dbg", final=True)

        def interleave(gens):
            gens = list(gens)
            while gens:
                for g in list(gens):
                    try:
                        next(g)
                    except StopIteration:
                        gens.remove(g)

        def setup():
            P.dma("sp", ident_f[:], c_ident, writes=[r_identf], sem="c8")
            P.dma("pool", ident_b[:], c_ident, writes=[r_identb], sem="c9")
            P.dma("pool", ones_b[:], c_ones, writes=[r_ones], sem="c10")
            P.dma("sp", mask64[:], c_mask64, writes=[r_mask], sem="c5")
            P.dma("sp", cm[:], c_cm, writes=[r_cm], sem="c6")
            P.dma("sp", tp1[:], c_tp1, writes=[r_tp1], sem="c7")
            I("pool", "memset", epsb[:], EPS, writes=[r_par])
            pd = lambda o, i_: P.dma("sp", o, i_, writes=[r_par], sem="c0", **NC_DMA)
            pd(normw[:], norm_w.rearrange("l (k p) -> p l k", p=128))
            pd(fnorm[:], final_norm.rearrange("(k p) -> p k", p=128))
            pd(s5dd[:], s5_d.rearrange("l (k p) -> p l k", p=128))
            pd(bglu[:], b_glu.rearrange("l (k p) -> p l k", p=128))
            pd(gn_h[:], hgrn_norm.rearrange("l p -> p l"))
            pd(gn_g[:], gla_norm.rearrange("l p -> p l"))
            pd(lbp[:], hgrn_lb_raw.rearrange("l (k p) -> p l k", p=128))
            pd(nbgk[:], b_gk.rearrange("l (k p) -> p l k", p=64))
            for l in range(DEPTH):
                pd(cw[:, l], conv_w[l].rearrange("j (t p) -> p j t", p=128))
                P.dma("pool", wgk[:, l, :], w_gk[l], writes=[r_wgk], sem="c1")
            ex = T(0, 16).rearrange("p (l k) -> p l k", k=4)
            I("act", "activation", ex, lbp[:], AF.Exp, reads=[r_par], writes=RT(0))
            sm = T(1, 4)
            I("dve", "tensor_tensor", sm, ex[:, 0, :], ex[:, 1, :], ALU.add, reads=RT(0), writes=RT(1))
            I("dve", "tensor_tensor", sm, sm, ex[:, 2, :], ALU.add, reads=RT(0) + RT(1), writes=RT(1))
            I("dve", "tensor_tensor", sm, sm, ex[:, 3, :], ALU.add, reads=RT(0) + RT(1), writes=RT(1))
            I("dve", "reciprocal", sm, sm, reads=RT(1), writes=RT(1))
            for l in range(1, DEPTH):
                I("dve", "tensor_tensor", ex[:, l, :], ex[:, l, :], sm, ALU.mult, reads=RT(0) + RT(1), writes=RT(0))
            I("dve", "memset", lbp[:, 0, :], 0.0, reads=RT(0), writes=[r_par])
            I("dve", "tensor_copy", lbp[:, 1, :], ex[:, 1, :], reads=RT(0), writes=[r_par])
            I("dve", "tensor_tensor", lbp[:, 2, :], lbp[:, 1, :], ex[:, 2, :], ALU.add, reads=RT(0) + [r_par], writes=[r_par])
            I("dve", "tensor_tensor", lbp[:, 3, :], lbp[:, 2, :], ex[:, 3, :], ALU.add, reads=RT(0) + [r_par], writes=[r_par])
            I("dve", "tensor_scalar", oml[:], lbp[:], -1.0, 1.0, ALU.mult, ALU.add, reads=[r_par], writes=[r_par])
            I("dve", "tensor_scalar", nbgk[:], nbgk[:], -1.0, None, ALU.mult, reads=[r_par], writes=[r_par])
            ar_, ai_, ldt = T(2, 64), T(3, 64), T(4, 64)
            for l in range(DEPTH):
                for g in range(2):
                    ps_ = slice(g * 64, (g + 1) * 64)
                    cs_ = slice(l * 16, (l + 1) * 16)
                    P.dma("sp", AR[ps_, 2 * 512 + l * 16:2 * 512 + (l + 1) * 16],
                          s5_a_re[l].rearrange("(j g) n -> g n j", g=2)[g], writes=RT(2), sem="c2", **NC_DMA)
                    P.dma("sp", AR[ps_, 3 * 512 + l * 16:3 * 512 + (l + 1) * 16],
                          s5_a_im[l].rearrange("(j g) n -> g n j", g=2)[g], writes=RT(3), sem="c3", **NC_DMA)
                    P.dma("sp", AR[ps_, 4 * 512 + l * 16:4 * 512 + (l + 1) * 16],
                          s5_log_dt[l].rearrange("(j g) -> g j", g=2)[g:g + 1, :].broadcast_to([64, 16]),
                          writes=RT(4), sem="c4", **NC_DMA)
            dt_, th, us, kf, fs, cs_t, sn_t = T(5, 64), T(6, 64), T(7, 64), T(8, 64), T(9, 64), T(10, 64), T(11, 64)
            ki = T(12, 64).bitcast(I32)
            I("act", "activation", dt_, ldt, AF.Exp, reads=RT(4), writes=RT(5))
            I("dve", "tensor_tensor", th, dt_, ai_, ALU.mult, reads=RT(5) + RT(3), writes=RT(6))
            I("dve", "tensor_scalar", s5thn[:], th, 1.0 / TWO_PI, None, ALU.mult, reads=RT(6), writes=[r_par])
            for (shift, dst, dt_i) in ((64.0, sn_t, 11), (64.25, cs_t, 10)):
                I("dve", "tensor_scalar", us, s5thn[:], shift, None, ALU.add, reads=[r_par], writes=RT(7))
                I("dve", "tensor_copy", ki, us, reads=RT(7), writes=RT(12))
                I("dve", "tensor_copy", kf, ki, reads=RT(12), writes=RT(8))
                I("dve", "tensor_tensor", fs, us, kf, ALU.subtract, reads=RT(7) + RT(8), writes=RT(9))
                I("act", "activation", dst, fs, AF.Sin, scale=SIN_SCALE, reads=RT(9), writes=RT(dt_i))
            mg = s5r[:]
            I("dve", "tensor_tensor", us, dt_, ar_, ALU.mult, reads=RT(5) + RT(2), writes=RT(7))
            I("act", "activation", mg, us, AF.Exp, reads=RT(7), writes=[r_par])
            abr, abi = T(13, 64), T(14, 64)
            I("dve", "tensor_tensor", abr, mg, cs_t, ALU.mult, reads=[r_par] + RT(10), writes=RT(13))
            I("dve", "tensor_tensor", abi, mg, sn_t, ALU.mult, reads=[r_par] + RT(11), writes=RT(14))
            den, t1_, t2_ = T(15, 64), T(16, 64), T(17, 64)
            I("dve", "tensor_tensor", den, ar_, ar_, ALU.mult, reads=RT(2), writes=RT(15))
            I("dve", "tensor_tensor", t1_, ai_, ai_, ALU.mult, reads=RT(3), writes=RT(16))
            I("dve", "tensor_tensor", den, den, t1_, ALU.add, reads=RT(15) + RT(16), writes=RT(15))
            I("dve", "reciprocal", den, den, reads=RT(15), writes=RT(15))
            I("dve", "tensor_scalar", abr, abr, -1.0, None, ALU.add, reads=RT(13), writes=RT(13))
            I("dve", "tensor_tensor", t1_, abr, ar_, ALU.mult, reads=RT(13) + RT(2), writes=RT(16))
            I("dve", "tensor_tensor", t2_, abi, ai_, ALU.mult, reads=RT(14) + RT(3), writes=RT(17))
            I("dve", "tensor_tensor", t1_, t1_, t2_, ALU.add, reads=RT(16) + RT(17), writes=RT(16))
            I("dve", "tensor_tensor", s5zr[:], t1_, den, ALU.mult, reads=RT(16) + RT(15), writes=[r_par])
            I("dve", "tensor_tensor", t1_, abi, ar_, ALU.mult, reads=RT(14) + RT(2), writes=RT(16))
            I("dve", "tensor_tensor", t2_, abr, ai_, ALU.mult, reads=RT(13) + RT(3), writes=RT(17))
            I("dve", "tensor_tensor", t1_, t1_, t2_, ALU.subtract, reads=RT(16) + RT(17), writes=RT(16))
            I("dve", "tensor_tensor", s5zi[:], t1_, den, ALU.mult, reads=RT(16) + RT(15), writes=[r_par])

            for tt in range(17):
                src = xp[tt * 128:(tt + 1) * 128, :] if tt < 16 else xs
                a0 = 18 + (tt % 2) * 2
                xb_ = AR[:, a0 * 512:(a0 + 2) * 512]
                rxb = RT(a0, 2)
                P.dma("sp", xb_, src, writes=rxb, sem="xin%d" % (tt % 2))
                ci = tt // 4
                for h in range(2):
                    bi = bank()
                    for q in range(4):
                        ft = h * 4 + q
                        I("pe", "transpose", banks[bi][:, q * 128:(q + 1) * 128], xb_[:, ft * 128:(ft + 1) * 128],
                          ident_f[:], reads=rxb + [r_identf], writes=rb(bi))
                    dst = xT[:, h * 4:h * 4 + 4, tt * 128:(tt + 1) * 128]
                    srcv = banks[bi][:, :].rearrange("p (a n) -> p a n", a=4)
                    if h == 0:
                        I("act", "activation", dst, srcv, AF.Copy, reads=rb(bi), writes=[r_xT[ci]])
                    else:
                        I("dve", "tensor_copy", dst, srcv, reads=rb(bi), writes=[r_xT[ci]])

        def rstd_tile(src_list, src_res, n, scale, sq_tiles, out_tile, ones_k=128):
            bi = bank()
            nk = len(src_list)
            for kt, src in enumerate(src_list):
                w_ = sq_tiles[kt % len(sq_tiles)]
                I("act", "activation", TB(w_, n), src, AF.Square, reads=src_res, writes=RT(w_))
                I("pe", "matmul", banks[bi][:, 0:n], ones_b[0:ones_k, :], TB(w_, n)[0:ones_k, :], start=(kt == 0), stop=(kt == nk - 1),
                  reads=RT(w_) + [r_ones], writes=rb(bi, 0, n))
            o = T(out_tile, n)
            I("act", "activation", o, banks[bi][:, 0:n], AF.Ln, scale=scale, bias=epsb[:, 0:1], reads=rb(bi, 0, n) + [r_par], writes=RT(out_tile))
            I("act", "activation", o, o, AF.Exp, scale=-0.5, reads=RT(out_tile), writes=RT(out_tile))

        def rmsnorm_h(l):
            for ci, (c0, n) in enumerate(TC):
                rstd_tile([xT[:, kt, c0:c0 + n] for kt in range(8)], [r_xT[ci]], n, 1.0 / D, (0, 1), 2)
                for kt in range(8):
                    I("dve", "scalar_tensor_tensor", hT[:, kt, c0:c0 + n], xT[:, kt, c0:c0 + n], normw[:, l, kt:kt + 1],
                      T(2, n), ALU.mult, ALU.mult, reads=[r_xT[ci], r_par] + RT(2), writes=[r_hT[ci]])

        def proj(wv, wres, col0, m, ci, bi, boff=0):
            c0, n = TC[ci]
            for kt in range(8):
                I("pe", "matmul", banks[bi][0:m, boff:boff + n], wv[:, kt, col0:col0 + m], hT[:, kt, c0:c0 + n],
                  start=(kt == 0), stop=(kt == 7), reads=[wres, r_hT[ci]], writes=rb(bi, boff, n))

        def branch_a(l):
            wl = w_in[l].rearrange("(k p) c -> p k c", p=128)
            EXT0 = 0
            ext = AR[:, 0:2 + NP_]
            r_ext = RT(0, 5)
            ext_s = T(5, 160).rearrange("p (s t) -> p s t", t=10)
            r_exts = RT(5)
            I("pool", "memset", ext[:, 0:2], 0.0, writes=r_ext)
            for jt in range(4):
                parts = [wl[:, :, base + jt * 128: base + (jt + 1) * 128] for base in (A_X, A_B, A_C, A_Z)]
                (vx, vb, vc, vz), wres, wi = ws.take(parts)
                for j in range(2):
                    P.dma("sp", ext_s[:, :, j], st_conv[l, :, j, jt * 128:(jt + 1) * 128].rearrange("s c -> c s"),
                          writes=r_exts, sem="cst", **NC_DMA)
                for ci, (c0, n) in enumerate(TC):
                    smp = ci == 4
                    bx, bc, bz, bb = bank(), bank(), bank(), bank()
                    proj(vx, wres, 0, 128, ci, bx)
                    proj(vc, wres, 0, 128, ci, bc)
                    proj(vz, wres, 0, 128, ci, bz)
                    proj(vb, wres, 0, 128, ci, bb)
                    I("act", "activation", T(6, n), banks[bx][:, 0:n], AF.Copy, reads=rb(bx, 0, n), writes=RT(6))
                    v3 = lambda ap: ap.rearrange("p (s t) -> p s t", t=8)
                    if not smp:
                        I("dve", "tensor_tensor", ext[:, 2 + c0:2 + c0 + n], banks[bc][:, 0:n], T(6, n), ALU.mult,
                          reads=rb(bc, 0, n) + RT(6), writes=r_ext)
                    else:
                        I("dve", "tensor_tensor", ext_s[:, :, 2:10], v3(banks[bc][:, 0:128]), v3(T(6, 128)), ALU.mult,
                          reads=rb(bc, 0, n) + RT(6), writes=r_exts)
                    I("act", "activation", T(7, n), banks[bz][:, 0:n], AF.Silu, reads=rb(bz, 0, n), writes=RT(7))
                    I("dve", "tensor_tensor", T(8, n), banks[bb][:, 0:n], T(7, n), ALU.mult, reads=rb(bb, 0, n) + RT(7), writes=RT(8))
                    if not smp:
                        e0 = lambda k: ext[:, c0 + k:c0 + k + n]
                        acc = T(9, n)
                        rsrc = r_ext
                    else:
                        e0 = lambda k: ext_s[:, :, k:k + 8]
                        acc = v3(T(9, 128))
                        rsrc = r_exts
                    I("act", "activation", acc, e0(0), AF.Copy, scale=cw[:, l, 0, jt:jt + 1], reads=rsrc + [r_par], writes=RT(9))
                    for k in (1, 2):
                        I("dve", "scalar_tensor_tensor", acc, e0(k), cw[:, l, k, jt:jt + 1], acc, ALU.mult, ALU.add,
                          reads=rsrc + [r_par] + RT(9), writes=RT(9))
                    I("dve", "tensor_tensor", yT[:, jt, c0:c0 + n], T(9, n), T(8, n), ALU.mult, reads=RT(9) + RT(8), writes=[r_yT[ci]])
                    if ci == 3:
                        P.dma("sp", o_conv_p[l].rearrange("j c -> c j")[jt * 128:(jt + 1) * 128],
                              ext[:, NP_:NP_ + 2], reads=r_ext, sem="o_ext", final=True, **NC_DMA)
                    if ci == 4:
                        for j in range(2):
                            P.dma("sp", o_conv_s[l, :, j, jt * 128:(jt + 1) * 128].rearrange("s c -> c s"),
                                  ext_s[:, :, 8 + j], reads=r_exts, sem="o_exts", final=True, **NC_DMA)
                ws.release(wi)

        def branch_s5(l):
            wl = w_in[l].rearrange("(k p) c -> p k c", p=128)
            (vu,), wres, wi = ws.take([wl[:, :, S_U:S_U + 512]])
            for ci, (c0, n) in enumerate(TC):
                for ut in range(4):
                    bi = bank()
                    proj(vu, wres, ut * 128, 128, ci, bi)
                    if ut % 2 == 0:
                        I("act", "activation", yT[:, ut, c0:c0 + n], banks[bi][:, 0:n], AF.Copy, reads=rb(bi, 0, n), writes=[r_yT[ci]])
                    else:
                        I("dve", "tensor_copy", yT[:, ut, c0:c0 + n], banks[bi][:, 0:n], reads=rb(bi, 0, n), writes=[r_yT[ci]])
            ws.release(wi)
            stgB = TB(0).rearrange("p (r j n) -> p r j n", r=2, j=4)
            stgC = TB(1).rearrange("p (r j n) -> p r j n", r=2, j=4)
            Bt = TB(18).rearrange("p (r j n) -> p r j n", r=2, j=4)
            Ct = TB(19).rearrange("p (r j n) -> p r j n", r=2, j=4)
            h0re = T(20, 256).rearrange("p (j s) -> p j s", s=16)
            h0im = T(20, 256, 256).rearrange("p (j s) -> p j s", s=16)
            Hsre = T(21, 256).rearrange("p (j s) -> p j s", s=16)
            Hsim = T(21, 256, 256).rearrange("p (j s) -> p j s", s=16)
            for s_ in range(NSEQ):
                P.dma("sp", h0re[:, :, s_], st_re[l, s_].rearrange("(j g) n -> (g n) j", g=2), writes=RT(20), sem="s5st", **NC_DMA)
                P.dma("sp", h0im[:, :, s_], st_im[l, s_].rearrange("(j g) n -> (g n) j", g=2), writes=RT(20), sem="s5st", **NC_DMA)
            YB = [0, 1, 2, 3, 4]
            h2 = lambda ap: ap.rearrange("p (h x) -> p h x", h=2)

            def gen_tables(cb, lj):
                thn_c, r_c, zr_c, zi_c = s5thn[:, lj:lj + 1], s5r[:, lj:lj + 1], s5zr[:, lj:lj + 1], s5zi[:, lj:lj + 1]
                us, kf = T(cb + 0, TW), T(cb + 0, TW, TW)
                ki, fs = T(cb + 1, TW).bitcast(I32), T(cb + 1, TW, TW)
                TR, EC = T(cb + 5, TW), T(cb + 5, TW, TW)
                TI2, ES2 = T(cb + 6), T(cb + 7)
                rt, rt_s = T(cb + 8, TW), T(cb + 8, 128, TW)
                Es = ES2[:, TW:2 * TW]
                for (shift, dst, dtile) in ((64.0, Es, cb + 7), (64.25, EC, cb + 5)):
                    I("dve", "tensor_scalar", us, tp1[:, 0:TW], thn_c, shift, ALU.mult, ALU.add, reads=[r_tp1, r_par], writes=RT(cb + 0))
                    I("dve", "tensor_copy", ki, us, reads=RT(cb + 0), writes=RT(cb + 1))
                    I("dve", "tensor_copy", kf, ki, reads=RT(cb + 1), writes=RT(cb + 0))
                    I("dve", "tensor_tensor", fs, us, kf, ALU.subtract, reads=RT(cb + 0), writes=RT(cb + 1))
                    I("act", "activation", dst, fs, AF.Sin, scale=SIN_SCALE, reads=RT(cb + 1), writes=RT(dtile))
                tA, tB_ = T(cb + 3, TW), T(cb + 3, TW, TW)
                I("act", "activation", tA, Es, AF.Copy, scale=zi_c, reads=RT(cb + 7) + [r_par], writes=RT(cb + 3))
                I("dve", "scalar_tensor_tensor", TR, EC, zr_c, tA, ALU.mult, ALU.add, reads=RT(cb + 5) + RT(cb + 3) + [r_par], writes=RT(cb + 5))
                I("act", "activation", tB_, Es, AF.Copy, scale=zr_c, reads=RT(cb + 7) + [r_par], writes=RT(cb + 3))
                I("dve", "scalar_tensor_tensor", TI2[:, TW:2 * TW], EC, zi_c, tB_, ALU.mult, ALU.subtract,
                  reads=RT(cb + 5) + RT(cb + 3) + [r_par], writes=RT(cb + 6))
                I("act", "activation", TI2[:, 0:TW], TI2[:, TW:2 * TW], AF.Copy, scale=-1.0, reads=RT(cb + 6), writes=RT(cb + 6))
                I("act", "activation", ES2[:, 0:TW], Es, AF.Copy, scale=-1.0, reads=RT(cb + 7), writes=RT(cb + 7))
                I("dve", "tensor_scalar", rt, tp1[:, 0:TW], 0.0, r_c, ALU.mult, ALU.add, reads=[r_tp1, r_par], writes=RT(cb + 8))
                I("dve", "tensor_scalar", rt_s, cm[:, 512:640], r_c, None, ALU.mult, reads=[r_cm, r_par], writes=RT(cb + 8))

            def window(cb, bk, ut, jp, j, lj, wi_, c0, n, smp, nwin):
                r_c = s5r[:, lj:lj + 1]
                ci = c0 // 512
                boff = c0 % 512
                TR, EC = T(cb + 5, TW), T(cb + 5, TW, TW)
                TI2, ES2 = h2(T(cb + 6)), h2(T(cb + 7))
                rt, rt_s = T(cb + 8, TW), T(cb + 8, 128, TW)
                Xp = h2(banks[bk][:, :])[:, :, 0:n]
                Xs = h2(T(cb + 0))[:, :, 0:n]
                A = h2(T(cb + 1))[:, :, 0:n]
                Bm = h2(T(cb + 2))[:, :, 0:n]
                G = h2(T(cb + 3))[:, :, 0:n]
                A2 = Xs
                Hb = h2(TB(cb + 4, 512))[:, :, 0:n]
                u_rhs = yT[:, ut, c0:c0 + n]
                I("pe", "matmul", banks[bk][:, 0:n], Bt[:, 0, jp, :], u_rhs, start=True, stop=True,
                  reads=RT(18) + [r_yT[ci]], writes=rb(bk, 0, n))
                I("pe", "matmul", banks[bk][:, 256:256 + n], Bt[:, 1, jp, :], u_rhs, start=True, stop=True,
                  reads=RT(18) + [r_yT[ci]], writes=rb(bk, 256, n))
                yield
                rbk = rb(bk, 0, n) + rb(bk, 256, n)
                I("act", "activation", Xs, Xp, AF.Copy, reads=rbk, writes=RT(cb + 0))
                yield
                f3 = lambda ap: ap.rearrange("p h (s t) -> p h s t", t=8)
                if not smp:
                    bc2 = lambda tab: tab[:, 0:n].unsqueeze(1).broadcast_to([128, 2, n])
                    tv = lambda ap: ap
                    sv = lambda ap: ap
                    rtab = rt[:, 0:n]
                else:
                    bc2 = lambda tab: tab[:, 0:8].unsqueeze(1).unsqueeze(1).broadcast_to([128, 2, NSEQ, 8])
                    tv = lambda ap: ap[:, 0:8].unsqueeze(1).broadcast_to([128, NSEQ, 8])
                    sv = lambda ap: ap.rearrange("p (s t) -> p s t", t=8)
                    rtab = rt_s
                if not smp:
                    I("dve", "tensor_tensor", A, Xs, bc2(TR), ALU.mult, reads=RT(cb + 0) + RT(cb + 5), writes=RT(cb + 1))
                else:
                    I("dve", "tensor_tensor", f3(A), f3(Xs), bc2(TR), ALU.mult, reads=RT(cb + 0) + RT(cb + 5), writes=RT(cb + 1))
                yield
                I("pool", "tensor_tensor", sv(Bm[:, 0, :]), sv(Xs[:, 1, :]), tv(TI2[:, 0, :]), ALU.mult, reads=RT(cb + 0) + RT(cb + 6), writes=RT(cb + 2))
                I("pool", "tensor_tensor", sv(Bm[:, 1, :]), sv(Xs[:, 0, :]), tv(TI2[:, 1, :]), ALU.mult, reads=RT(cb + 0) + RT(cb + 6), writes=RT(cb + 2))
                yield
                I("dve", "tensor_tensor", A, A, Bm, ALU.add, reads=RT(cb + 1) + RT(cb + 2), writes=RT(cb + 1))
                yield
                if smp:
                    w0r = sv(A[:, 0, :])[:, :, 0]
                    w0i = sv(A[:, 1, :])[:, :, 0]
                    I("dve", "scalar_tensor_tensor", w0r, h0re[:, j, :], r_c, w0r, ALU.mult, ALU.add,
                      reads=RT(20) + RT(cb + 1) + [r_par], writes=RT(cb + 1))
                    I("dve", "scalar_tensor_tensor", w0i, h0im[:, j, :], r_c, w0i, ALU.mult, ALU.add,
                      reads=RT(20) + RT(cb + 1) + [r_par], writes=RT(cb + 1))
                init_re = 0.0 if (wi_ == 0 or smp) else Hst[:, j, 0:1]
                init_im = 0.0 if (wi_ == 0 or smp) else Hst[:, j, 1:2]
                I("dve", "tensor_tensor_scan", G[:, 0, :], rtab, A[:, 0, :], init_re, ALU.mult, ALU.add,
                  reads=RT(cb + 8) + RT(cb + 1) + [r_H], writes=RT(cb + 3))
                I("dve", "tensor_tensor_scan", G[:, 1, :], rtab, A[:, 1, :], init_im, ALU.mult, ALU.add,
                  reads=RT(cb + 8) + RT(cb + 1) + [r_H], writes=RT(cb + 3))
                yield
                if not smp:
                    I("dve", "tensor_tensor", A2, G, bc2(EC), ALU.mult, reads=RT(cb + 3) + RT(cb + 5), writes=RT(cb + 0))
                else:
                    I("dve", "tensor_tensor", f3(A2), f3(G), bc2(EC), ALU.mult, reads=RT(cb + 3) + RT(cb + 5), writes=RT(cb + 0))
                I("pool", "tensor_tensor", sv(Bm[:, 0, :]), sv(G[:, 1, :]), tv(ES2[:, 0, :]), ALU.mult, reads=RT(cb + 3) + RT(cb + 7), writes=RT(cb + 2))
                I("pool", "tensor_tensor", sv(Bm[:, 1, :]), sv(G[:, 0, :]), tv(ES2[:, 1, :]), ALU.mult, reads=RT(cb + 3) + RT(cb + 7), writes=RT(cb + 2))
                yield
                I("dve", "tensor_tensor", Hb, A2, Bm, ALU.add, reads=RT(cb + 0) + RT(cb + 2), writes=RT(cb + 4))
                yield
                if not smp:
                    I("dve", "tensor_tensor", Hst[:, j, :], A2[:, :, n - 1], Bm[:, :, n - 1], ALU.add, reads=RT(cb + 0) + RT(cb + 2), writes=[r_H])
                else:
                    I("dve", "tensor_tensor", Hsre[:, j, :], sv(A2[:, 0, :])[:, :, 7], sv(Bm[:, 0, :])[:, :, 7], ALU.add,
                      reads=RT(cb + 0) + RT(cb + 2), writes=RT(21))
                    I("dve", "tensor_tensor", Hsim[:, j, :], sv(A2[:, 1, :])[:, :, 7], sv(Bm[:, 1, :])[:, :, 7], ALU.add,
                      reads=RT(cb + 0) + RT(cb + 2), writes=RT(21))
                yb = YB[ci]
                I("pe", "matmul", banks[yb][:, boff:boff + n], Ct[:, 0, jp, :], Hb[:, 0, :], start=(jp == 0 and boff == 0), stop=False,
                  skip_group_check=True, reads=RT(19) + RT(cb + 4), writes=rb(yb, boff, n))
                I("pe", "matmul", banks[yb][:, boff:boff + n], Ct[:, 1, jp, :], Hb[:, 1, :], start=False, stop=(jp == 3 and (boff != 0 or ci == 4)),
                  skip_group_check=True, reads=RT(19) + RT(cb + 4), writes=rb(yb, boff, n))

            wins = [(w * TW, TW, False) for w in range(NP_ // TW)] + [(NP_, NS_, True)]
            for ut in range(4):
                I("pool", "memset", TB(0), 0.0, writes=RT(0))
                I("pool", "memset", TB(1), 0.0, writes=RT(1))
                for jp in range(4):
                    j = 4 * ut + jp
                    for g in range(2):
                        grp = 2 * j + g
                        for ri, (bsrc, csrc) in enumerate(((s5_b_re, s5_c_re), (s5_b_im, s5_c_im))):
                            P.dma("pool", stgB[g * 64:(g + 1) * 64, ri, jp, 32 * jp + 16 * g:32 * jp + 16 * g + 16],
                                  bsrc[l, grp], writes=RT(0), sem="s5wB")
                            P.dma("pool", stgC[32 * jp + 16 * g:32 * jp + 16 * g + 16, ri, jp, g * 64:(g + 1) * 64],
                                  csrc[l, grp], writes=RT(1), sem="s5wC")
                for (stg, dstT, rs_, rd_, negim) in ((stgB, Bt, RT(0), RT(18), False), (stgC, Ct, RT(1), RT(19), True)):
                    for ri in range(2):
                        bi = 7
                        pb = banks[bi][:, :].bitcast(BF16)
                        for jp in range(4):
                            I("pe", "transpose", pb[:, jp * 128:(jp + 1) * 128], stg[:, ri, jp, :], ident_b[:],
                              reads=rs_ + [r_identb], writes=rb(bi))
                        dst = dstT[:, ri].rearrange("p j n -> p (j n)")
                        if negim and ri == 1:
                            I("act", "activation", dst, pb[:, 0:512], AF.Copy, scale=-1.0, reads=rb(bi), writes=rd_)
                        else:
                            I("act", "activation", dst, pb[:, 0:512], AF.Copy, reads=rb(bi), writes=rd_)
                for pair in range(2):
                    jps = (2 * pair, 2 * pair + 1)
                    for c, jp in enumerate(jps):
                        gen_tables(9 * c, l * 16 + 4 * ut + jp)
                    for wi_, (c0, n, smp) in enumerate(wins):
                        interleave([window(9 * c, 5 + c, ut, jp, 4 * ut + jp, l * 16 + 4 * ut + jp, wi_, c0, n, smp, len(wins))
                                    for c, jp in enumerate(jps)])
                if ut == 3:
                    P.dma("sp", o_re_p[l].rearrange("(j p) -> p j", p=128), Hst[:, :, 0], reads=[r_H], sem="o_H", final=True, **NC_DMA)
                    P.dma("sp", o_im_p[l].rearrange("(j p) -> p j", p=128), Hst[:, :, 1], reads=[r_H], sem="o_H", final=True, **NC_DMA)
                for ci, (c0, n) in enumerate(TC):
                    ys, t_, t2 = T(9, n), T(10, n), T(11, n)
                    I("dve", "scalar_tensor_tensor", ys, yT[:, ut, c0:c0 + n], s5dd[:, l, ut:ut + 1], banks[YB[ci]][:, 0:n],
                      ALU.mult, ALU.add, reads=[r_yT[ci], r_par] + rb(YB[ci], 0, n), writes=RT(9))
                    I("act", "activation", t_, ys, AF.Square, reads=RT(9), writes=RT(10))
                    I("dve", "tensor_scalar", t_, t_, 0.044715, 1.0, ALU.mult, ALU.add, reads=RT(10), writes=RT(10))
                    I("pool", "tensor_tensor", t_, t_, ys, ALU.mult, reads=RT(10) + RT(9), writes=RT(10))
                    I("act", "activation", t2, t_, AF.Sigmoid, scale=GELU_C, reads=RT(10), writes=RT(11))
                    I("dve", "tensor_tensor", yT[:, ut, c0:c0 + n], ys, t2, ALU.mult, reads=RT(9) + RT(11), writes=[r_yT[ci]])
            for s_ in range(NSEQ):
                P.dma("sp", o_re_s[l, s_].rearrange("(j p) -> p j", p=128), Hsre[:, :, s_], reads=RT(21), sem="o_Hs", final=True, **NC_DMA)
                P.dma("sp", o_im_s[l, s_].rearrange("(j p) -> p j", p=128), Hsim[:, :, s_], reads=RT(21), sem="o_Hs", final=True, **NC_DMA)
            (vg,), gres, gi = ws.take([w_glu[l].rearrange("(k p) c -> p k c", p=128)])
            (vz,), zres, zi_ = ws.take([wl[:, :, S_Z:S_Z + 512]])
            resb = TB(0, 2048, 0, 2).rearrange("p (k n) -> p k n", k=4)
            for ci, (c0, n) in enumerate(TC):
                for ot in range(4):
                    b1, b2 = bank(), bank()
                    for kt in range(4):
                        I("pe", "matmul", banks[b1][:, 0:n], vg[:, kt, ot * 128:(ot + 1) * 128], yT[:, kt, c0:c0 + n],
                          start=(kt == 0), stop=(kt == 3), reads=[gres, r_yT[ci]], writes=rb(b1, 0, n))
                    proj(vz, zres, ot * 128, 128, ci, b2)
                    I("act", "activation", T(2, n), banks[b1][:, 0:n], AF.Sigmoid, bias=bglu[:, l, ot:ot + 1], reads=rb(b1, 0, n) + [r_par], writes=RT(2))
                    I("act", "activation", T(3, n), banks[b2][:, 0:n], AF.Silu, reads=rb(b2, 0, n), writes=RT(3))
                    I("dve", "tensor_tensor", T(4, n), yT[:, ot, c0:c0 + n], T(2, n), ALU.mult, reads=[r_yT[ci]] + RT(2), writes=RT(4))
                    I("dve", "tensor_tensor", resb[:, ot, 0:n], T(4, n), T(3, n), ALU.mult, reads=RT(4) + RT(3), writes=RT(0, 2))
                I("pool", "tensor_copy", yT[:, :, c0:c0 + n], resb[:, :, 0:n], reads=RT(0, 2), writes=[r_yT[ci]])
            ws.release(gi)
            ws.release(zi_)

        def branch_gated(l, kind):
            wl = w_in[l].rearrange("(k p) c -> p k c", p=128)
            hg = kind == "hgrn"
            K = 128 if hg else 64
            if hg:
                (vq,), rq, iq = ws.take([wl[:, :, C_Q:C_Q + 512]])
                (vf,), rf, if_ = ws.take([wl[:, :, C_F:C_F + 512]])
                (vv,), rv, iv = ws.take([wl[:, :, C_I:C_I + 512]])
                (vz,), rz, iz = ws.take([wl[:, :, C_Z:C_Z + 512]])
                handles = [iq, if_, iv, iz]
                gn = gn_h
                st_in, o_p, o_s = st_hg, o_hg_p, o_hg_s
            else:
                (vqk,), rq, iq = ws.take([wl[:, :, G_Q:G_Q + 512]])
                (vv,), rv, iv = ws.take([wl[:, :, G_V:G_V + 512]])
                (vz,), rz, iz = ws.take([wl[:, :, G_Z:G_Z + 528]])
                handles = [iq, iv, iz]
                gn = gn_g
                st_in, o_p, o_s = st_gla, o_gla_p, o_gla_s
            qb = TB(0, 2048, 0, 2).rearrange("p (h n) -> p h n", h=4)
            kb = TB(2, 2048, 0, 2).rearrange("p (h n) -> p h n", h=4)
            oT = AR[:, 4 * 512:8 * 512].rearrange("p (h n) -> p h n", h=4)
            vtok = [TB(8, 512, 0), TB(8, 512, 512)]
            attm = [TB(9, 64, 128 * i) for i in range(4)]
            kbTs = [TB(9, 128, 512 + 128 * i) for i in range(4)]
            Sf = T(10).rearrange("p (h n) -> p h n", h=4)
            Sb = TB(11, 512, 0).rearrange("p (h n) -> p h n", h=4)
            elast = T(11, 64, 256).rearrange("p (h n) -> p h n", h=4)
            Ss = [T(12, 128, 128 * i) for i in range(4)]
            Ssb = [TB(13, 128, 128 * i) for i in range(4)]
            tmpS = [T(18, 128, 128 * i) for i in range(4)]
            rT = TB(19, 512, 0)
            r_vtok = [Res("vtok%d" % i) for i in range(2)]
            r_attm = [Res("attm%d" % i) for i in range(4)]
            r_kbT = [Res("kbT%d" % i) for i in range(4)]
            r_Sf = [Res("Sf%d" % i) for i in range(4)]
            r_Sb = [Res("Sb%d" % i) for i in range(4)]
            r_el = Res("elast")
            r_Ss = [Res("Ss%d" % i) for i in range(4)]
            r_Ssb = [Res("Ssb%d" % i) for i in range(4)]
            r_tmp = [Res("tmpS%d" % i) for i in range(4)]
            fine = {8: r_vtok, 9: r_attm + r_kbT, 10: r_Sf, 11: r_Sb + [r_el], 12: r_Ss, 13: r_Ssb, 18: r_tmp}

            def touch():
                for tile_, toks in fine.items():
                    I("pool", "memset", T(tile_, 2), 0.0, writes=RT(tile_) + toks)
            touch()
            I("pool", "memset", T(10), 0.0, writes=r_Sf)
            I("pool", "memset", TB(11, 512, 0), 0.0, writes=r_Sb)
            BQ, BF_, BV = (0, 1), (0, 1), (2, 2)
            vcount = {"i": 0}
            ucount = {"i": 0}
            for ci, (c0, n) in enumerate(TC):
                smp = ci == 4
                csz = 8 if smp else 64
                nch = n // csz
                cmt = cm[:, 512:640] if smp else cm[:, 0:512]
                if not hg:
                    bi = bank() % 2
                    proj(vz, rz, 512, 16, ci, bi)
                    I("act", "activation", rT[0:16, 0:n], banks[bi][0:16, 0:n], AF.Copy, reads=rb(bi, 0, n), writes=RT(19))
                for hd in range(4):
                    b1, b2 = 0, 1
                    t0_, t1_, t2_, t3_ = T(14, n), T(15, n), T(16, n), T(17, n)
                    if hg:
                        proj(vq, rq, hd * 128, 128, ci, b1)
                        proj(vf, rf, hd * 128, 128, ci, b2)
                        I("act", "activation", t0_, banks[b2][:, 0:n], AF.Sigmoid, reads=rb(b2, 0, n), writes=RT(14))
                        I("dve", "tensor_scalar", t0_, t0_, oml[:, l, hd:hd + 1], lbp[:, l, hd:hd + 1], ALU.mult, ALU.add,
                          reads=RT(14) + [r_par], writes=RT(14))
                        I("act", "activation", t1_, t0_, AF.Ln, reads=RT(14), writes=RT(15))
                        I("dve", "tensor_tensor_scan", t2_, cmt, t1_, 0.0, ALU.mult, ALU.add, reads=[r_cm] + RT(15), writes=RT(16))
                        I("act", "activation", t1_, t2_, AF.Exp, reads=RT(16), writes=RT(15))
                        I("act", "activation", t3_, t2_, AF.Exp, scale=-1.0, reads=RT(16), writes=RT(17))
                        I("dve", "tensor_scalar", t0_, t0_, -1.0, 1.0, ALU.mult, ALU.add, reads=RT(14), writes=RT(14))
                        I("dve", "tensor_tensor", kb[:, hd, 0:n], t0_, t3_, ALU.mult, reads=RT(14) + RT(17), writes=RT(2, 2))
                        I("act", "activation", t2_, banks[b1][:, 0:n], AF.Silu, reads=rb(b1, 0, n), writes=RT(16))
                        I("dve", "scalar_tensor_tensor", qb[:, hd, 0:n], t2_, float(K) ** -0.5, t1_, ALU.mult, ALU.mult,
                          reads=RT(16) + RT(15), writes=RT(0, 2))
                    else:
                        proj(vqk, rq, hd * 64, 64, ci, b1)
                        proj(vqk, rq, 256 + hd * 64, 64, ci, b2)
                        b3 = 6
                        I("pe", "matmul", banks[b3][0:64, 0:n], wgk[0:16, l, hd * 64:(hd + 1) * 64], rT[0:16, 0:n], start=True, stop=True,
                          reads=[r_wgk] + RT(19), writes=rb(b3, 0, n))
                        I("act", "activation", t0_[0:64], banks[b3][0:64, 0:n], AF.Exp, scale=-1.0, bias=nbgk[:, l, hd:hd + 1],
                          reads=rb(b3, 0, n) + [r_par], writes=RT(14))
                        I("act", "activation", t1_[0:64], t0_[0:64], AF.Ln, bias=1.0, reads=RT(14), writes=RT(15))
                        I("dve", "tensor_tensor_scan", t2_[0:64], cmt[0:64], t1_[0:64], 0.0, ALU.mult, ALU.add, reads=[r_cm] + RT(15), writes=RT(16))
                        I("act", "activation", t1_[0:64], t2_[0:64], AF.Exp, scale=-1.0 / 16.0, reads=RT(16), writes=RT(15))
                        I("act", "activation", t3_[0:64], t2_[0:64], AF.Exp, scale=1.0 / 16.0, reads=RT(16), writes=RT(17))
                        I("dve", "tensor_tensor", kb[0:64, hd, 0:n], banks[b2][0:64, 0:n], t3_[0:64], ALU.mult, reads=rb(b2, 0, n) + RT(17), writes=RT(2, 2))
                        I("dve", "scalar_tensor_tensor", qb[0:64, hd, 0:n], banks[b1][0:64, 0:n], float(K) ** -0.5, t1_[0:64], ALU.mult, ALU.mult,
                          reads=rb(b1, 0, n) + RT(15), writes=RT(0, 2))
                    ev = t1_[0:K].rearrange("p (c t) -> p c t", t=csz)[:, :, csz - 1]
                    I("pool", "tensor_copy", elast[0:K, hd, 0:nch], ev, reads=RT(15), writes=[r_el])
                def unit(ch, hd, vi, u_):
                    tl = ch * csz
                    pa = hd
                    if smp:
                        su = ch * 4 + hd
                        p4 = su % 4
                        S_f, S_b = Ss[p4][0:K, :], Ssb[p4][0:K, :]
                        for ahead in ((0, 1, 2) if su == 0 else (2,)):
                            sn = su + ahead
                            if sn < 4 * nch:
                                P.dma("sp", Ss[sn % 4][0:K, :], st_in[l, sn // 4, sn % 4], writes=[r_Ss[sn % 4]], sem="sst%d" % (sn % 4))
                        I("act", "activation", S_b, S_f, AF.Copy, reads=[r_Ss[p4]], writes=[r_Ssb[p4]])
                        rSf, rSb = [r_Ss[p4]], [r_Ssb[p4]]
                    else:
                        S_f, S_b = Sf[0:K, hd, :], Sb[0:K, hd, :]
                        rSf, rSb = [r_Sf[hd]], [r_Sb[hd]]
                    q_c = qb[0:K, hd, tl:tl + csz]
                    k_c = kb[0:K, hd, tl:tl + csz]
                    ev_ = hd % 2 == 0
                    ab, tbk, obk, pk = (4, 6, 5, 7) if ev_ else (2, 3, 0, 1)
                    I("pe", "matmul", banks[ab][0:csz, 0:csz], k_c, q_c, start=True, stop=True,
                      reads=RT(0, 4), writes=rb(ab))
                    pbf = banks[tbk][:, :].bitcast(BF16)
                    I("pe", "transpose", pbf[0:csz, 0:K], k_c, ident_b[0:K, 0:K], reads=RT(2, 2) + [r_identb], writes=rb(tbk))
                    yield
                    I("dve", "tensor_tensor", attm[pa][0:csz, 0:csz], banks[ab][0:csz, 0:csz], mask64[0:csz, 0:csz], ALU.mult,
                      reads=rb(ab) + [r_mask], writes=[r_attm[pa]])
                    I("act", "activation", kbTs[pa][0:csz, 0:K], pbf[0:csz, 0:K], AF.Copy, reads=rb(tbk), writes=[r_kbT[pa]])
                    yield
                    I("pe", "matmul", banks[obk][:, 0:csz], S_b, q_c, start=True, stop=False,
                      reads=rSb + RT(0, 2), writes=rb(obk))
                    I("pe", "matmul", banks[obk][:, 0:csz], vtok[vi][0:csz, hd * 128:(hd + 1) * 128], attm[pa][0:csz, 0:csz],
                      start=False, stop=True, reads=[r_vtok[vi], r_attm[pa]], writes=rb(obk))
                    I("pe", "matmul", banks[pk][0:K, 0:128], kbTs[pa][0:csz, 0:K], vtok[vi][0:csz, hd * 128:(hd + 1) * 128],
                      start=True, stop=True, reads=[r_kbT[pa], r_vtok[vi]], writes=rb(pk))
                    yield
                    I("act", "activation", oT[:, hd, tl:tl + csz], banks[obk][:, 0:csz], AF.Copy, reads=rb(obk), writes=RT(4, 4))
                    tm = tmpS[pa][0:K, :]
                    I("dve", "tensor_tensor", tm, banks[pk][0:K, 0:128], S_f, ALU.add, reads=rb(pk) + rSf, writes=[r_tmp[pa]])
                    yield
                    I("act", "activation", S_f, tm, AF.Copy, scale=elast[0:K, hd, ch:ch + 1], reads=[r_tmp[pa], r_el], writes=rSf)
                    if not smp:
                        I("dve", "tensor_scalar", S_b, tm, elast[0:K, hd, ch:ch + 1], None, ALU.mult, reads=[r_tmp[pa], r_el], writes=rSb)
                    if smp:
                        P.dma("sp", o_s[l, ch, hd], S_f, reads=rSf, sem="o_ss%d" % p4, final=True)
                    elif ci == 3 and ch == nch - 1:
                        P.dma("sp", o_p[l, hd], S_f, reads=rSf, sem="o_sp", final=True)
                    yield

                for ch in range(nch):
                    t0g = c0 + ch * csz
                    vi = vcount["i"] % 2
                    vcount["i"] += 1
                    bv = BV[vi]
                    for kt in range(8):
                        I("pe", "matmul", banks[bv][0:csz, 0:512], hT[:, kt, t0g:t0g + csz], vv[:, kt, :], start=(kt == 0), stop=(kt == 7),
                          reads=[rv, r_hT[ci]], writes=rb(bv))
                    I("act", "activation", vtok[vi][0:csz, :], banks[bv][0:csz, 0:512], AF.Copy, reads=rb(bv), writes=[r_vtok[vi]])
                    u0 = ucount["i"]
                    ucount["i"] += 4
                    interleave([unit(ch, hd, vi, u0 + hd) for hd in (0, 1)])
                    interleave([unit(ch, hd, vi, u0 + hd) for hd in (2, 3)])
                for hd in range(4):
                    rstd_tile([oT[:, hd, 0:n]], RT(4, 4), n, 1.0 / 128.0, (20,), 21)
                    bz = bank() % 2 + 2
                    proj(vz, rz, hd * 128, 128, ci, bz)
                    I("act", "activation", T(20, n), banks[bz][:, 0:n], AF.Silu, reads=rb(bz, 0, n), writes=RT(20))
                    I("dve", "scalar_tensor_tensor", T(19, n), oT[:, hd, 0:n], gn[:, l:l + 1], T(21, n), ALU.mult, ALU.mult,
                      reads=RT(4, 4) + RT(21) + [r_par], writes=RT(19))
                    I("dve", "tensor_tensor", yT[:, hd, c0:c0 + n], T(19, n), T(20, n), ALU.mult, reads=RT(19) + RT(20), writes=[r_yT[ci]])
            touch()
            for h_ in handles:
                ws.release(h_)

        def merge_branch(l, b):
            wl = w_in[l].rearrange("(k p) c -> p k c", p=128)
            wbl = w_branch[l, b].rearrange("(k p) c -> p k c", p=128)
            wol = w_out[l].rearrange("(k p) c -> p k c", p=128)
            mb = TB(0, 2048, 0, 2).rearrange("p (k n) -> p k n", k=4)
            for hf in range(2):
                g0 = M_G + b * 1024 + hf * 512
                (vg,), rg, ig = ws.take([wl[:, :, g0:g0 + 512]])
                (vb,), rbw, ib = ws.take([wbl[:, :, hf * 512:(hf + 1) * 512]])
                (vo,), ro, io = ws.take([wol[:, hf * 4:(hf + 1) * 4, :]])
                for ci, (c0, n) in enumerate(TC):
                    for dt in range(4):
                        b1, b2 = bank(), bank()
                        for kt in range(4):
                            I("pe", "matmul", banks[b1][:, 0:n], vb[:, kt, dt * 128:(dt + 1) * 128], yT[:, kt, c0:c0 + n],
                              start=(kt == 0), stop=(kt == 3), reads=[rbw, r_yT[ci]], writes=rb(b1, 0, n))
                        proj(vg, rg, dt * 128, 128, ci, b2)
                        I("act", "activation", T(2 + dt % 2, n), banks[b2][:, 0:n], AF.Sigmoid, reads=rb(b2, 0, n), writes=RT(2 + dt % 2))
                        I("dve", "tensor_tensor", mb[:, dt, 0:n], banks[b1][:, 0:n], T(2 + dt % 2, n), ALU.mult,
                          reads=rb(b1, 0, n) + RT(2 + dt % 2), writes=RT(0, 2))
                    for od in range(8):
                        b3 = bank()
                        for kt in range(4):
                            I("pe", "matmul", banks[b3][:, 0:n], vo[:, kt, od * 128:(od + 1) * 128], mb[:, kt, 0:n],
                              start=(kt == 0), stop=(kt == 3), reads=[ro] + RT(0, 2), writes=rb(b3, 0, n))
                        I("dve", "tensor_tensor", xT[:, od, c0:c0 + n], banks[b3][:, 0:n], xT[:, od, c0:c0 + n], ALU.add,
                          reads=rb(b3, 0, n) + [r_xT[ci]], writes=[r_xT[ci]])
                ws.release(ig)
                ws.release(ib)
                ws.release(io)

        def final_out():
            for ci, (c0, n) in enumerate(TC):
                rstd_tile([xT[:, kt, c0:c0 + n] for kt in range(8)], [r_xT[ci]], n, 1.0 / D, (0, 1), 2)
                for sub in range(n // 128):
                    t0 = c0 + sub * 128
                    a0 = 18 + (sub % 2) * 2
                    ob = AR[:, a0 * 512:(a0 + 2) * 512]
                    rob = RT(a0, 2)
                    for h in range(2):
                        bi = bank()
                        for q in range(4):
                            kt = h * 4 + q
                            w_ = 3 + (kt % 2)
                            I("dve", "scalar_tensor_tensor", T(w_, 128), xT[:, kt, t0:t0 + 128], fnorm[:, kt:kt + 1],
                              T(2, 128, sub * 128), ALU.mult, ALU.mult, reads=[r_xT[ci], r_par] + RT(2), writes=RT(w_))
                            I("pe", "transpose", banks[bi][:, q * 128:(q + 1) * 128], T(w_, 128), ident_f[:],
                              reads=RT(w_) + [r_identf], writes=rb(bi))
                        if h == 0:
                            I("act", "activation", ob[:, 0:512], banks[bi][:, :], AF.Copy, reads=rb(bi), writes=rob)
                        else:
                            I("dve", "tensor_copy", ob[:, 512:1024], banks[bi][:, :], reads=rb(bi), writes=rob)
                    dst = o_yp[t0:t0 + 128, :] if ci < 4 else o_ys
                    P.dma("sp", dst, ob, reads=rob, sem="o_y%d" % (sub % 2), final=True)

        def emit_all():
            rr["i"] = 0
            P.mark("setup")
            setup()
            for l in range(DEPTH):
                P.mark("L%d norm" % l)
                rmsnorm_h(l)
                if stage in (1, 99):
                    P.mark("L%d A" % l)
                    branch_a(l)
                    if stage == 99:
                        P.mark("L%d mergeA" % l)
                        merge_branch(l, 0)
                if stage in (2, 99):
                    P.mark("L%d S5" % l)
                    branch_s5(l)
                    if stage == 99:
                        P.mark("L%d mergeS5" % l)
                        merge_branch(l, 1)
                if stage in (3, 99):
                    P.mark("L%d hgrn" % l)
                    branch_gated(l, "hgrn")
                    if stage == 99:
                        P.mark("L%d mergeH" % l)
                        merge_branch(l, 2)
                if stage in (4, 99):
                    P.mark("L%d gla" % l)
                    branch_gated(l, "gla")
                    if stage == 99:
                        P.mark("L%d mergeG" % l)
                        merge_branch(l, 3)
                if stage != 99:
                    break
            P.mark("final")
            final_out()
            P.mark("end")

        P.dry = True
        emit_all()
        P.dry = False
        ws.reset()
        emit_all()
        P.emit()
        if os.environ.get("MK_MARKS"):
            import json
            json.dump(P.marks, open(os.environ["MK_MARKS"], "w"))
        print("[kernel] instr per engine:", {e: len(s) for e, s in P.streams.items()},
              "sbuf left:", nc.sbuf_bytes_remaining, "sems:", len(P.sems), flush=True)
    return nc


_CONSTS = None


def _consts():
    global _CONSTS
    if _CONSTS is None:
        cm = np.ones((128, 640), np.float32)
        cm[:, 0:512:64] = 0.0
        cm[:, 512:640:8] = 0.0
        s = np.arange(64)
        _CONSTS = {
            "c_ident": np.eye(128, dtype=np.float32),
            "c_ones": np.ones((128, 128), np.float32),
            "c_mask64": (s[:, None] <= s[None, :]).astype(np.float32),
            "c_cm": cm,
            "c_tp1": np.broadcast_to(np.arange(1, 257, dtype=np.float32), (128, 256)).copy(),
        }
    return _CONSTS


_WNAMES = ["norm_w", "w_in", "conv_w", "s5_a_re", "s5_a_im", "s5_log_dt", "s5_b_re", "s5_b_im",
           "s5_c_re", "s5_c_im", "s5_d", "w_glu", "b_glu", "hgrn_lb_raw", "hgrn_norm", "w_gk", "b_gk",
           "gla_norm", "w_branch", "w_out", "final_norm"]


def kernel(**inputs):
    global LAST_RESULTS
    stage = int(os.environ.get("MK_STAGE", "99"))
    nc = build_program(stage)
    f = lambda a: np.ascontiguousarray(np.asarray(a, dtype=np.float32))
    shared = {k: f(inputs[k]) for k in _WNAMES}
    shared.update(_consts())
    x_prompt = f(inputs["x_prompt"]); x_sample = f(inputs["x_sample"])
    in_maps = []
    for c in range(8):
        sq = slice(c * NSEQ, (c + 1) * NSEQ)
        m = dict(shared)
        m["xp"] = x_prompt[c]
        m["xs"] = x_sample[sq].reshape(NS_, D)
        m["st_conv"] = f(inputs["state_conv"][:, sq])
        m["st_re"] = f(inputs["state_ssm_re"][:, sq])
        m["st_im"] = f(inputs["state_ssm_im"][:, sq])
        m["st_hg"] = f(inputs["state_hgrn"][:, sq])
        m["st_gla"] = f(inputs["state_gla"][:, sq])
        in_maps.append(m)
    res = run_bass_kernel_spmd(nc, in_maps, core_ids=list(range(8)))
    R = res.results
    LAST_RESULTS = R
    cat_p = lambda k, shp: np.stack([R[c][k].reshape(shp) for c in range(8)], axis=1)
    cat_s = lambda k, shp: np.concatenate([R[c][k].reshape(shp) for c in range(8)], axis=1)
    y_prompt = np.stack([R[c]["o_yp"] for c in range(8)], axis=0)
    y_sample = np.concatenate([R[c]["o_ys"].reshape(NSEQ, 8, D) for c in range(8)], axis=0)
    return (y_prompt, y_sample,
            cat_p("o_conv_p", (DEPTH, 2, WBR)), cat_s("o_conv_s", (DEPTH, NSEQ, 2, WBR)),
            cat_p("o_re_p", (DEPTH, 32, 64)), cat_s("o_re_s", (DEPTH, NSEQ, 32, 64)),
            cat_p("o_im_p", (DEPTH, 32, 64)), cat_s("o_im_s", (DEPTH, NSEQ, 32, 64)),
            cat_p("o_hg_p", (DEPTH, 4, 128, 128)), cat_s("o_hg_s", (DEPTH, NSEQ, 4, 128, 128)),
            cat_p("o_gla_p", (DEPTH, 4, 64, 128)), cat_s("o_gla_s", (DEPTH, NSEQ, 4, 64, 128)))
```

```python
import math
import os
from contextlib import ExitStack

import numpy as np
import concourse.bass as bass
import concourse.mybir as mybir
from concourse.bass_utils import run_bass_kernel_spmd

F32 = mybir.dt.float32
BF16 = mybir.dt.bfloat16
I32 = mybir.dt.int32
ALU = mybir.AluOpType
AF = mybir.ActivationFunctionType

D = 1024
NP_ = 2048
NS_ = 128
NT = NP_ + NS_
DEPTH = 4
WBR = 512
DIN = 10768
NSEQ = 16
EPS = 1e-6
TC = [(0, 512), (512, 512), (1024, 512), (1536, 512), (2048, 128)]
A_X, A_B, A_C, A_Z = 0, 512, 1024, 1536
S_U, S_Z = 2048, 2560
C_Q, C_F, C_I, C_Z = 3072, 3584, 4096, 4608
G_Q, G_K, G_V, G_Z, G_R = 5120, 5376, 5632, 6144, 6656
M_G = 6672
TWO_PI = 2.0 * math.pi

ENGS = ("pe", "act", "dve", "pool", "sp")


class Res:
    __slots__ = ("name", "w", "r")

    def __init__(self, name):
        self.name = name
        self.w = None
        self.r = {}


class Prog:
    def __init__(self, nc, stack):
        self.nc = nc
        self.stack = stack
        self.streams = {e: [] for e in ENGS}
        self.sems = {}
        self.count = {}
        self.known = {e: {} for e in ENGS}
        for e in ("pe", "act", "dve", "pool"):
            self._newsem("eng_" + e)
        self.final_waits = {}
        self.dry = False
        self.marks = []

    def _newsem(self, key):
        s = self.stack.enter_context(self.nc.semaphore(key))
        self.sems[key] = s
        self.count[key] = 0
        return s

    def _deps(self, eng, reads, writes, own=None):
        need = {}

        def add(tok):
            if tok is None:
                return
            k, v = tok
            if need.get(k, 0) < v:
                need[k] = v
        for r in reads:
            add(r.w)
        for w in writes:
            if not (own is not None and w.w is not None and w.w[0] == own):
                add(w.w)
            for k, v in w.r.items():
                add((k, v))
        waits = []
        kn = self.known[eng]
        for k, v in need.items():
            if k == "eng_pe" and eng == "pe":
                continue
            if kn.get(k, 0) >= v:
                continue
            kn[k] = v
            waits.append((k, v))
        return waits

    def _commit(self, tok, reads, writes):
        k, v = tok
        for r in reads:
            if r.r.get(k, 0) < v:
                r.r[k] = v
        for w in writes:
            w.w = tok
            w.r = {}

    def op(self, eng, fn, reads=(), writes=()):
        if self.dry:
            return
        waits = self._deps(eng, reads, writes)
        k = "eng_" + eng
        self.count[k] += 1
        tok = (k, self.count[k])
        self.streams[eng].append((waits, fn, (k, 1)))
        self._commit(tok, reads, writes)

    def mark(self, name):
        if not self.dry:
            self.marks.append((name, {e: len(st_) for e, st_ in self.streams.items()}))

    def I(self, eng, name, *args, reads=(), writes=(), **kw):
        self.op(eng, (name, args, kw), reads, writes)

    def dma(self, q, out, in_, reads=(), writes=(), sem="g", final=False, **kw):
        if self.dry:
            return
        k = "dma_" + sem
        waits = self._deps(q, reads, writes, own=k)
        if k not in self.sems:
            self._newsem(k)
        self.count[k] += 16
        tok = (k, self.count[k])
        self.streams[q].append(
            (waits, lambda e, o=out, i=in_, kw=kw: e.dma_start(out=o, in_=i, **kw), (k, 16)))
        self._commit(tok, reads, writes)
        if final:
            self.final_waits[k] = self.count[k]

    def emit(self):
        nc = self.nc
        fw = list(self.final_waits.items())
        with nc.Block() as block:
            def replay(e, name, extra=()):
                for waits, fn, inc in self.streams[name]:
                    for k, v in waits:
                        e.wait_ge(self.sems[k], v)
                    if isinstance(fn, tuple):
                        ins = getattr(e, fn[0])(*fn[1], **fn[2])
                    else:
                        ins = fn(e)
                    ins.then_inc(self.sems[inc[0]], inc[1])
                for k, v in extra:
                    e.wait_ge(self.sems[k], v)

            @block.tensor
            def _(e):
                replay(e, "pe")

            @block.scalar
            def _(e):
                replay(e, "act")

            @block.vector
            def _(e):
                replay(e, "dve")

            @block.gpsimd
            def _(e):
                replay(e, "pool")

            @block.sync
            def _(e):
                replay(e, "sp", fw)


class WS:
    AHEAD = 2

    def __init__(self, P, slots, slot_elems):
        self.P = P
        self.slots = slots
        self.res = [Res("wslot%d" % i) for i in range(len(slots))]
        self.slot_elems = slot_elems
        self.plan = []
        self.reset()

    def reset(self):
        self.free = list(range(len(self.slots)))
        self.issued = 0
        self.taken = 0
        self.slot_of = {}

    def _views(self, slot, parts):
        t = self.slots[slot]
        off = 0
        views = []
        for ap in parts:
            shp = list(ap.shape)
            n = 1
            for s_ in shp[1:]:
                n *= s_
            v = t[0:shp[0], off:off + n]
            if len(shp) == 3:
                v = v.rearrange("p (a n) -> p a n", a=shp[1])
            views.append(v)
            off += n
        assert off <= self.slot_elems, off
        return views

    def _pump(self):
        while (self.issued < len(self.plan) and self.free
               and self.issued < self.taken + self.AHEAD):
            i = self.issued
            slot = self.free.pop(0)
            self.slot_of[i] = slot
            parts = self.plan[i]
            for ap, v in zip(parts, self._views(slot, parts)):
                self.P.dma("pool", v, ap, writes=[self.res[slot]], sem="w%d" % slot)
            self.issued += 1

    def take(self, parts):
        if self.P.dry:
            self.plan.append(parts)
            return self._views(0, parts), self.res[0], -1
        i = self.taken
        self.taken += 1
        self._pump()
        if i not in self.slot_of:
            raise RuntimeError("weight stream: no free slot for block %d" % i)
        slot = self.slot_of[i]
        return self._views(slot, self.plan[i]), self.res[slot], i

    def release(self, i):
        if self.P.dry:
            return
        self.free.append(self.slot_of[i])
        self._pump()


DEBUG = bool(int(os.environ.get("MK_DEBUG", "0")))
LAST_RESULTS = None
NWK = 22
SLOT_E = 4224
NSLOT = 4
GELU_C = 1.5957691216057308
SIN_SCALE = TWO_PI * (1.0 - 2e-6)
TW = 256


def build_program(stage=99):
    nc = bass.Bass("TRN2", target_bir_lowering=False)

    def din(name, shape, dt=F32):
        return nc.dram_tensor(name, list(shape), dt, kind="ExternalInput").ap()

    def dout(name, shape):
        return nc.dram_tensor(name, list(shape), F32, kind="ExternalOutput").ap()

    xp = din("xp", [NP_, D])
    xs = din("xs", [NS_, D])
    st_conv = din("st_conv", [DEPTH, NSEQ, 2, WBR])
    st_re = din("st_re", [DEPTH, NSEQ, 32, 64])
    st_im = din("st_im", [DEPTH, NSEQ, 32, 64])
    st_hg = din("st_hg", [DEPTH, NSEQ, 4, 128, 128])
    st_gla = din("st_gla", [DEPTH, NSEQ, 4, 64, 128])
    norm_w = din("norm_w", [DEPTH, D])
    w_in = din("w_in", [DEPTH, D, DIN])
    conv_w = din("conv_w", [DEPTH, 3, WBR])
    s5_a_re = din("s5_a_re", [DEPTH, 32, 64])
    s5_a_im = din("s5_a_im", [DEPTH, 32, 64])
    s5_log_dt = din("s5_log_dt", [DEPTH, 32])
    s5_b_re = din("s5_b_re", [DEPTH, 32, 64, 16])
    s5_b_im = din("s5_b_im", [DEPTH, 32, 64, 16])
    s5_c_re = din("s5_c_re", [DEPTH, 32, 16, 64])
    s5_c_im = din("s5_c_im", [DEPTH, 32, 16, 64])
    s5_d = din("s5_d", [DEPTH, WBR])
    w_glu = din("w_glu", [DEPTH, WBR, WBR])
    b_glu = din("b_glu", [DEPTH, WBR])
    hgrn_lb_raw = din("hgrn_lb_raw", [DEPTH, WBR])
    hgrn_norm = din("hgrn_norm", [DEPTH, 128])
    w_gk = din("w_gk", [DEPTH, 16, 256])
    b_gk = din("b_gk", [DEPTH, 256])
    gla_norm = din("gla_norm", [DEPTH, 128])
    w_branch = din("w_branch", [DEPTH, 4, WBR, D])
    w_out = din("w_out", [DEPTH, D, D])
    final_norm = din("final_norm", [D])
    c_ident = din("c_ident", [128, 128])
    c_ones = din("c_ones", [128, 128])
    c_mask64 = din("c_mask64", [64, 64])
    c_cm = din("c_cm", [128, 640])
    c_tp1 = din("c_tp1", [128, 256])

    o_yp = dout("o_yp", [NP_, D])
    o_ys = dout("o_ys", [NS_, D])
    o_conv_p = dout("o_conv_p", [DEPTH, 2, WBR])
    o_conv_s = dout("o_conv_s", [DEPTH, NSEQ, 2, WBR])
    o_re_p = dout("o_re_p", [DEPTH, 2048])
    o_re_s = dout("o_re_s", [DEPTH, NSEQ, 2048])
    o_im_p = dout("o_im_p", [DEPTH, 2048])
    o_im_s = dout("o_im_s", [DEPTH, NSEQ, 2048])
    o_hg_p = dout("o_hg_p", [DEPTH, 4, 128, 128])
    o_hg_s = dout("o_hg_s", [DEPTH, NSEQ, 4, 128, 128])
    o_gla_p = dout("o_gla_p", [DEPTH, 4, 64, 128])
    o_gla_s = dout("o_gla_s", [DEPTH, NSEQ, 4, 64, 128])

    st = ExitStack()
    with st:
        P = Prog(nc, st)
        I = P.I

        def sb(name, shape, dt=F32):
            return st.enter_context(nc.sbuf_tensor(name, list(shape), dt))

        xT = sb("xT", [128, 8, NT]); r_xT = [Res("xT%d" % i) for i in range(len(TC))]
        hT = sb("hT", [128, 8, NT], BF16); r_hT = [Res("hT%d" % i) for i in range(len(TC))]
        yT = sb("yT", [128, 4, NT], BF16); r_yT = [Res("yT%d" % i) for i in range(len(TC))]
        wslots = [sb("wslot%d" % i, [128, SLOT_E], BF16) for i in range(NSLOT)]
        ws = WS(P, wslots, SLOT_E)
        banks = [st.enter_context(nc.psum_tensor("bank%d" % i, [128, 512], F32)) for i in range(8)]
        r_breg = [[Res("bank%d_%d" % (i, q)) for q in range(8)] for i in range(8)]

        def rb(i, c0=0, n=512):
            return r_breg[i][c0 // 64:(c0 + n + 63) // 64]

        rr = {"i": 0}

        def bank():
            i = rr["i"] % 8
            rr["i"] += 1
            return i

        AR = sb("AR", [128, NWK * 512])
        r_ar = [Res("ar%d" % i) for i in range(NWK)]

        def T(i, n=512, off=0):
            return AR[:, i * 512 + off:i * 512 + off + n]

        def TB(i, n=1024, off=0, nt=1):
            return AR[:, i * 512:(i + nt) * 512].bitcast(BF16)[:, off:off + n]

        def RT(i, nt=1):
            return r_ar[i:i + nt]

        ident_f = sb("ident_f", [128, 128]); r_identf = Res("identf")
        ident_b = sb("ident_b", [128, 128], BF16); r_identb = Res("identb")
        ones_b = sb("ones_b", [128, 128], BF16); r_ones = Res("ones")
        mask64 = sb("mask64", [64, 64]); r_mask = Res("mask64")
        cm = sb("cm", [128, 640]); r_cm = Res("cm")
        tp1 = sb("tp1", [128, 256]); r_tp1 = Res("tp1")
        normw = sb("normw", [128, DEPTH, 8]); r_par = Res("params")
        fnorm = sb("fnorm", [128, 8])
        cw = sb("cw", [128, DEPTH, 3, 4])
        s5r = sb("s5r", [128, 64]); s5thn = sb("s5thn", [128, 64]); s5zr = sb("s5zr", [128, 64]); s5zi = sb("s5zi", [128, 64])
        s5dd = sb("s5dd", [128, DEPTH, 4])
        bglu = sb("bglu", [128, DEPTH, 4])
        lbp = sb("lbp", [128, DEPTH, 4]); oml = sb("oml", [128, DEPTH, 4])
        gn_h = sb("gn_h", [128, DEPTH]); gn_g = sb("gn_g", [128, DEPTH])
        nbgk = sb("nbgk", [64, DEPTH, 4])
        wgk = sb("wgk", [16, DEPTH, 256], BF16); r_wgk = Res("wgk")
        Hst = sb("Hst", [128, 16, 2]); r_H = Res("H")
        epsb = sb("epsb", [128, 1])

        NC_DMA = dict(allow_slow_non_contiguous=True)
        DBG = {}

        def dbg(name, ap, res):
            if P.dry or name in DBG or not DEBUG:
                return
            shp = list(ap.shape)
            DBG[name] = shp
            o = nc.dram_tensor("dbg_" + name, shp, F32, kind="ExternalOutput").ap()
            P.dma("sp", o, ap, reads=list(res), sem="dbg", final=True)

        def interleave(gens):
            gens = list(gens)
            while gens:
                for g in list(gens):
                    try:
                        next(g)
                    except StopIteration:
                        gens.remove(g)

        def setup():
            P.dma("sp", ident_f[:], c_ident, writes=[r_identf], sem="c8")
            P.dma("pool", ident_b[:], c_ident, writes=[r_identb], sem="c9")
            P.dma("pool", ones_b[:], c_ones, writes=[r_ones], sem="c10")
            P.dma("sp", mask64[:], c_mask64, writes=[r_mask], sem="c5")
            P.dma("sp", cm[:], c_cm, writes=[r_cm], sem="c6")
            P.dma("sp", tp1[:], c_tp1, writes=[r_tp1], sem="c7")
            I("pool", "memset", epsb[:], EPS, writes=[r_par])
            pd = lambda o, i_: P.dma("sp", o, i_, writes=[r_par], sem="c0", **NC_DMA)
            pd(normw[:], norm_w.rearrange("l (k p) -> p l k", p=128))
            pd(fnorm[:], final_norm.rearrange("(k p) -> p k", p=128))
            pd(s5dd[:], s5_d.rearrange("l (k p) -> p l k", p=128))
            pd(bglu[:], b_glu.rearrange("l (k p) -> p l k", p=128))
            pd(gn_h[:], hgrn_norm.rearrange("l p -> p l"))
            pd(gn_g[:], gla_norm.rearrange("l p -> p l"))
            pd(lbp[:], hgrn_lb_raw.rearrange("l (k p) -> p l k", p=128))
            pd(nbgk[:], b_gk.rearrange("l (k p) -> p l k", p=64))
            for l in range(DEPTH):
                pd(cw[:, l], conv_w[l].rearrange("j (t p) -> p j t", p=128))
                P.dma("pool", wgk[:, l, :], w_gk[l], writes=[r_wgk], sem="c1")
            ex = T(0, 16).rearrange("p (l k) -> p l k", k=4)
            I("act", "activation", ex, lbp[:], AF.Exp, reads=[r_par], writes=RT(0))
            sm = T(1, 4)
            I("dve", "tensor_tensor", sm, ex[:, 0, :], ex[:, 1, :], ALU.add, reads=RT(0), writes=RT(1))
            I("dve", "tensor_tensor", sm, sm, ex[:, 2, :], ALU.add, reads=RT(0) + RT(1), writes=RT(1))
            I("dve", "tensor_tensor", sm, sm, ex[:, 3, :], ALU.add, reads=RT(0) + RT(1), writes=RT(1))
            I("dve", "reciprocal", sm, sm, reads=RT(1), writes=RT(1))
            for l in range(1, DEPTH):
                I("dve", "tensor_tensor", ex[:, l, :], ex[:, l, :], sm, ALU.mult, reads=RT(0) + RT(1), writes=RT(0))
            I("dve", "memset", lbp[:, 0, :], 0.0, reads=RT(0), writes=[r_par])
            I("dve", "tensor_copy", lbp[:, 1, :], ex[:, 1, :], reads=RT(0), writes=[r_par])
            I("dve", "tensor_tensor", lbp[:, 2, :], lbp[:, 1, :], ex[:, 2, :], ALU.add, reads=RT(0) + [r_par], writes=[r_par])
            I("dve", "tensor_tensor", lbp[:, 3, :], lbp[:, 2, :], ex[:, 3, :], ALU.add, reads=RT(0) + [r_par], writes=[r_par])
            I("dve", "tensor_scalar", oml[:], lbp[:], -1.0, 1.0, ALU.mult, ALU.add, reads=[r_par], writes=[r_par])
            I("dve", "tensor_scalar", nbgk[:], nbgk[:], -1.0, None, ALU.mult, reads=[r_par], writes=[r_par])
            ar_, ai_, ldt = T(2, 64), T(3, 64), T(4, 64)
            for l in range(DEPTH):
                for g in range(2):
                    ps_ = slice(g * 64, (g + 1) * 64)
                    cs_ = slice(l * 16, (l + 1) * 16)
                    P.dma("sp", AR[ps_, 2 * 512 + l * 16:2 * 512 + (l + 1) * 16],
                          s5_a_re[l].rearrange("(j g) n -> g n j", g=2)[g], writes=RT(2), sem="c2", **NC_DMA)
                    P.dma("sp", AR[ps_, 3 * 512 + l * 16:3 * 512 + (l + 1) * 16],
                          s5_a_im[l].rearrange("(j g) n -> g n j", g=2)[g], writes=RT(3), sem="c3", **NC_DMA)
                    P.dma("sp", AR[ps_, 4 * 512 + l * 16:4 * 512 + (l + 1) * 16],
                          s5_log_dt[l].rearrange("(j g) -> g j", g=2)[g:g + 1, :].broadcast_to([64, 16]),
                          writes=RT(4), sem="c4", **NC_DMA)
            dt_, th, us, kf, fs, cs_t, sn_t = T(5, 64), T(6, 64), T(7, 64), T(8, 64), T(9, 64), T(10, 64), T(11, 64)
            ki = T(12, 64).bitcast(I32)
            I("act", "activation", dt_, ldt, AF.Exp, reads=RT(4), writes=RT(5))
            I("dve", "tensor_tensor", th, dt_, ai_, ALU.mult, reads=RT(5) + RT(3), writes=RT(6))
            I("dve", "tensor_scalar", s5thn[:], th, 1.0 / TWO_PI, None, ALU.mult, reads=RT(6), writes=[r_par])
            for (shift, dst, dt_i) in ((64.0, sn_t, 11), (64.25, cs_t, 10)):
                I("dve", "tensor_scalar", us, s5thn[:], shift, None, ALU.add, reads=[r_par], writes=RT(7))
                I("dve", "tensor_copy", ki, us, reads=RT(7), writes=RT(12))
                I("dve", "tensor_copy", kf, ki, reads=RT(12), writes=RT(8))
                I("dve", "tensor_tensor", fs, us, kf, ALU.subtract, reads=RT(7) + RT(8), writes=RT(9))
                I("act", "activation", dst, fs, AF.Sin, scale=SIN_SCALE, reads=RT(9), writes=RT(dt_i))
            mg = s5r[:]
            I("dve", "tensor_tensor", us, dt_, ar_, ALU.mult, reads=RT(5) + RT(2), writes=RT(7))
            I("act", "activation", mg, us, AF.Exp, reads=RT(7), writes=[r_par])
            abr, abi = T(13, 64), T(14, 64)
            I("dve", "tensor_tensor", abr, mg, cs_t, ALU.mult, reads=[r_par] + RT(10), writes=RT(13))
            I("dve", "tensor_tensor", abi, mg, sn_t, ALU.mult, reads=[r_par] + RT(11), writes=RT(14))
            den, t1_, t2_ = T(15, 64), T(16, 64), T(17, 64)
            I("dve", "tensor_tensor", den, ar_, ar_, ALU.mult, reads=RT(2), writes=RT(15))
            I("dve", "tensor_tensor", t1_, ai_, ai_, ALU.mult, reads=RT(3), writes=RT(16))
            I("dve", "tensor_tensor", den, den, t1_, ALU.add, reads=RT(15) + RT(16), writes=RT(15))
            I("dve", "reciprocal", den, den, reads=RT(15), writes=RT(15))
            I("dve", "tensor_scalar", abr, abr, -1.0, None, ALU.add, reads=RT(13), writes=RT(13))
            I("dve", "tensor_tensor", t1_, abr, ar_, ALU.mult, reads=RT(13) + RT(2), writes=RT(16))
            I("dve", "tensor_tensor", t2_, abi, ai_, ALU.mult, reads=RT(14) + RT(3), writes=RT(17))
            I("dve", "tensor_tensor", t1_, t1_, t2_, ALU.add, reads=RT(16) + RT(17), writes=RT(16))
            I("dve", "tensor_tensor", s5zr[:], t1_, den, ALU.mult, reads=RT(16) + RT(15), writes=[r_par])
            I("dve", "tensor_tensor", t1_, abi, ar_, ALU.mult, reads=RT(14) + RT(2), writes=RT(16))
            I("dve", "tensor_tensor", t2_, abr, ai_, ALU.mult, reads=RT(13) + RT(3), writes=RT(17))
            I("dve", "tensor_tensor", t1_, t1_, t2_, ALU.subtract, reads=RT(16) + RT(17), writes=RT(16))
            I("dve", "tensor_tensor", s5zi[:], t1_, den, ALU.mult, reads=RT(16) + RT(15), writes=[r_par])

            for tt in range(17):
                src = xp[tt * 128:(tt + 1) * 128, :] if tt < 16 else xs
                a0 = 18 + (tt % 2) * 2
                xb_ = AR[:, a0 * 512:(a0 + 2) * 512]
                rxb = RT(a0, 2)
                P.dma("sp", xb_, src, writes=rxb, sem="xin%d" % (tt % 2))
                ci = tt // 4
                for h in range(2):
                    bi = bank()
                    for q in range(4):
                        ft = h * 4 + q
                        I("pe", "transpose", banks[bi][:, q * 128:(q + 1) * 128], xb_[:, ft * 128:(ft + 1) * 128],
                          ident_f[:], reads=rxb + [r_identf], writes=rb(bi))
                    dst = xT[:, h * 4:h * 4 + 4, tt * 128:(tt + 1) * 128]
                    srcv = banks[bi][:, :].rearrange("p (a n) -> p a n", a=4)
                    if h == 0:
                        I("act", "activation", dst, srcv, AF.Copy, reads=rb(bi), writes=[r_xT[ci]])
                    else:
                        I("dve", "tensor_copy", dst, srcv, reads=rb(bi), writes=[r_xT[ci]])

        def rstd_tile(src_list, src_res, n, scale, sq_tiles, out_tile, ones_k=128):
            bi = bank()
            nk = len(src_list)
            for kt, src in enumerate(src_list):
                w_ = sq_tiles[kt % len(sq_tiles)]
                I("act", "activation", TB(w_, n), src, AF.Square, reads=src_res, writes=RT(w_))
                I("pe", "matmul", banks[bi][:, 0:n], ones_b[0:ones_k, :], TB(w_, n)[0:ones_k, :], start=(kt == 0), stop=(kt == nk - 1),
                  reads=RT(w_) + [r_ones], writes=rb(bi, 0, n))
            o = T(out_tile, n)
            I("act", "activation", o, banks[bi][:, 0:n], AF.Ln, scale=scale, bias=epsb[:, 0:1], reads=rb(bi, 0, n) + [r_par], writes=RT(out_tile))
            I("act", "activation", o, o, AF.Exp, scale=-0.5, reads=RT(out_tile), writes=RT(out_tile))

        def rmsnorm_h(l):
            for ci, (c0, n) in enumerate(TC):
                rstd_tile([xT[:, kt, c0:c0 + n] for kt in range(8)], [r_xT[ci]], n, 1.0 / D, (0, 1), 2)
                for kt in range(8):
                    I("dve", "scalar_tensor_tensor", hT[:, kt, c0:c0 + n], xT[:, kt, c0:c0 + n], normw[:, l, kt:kt + 1],
                      T(2, n), ALU.mult, ALU.mult, reads=[r_xT[ci], r_par] + RT(2), writes=[r_hT[ci]])

        def proj(wv, wres, col0, m, ci, bi, boff=0):
            c0, n = TC[ci]
            for kt in range(8):
                I("pe", "matmul", banks[bi][0:m, boff:boff + n], wv[:, kt, col0:col0 + m], hT[:, kt, c0:c0 + n],
                  start=(kt == 0), stop=(kt == 7), reads=[wres, r_hT[ci]], writes=rb(bi, boff, n))

        def branch_a(l):
            wl = w_in[l].rearrange("(k p) c -> p k c", p=128)
            EXT0 = 0
            ext = AR[:, 0:2 + NP_]
            r_ext = RT(0, 5)
            ext_s = T(5, 160).rearrange("p (s t) -> p s t", t=10)
            r_exts = RT(5)
            I("pool", "memset", ext[:, 0:2], 0.0, writes=r_ext)
            for jt in range(4):
                parts = [wl[:, :, base + jt * 128: base + (jt + 1) * 128] for base in (A_X, A_B, A_C, A_Z)]
                (vx, vb, vc, vz), wres, wi = ws.take(parts)
                for j in range(2):
                    P.dma("sp", ext_s[:, :, j], st_conv[l, :, j, jt * 128:(jt + 1) * 128].rearrange("s c -> c s"),
                          writes=r_exts, sem="cst", **NC_DMA)
                for ci, (c0, n) in enumerate(TC):
                    smp = ci == 4
                    bx, bc, bz, bb = bank(), bank(), bank(), bank()
                    proj(vx, wres, 0, 128, ci, bx)
                    proj(vc, wres, 0, 128, ci, bc)
                    proj(vz, wres, 0, 128, ci, bz)
                    proj(vb, wres, 0, 128, ci, bb)
                    I("act", "activation", T(6, n), banks[bx][:, 0:n], AF.Copy, reads=rb(bx, 0, n), writes=RT(6))
                    v3 = lambda ap: ap.rearrange("p (s t) -> p s t", t=8)
                    if not smp:
                        I("dve", "tensor_tensor", ext[:, 2 + c0:2 + c0 + n], banks[bc][:, 0:n], T(6, n), ALU.mult,
                          reads=rb(bc, 0, n) + RT(6), writes=r_ext)
                    else:
                        I("dve", "tensor_tensor", ext_s[:, :, 2:10], v3(banks[bc][:, 0:128]), v3(T(6, 128)), ALU.mult,
                          reads=rb(bc, 0, n) + RT(6), writes=r_exts)
                    I("act", "activation", T(7, n), banks[bz][:, 0:n], AF.Silu, reads=rb(bz, 0, n), writes=RT(7))
                    I("dve", "tensor_tensor", T(8, n), banks[bb][:, 0:n], T(7, n), ALU.mult, reads=rb(bb, 0, n) + RT(7), writes=RT(8))
                    if not smp:
                        e0 = lambda k: ext[:, c0 + k:c0 + k + n]
                        acc = T(9, n)
                        rsrc = r_ext
                    else:
                        e0 = lambda k: ext_s[:, :, k:k + 8]
                        acc = v3(T(9, 128))
                        rsrc = r_exts
                    I("act", "activation", acc, e0(0), AF.Copy, scale=cw[:, l, 0, jt:jt + 1], reads=rsrc + [r_par], writes=RT(9))
                    for k in (1, 2):
                        I("dve", "scalar_tensor_tensor", acc, e0(k), cw[:, l, k, jt:jt + 1], acc, ALU.mult, ALU.add,
                          reads=rsrc + [r_par] + RT(9), writes=RT(9))
                    I("dve", "tensor_tensor", yT[:, jt, c0:c0 + n], T(9, n), T(8, n), ALU.mult, reads=RT(9) + RT(8), writes=[r_yT[ci]])
                    if ci == 3:
                        P.dma("sp", o_conv_p[l].rearrange("j c -> c j")[jt * 128:(jt + 1) * 128],
                              ext[:, NP_:NP_ + 2], reads=r_ext, sem="o_ext", final=True, **NC_DMA)
                    if ci == 4:
                        for j in range(2):
                            P.dma("sp", o_conv_s[l, :, j, jt * 128:(jt + 1) * 128].rearrange("s c -> c s"),
                                  ext_s[:, :, 8 + j], reads=r_exts, sem="o_exts", final=True, **NC_DMA)
                ws.release(wi)

        def branch_s5(l):
            wl = w_in[l].rearrange("(k p) c -> p k c", p=128)
            (vu,), wres, wi = ws.take([wl[:, :, S_U:S_U + 512]])
            for ci, (c0, n) in enumerate(TC):
                for ut in range(4):
                    bi = bank()
                    proj(vu, wres, ut * 128, 128, ci, bi)
                    if ut % 2 == 0:
                        I("act", "activation", yT[:, ut, c0:c0 + n], banks[bi][:, 0:n], AF.Copy, reads=rb(bi, 0, n), writes=[r_yT[ci]])
                    else:
                        I("dve", "tensor_copy", yT[:, ut, c0:c0 + n], banks[bi][:, 0:n], reads=rb(bi, 0, n), writes=[r_yT[ci]])
            ws.release(wi)
            stgB = TB(0).rearrange("p (r j n) -> p r j n", r=2, j=4)
            stgC = TB(1).rearrange("p (r j n) -> p r j n", r=2, j=4)
            Bt = TB(18).rearrange("p (r j n) -> p r j n", r=2, j=4)
            Ct = TB(19).rearrange("p (r j n) -> p r j n", r=2, j=4)
            h0re = T(20, 256).rearrange("p (j s) -> p j s", s=16)
            h0im = T(20, 256, 256).rearrange("p (j s) -> p j s", s=16)
            Hsre = T(21, 256).rearrange("p (j s) -> p j s", s=16)
            Hsim = T(21, 256, 256).rearrange("p (j s) -> p j s", s=16)
            for s_ in range(NSEQ):
                P.dma("sp", h0re[:, :, s_], st_re[l, s_].rearrange("(j g) n -> (g n) j", g=2), writes=RT(20), sem="s5st", **NC_DMA)
                P.dma("sp", h0im[:, :, s_], st_im[l, s_].rearrange("(j g) n -> (g n) j", g=2), writes=RT(20), sem="s5st", **NC_DMA)
            YB = [0, 1, 2, 3, 4]
            h2 = lambda ap: ap.rearrange("p (h x) -> p h x", h=2)

            def gen_tables(cb, lj):
                thn_c, r_c, zr_c, zi_c = s5thn[:, lj:lj + 1], s5r[:, lj:lj + 1], s5zr[:, lj:lj + 1], s5zi[:, lj:lj + 1]
                us, kf = T(cb + 0, TW), T(cb + 0, TW, TW)
                ki, fs = T(cb + 1, TW).bitcast(I32), T(cb + 1, TW, TW)
                TR, EC = T(cb + 5, TW), T(cb + 5, TW, TW)
                TI2, ES2 = T(cb + 6), T(cb + 7)
                rt, rt_s = T(cb + 8, TW), T(cb + 8, 128, TW)
                Es = ES2[:, TW:2 * TW]
                for (shift, dst, dtile) in ((64.0, Es, cb + 7), (64.25, EC, cb + 5)):
                    I("dve", "tensor_scalar", us, tp1[:, 0:TW], thn_c, shift, ALU.mult, ALU.add, reads=[r_tp1, r_par], writes=RT(cb + 0))
                    I("dve", "tensor_copy", ki, us, reads=RT(cb + 0), writes=RT(cb + 1))
                    I("dve", "tensor_copy", kf, ki, reads=RT(cb + 1), writes=RT(cb + 0))
                    I("dve", "tensor_tensor", fs, us, kf, ALU.subtract, reads=RT(cb + 0), writes=RT(cb + 1))
                    I("act", "activation", dst, fs, AF.Sin, scale=SIN_SCALE, reads=RT(cb + 1), writes=RT(dtile))
                tA, tB_ = T(cb + 3, TW), T(cb + 3, TW, TW)
                I("act", "activation", tA, Es, AF.Copy, scale=zi_c, reads=RT(cb + 7) + [r_par], writes=RT(cb + 3))
                I("dve", "scalar_tensor_tensor", TR, EC, zr_c, tA, ALU.mult, ALU.add, reads=RT(cb + 5) + RT(cb + 3) + [r_par], writes=RT(cb + 5))
                I("act", "activation", tB_, Es, AF.Copy, scale=zr_c, reads=RT(cb + 7) + [r_par], writes=RT(cb + 3))
                I("dve", "scalar_tensor_tensor", TI2[:, TW:2 * TW], EC, zi_c, tB_, ALU.mult, ALU.subtract,
                  reads=RT(cb + 5) + RT(cb + 3) + [r_par], writes=RT(cb + 6))
                I("act", "activation", TI2[:, 0:TW], TI2[:, TW:2 * TW], AF.Copy, scale=-1.0, reads=RT(cb + 6), writes=RT(cb + 6))
                I("act", "activation", ES2[:, 0:TW], Es, AF.Copy, scale=-1.0, reads=RT(cb + 7), writes=RT(cb + 7))
                I("dve", "tensor_scalar", rt, tp1[:, 0:TW], 0.0, r_c, ALU.mult, ALU.add, reads=[r_tp1, r_par], writes=RT(cb + 8))
                I("dve", "tensor_scalar", rt_s, cm[:, 512:640], r_c, None, ALU.mult, reads=[r_cm, r_par], writes=RT(cb + 8))

            def front(cb, bk, ut, jp, c0, n):
                ci = c0 // 512
                u_rhs = yT[:, ut, c0:c0 + n]
                I("pe", "matmul", banks[bk][:, 0:n], Bt[:, 0, jp, :], u_rhs, start=True, stop=True,
                  reads=RT(18) + [r_yT[ci]], writes=rb(bk, 0, n))
                I("pe", "matmul", banks[bk][:, 256:256 + n], Bt[:, 1, jp, :], u_rhs, start=True, stop=True,
                  reads=RT(18) + [r_yT[ci]], writes=rb(bk, 256, n))
                I("act", "activation", h2(T(cb + 0))[:, :, 0:n], h2(banks[bk][:, :])[:, :, 0:n], AF.Copy,
                  reads=rb(bk, 0, n) + rb(bk, 256, n), writes=RT(cb + 0))

            def window(cb, bk, ut, jp, j, lj, wi_, c0, n, smp, nwin, nxt=None):
                r_c = s5r[:, lj:lj + 1]
                ci = c0 // 512
                boff = c0 % 512
                TR, EC = T(cb + 5, TW), T(cb + 5, TW, TW)
                TI2, ES2 = h2(T(cb + 6)), h2(T(cb + 7))
                rt, rt_s = T(cb + 8, TW), T(cb + 8, 128, TW)
                Xp = h2(banks[bk][:, :])[:, :, 0:n]
                Xs = h2(T(cb + 0))[:, :, 0:n]
                A = h2(T(cb + 1))[:, :, 0:n]
                Bm = h2(T(cb + 2))[:, :, 0:n]
                G = h2(T(cb + 3))[:, :, 0:n]
                A2 = A
                Hb = h2(TB(cb + 4, 512))[:, :, 0:n]
                f3 = lambda ap: ap.rearrange("p h (s t) -> p h s t", t=8)
                if not smp:
                    bc2 = lambda tab: tab[:, 0:n].unsqueeze(1).broadcast_to([128, 2, n])
                    tv = lambda ap: ap
                    sv = lambda ap: ap
                    rtab = rt[:, 0:n]
                else:
                    bc2 = lambda tab: tab[:, 0:8].unsqueeze(1).unsqueeze(1).broadcast_to([128, 2, NSEQ, 8])
                    tv = lambda ap: ap[:, 0:8].unsqueeze(1).broadcast_to([128, NSEQ, 8])
                    sv = lambda ap: ap.rearrange("p (s t) -> p s t", t=8)
                    rtab = rt_s
                if not smp:
                    I("dve", "tensor_tensor", A, Xs, bc2(TR), ALU.mult, reads=RT(cb + 0) + RT(cb + 5), writes=RT(cb + 1))
                else:
                    I("dve", "tensor_tensor", f3(A), f3(Xs), bc2(TR), ALU.mult, reads=RT(cb + 0) + RT(cb + 5), writes=RT(cb + 1))
                yield
                I("pool", "tensor_tensor", sv(Bm[:, 0, :]), sv(Xs[:, 1, :]), tv(TI2[:, 0, :]), ALU.mult, reads=RT(cb + 0) + RT(cb + 6), writes=RT(cb + 2))
                I("pool", "tensor_tensor", sv(Bm[:, 1, :]), sv(Xs[:, 0, :]), tv(TI2[:, 1, :]), ALU.mult, reads=RT(cb + 0) + RT(cb + 6), writes=RT(cb + 2))
                yield
                if nxt is not None:
                    front(cb, bk, ut, jp, nxt[0], nxt[1])
                    yield
                I("dve", "tensor_tensor", A, A, Bm, ALU.add, reads=RT(cb + 1) + RT(cb + 2), writes=RT(cb + 1))
                yield
                if smp:
                    w0r = sv(A[:, 0, :])[:, :, 0]
                    w0i = sv(A[:, 1, :])[:, :, 0]
                    I("dve", "scalar_tensor_tensor", w0r, h0re[:, j, :], r_c, w0r, ALU.mult, ALU.add,
                      reads=RT(20) + RT(cb + 1) + [r_par], writes=RT(cb + 1))
                    I("dve", "scalar_tensor_tensor", w0i, h0im[:, j, :], r_c, w0i, ALU.mult, ALU.add,
                      reads=RT(20) + RT(cb + 1) + [r_par], writes=RT(cb + 1))
                init_re = 0.0 if (wi_ == 0 or smp) else Hst[:, j, 0:1]
                init_im = 0.0 if (wi_ == 0 or smp) else Hst[:, j, 1:2]
                I("dve", "tensor_tensor_scan", G[:, 0, :], rtab, A[:, 0, :], init_re, ALU.mult, ALU.add,
                  reads=RT(cb + 8) + RT(cb + 1) + [r_H], writes=RT(cb + 3))
                I("dve", "tensor_tensor_scan", G[:, 1, :], rtab, A[:, 1, :], init_im, ALU.mult, ALU.add,
                  reads=RT(cb + 8) + RT(cb + 1) + [r_H], writes=RT(cb + 3))
                yield
                if not smp:
                    I("dve", "tensor_tensor", A2, G, bc2(EC), ALU.mult, reads=RT(cb + 3) + RT(cb + 5), writes=RT(cb + 1))
                else:
                    I("dve", "tensor_tensor", f3(A2), f3(G), bc2(EC), ALU.mult, reads=RT(cb + 3) + RT(cb + 5), writes=RT(cb + 1))
                I("pool", "tensor_tensor", sv(Bm[:, 0, :]), sv(G[:, 1, :]), tv(ES2[:, 0, :]), ALU.mult, reads=RT(cb + 3) + RT(cb + 7), writes=RT(cb + 2))
                I("pool", "tensor_tensor", sv(Bm[:, 1, :]), sv(G[:, 0, :]), tv(ES2[:, 1, :]), ALU.mult, reads=RT(cb + 3) + RT(cb + 7), writes=RT(cb + 2))
                yield
                I("dve", "tensor_tensor", Hb, A2, Bm, ALU.add, reads=RT(cb + 1) + RT(cb + 2), writes=RT(cb + 4))
                yield
                if not smp:
                    I("dve", "tensor_tensor", Hst[:, j, :], A2[:, :, n - 1], Bm[:, :, n - 1], ALU.add, reads=RT(cb + 1) + RT(cb + 2), writes=[r_H])
                else:
                    I("dve", "tensor_tensor", Hsre[:, j, :], sv(A2[:, 0, :])[:, :, 7], sv(Bm[:, 0, :])[:, :, 7], ALU.add,
                      reads=RT(cb + 1) + RT(cb + 2), writes=RT(21))
                    I("dve", "tensor_tensor", Hsim[:, j, :], sv(A2[:, 1, :])[:, :, 7], sv(Bm[:, 1, :])[:, :, 7], ALU.add,
                      reads=RT(cb + 1) + RT(cb + 2), writes=RT(21))
                yb = YB[ci]
                I("pe", "matmul", banks[yb][:, boff:boff + n], Ct[:, 0, jp, :], Hb[:, 0, :], start=(jp == 0 and boff == 0), stop=False,
                  skip_group_check=True, reads=RT(19) + RT(cb + 4), writes=rb(yb, boff, n))
                I("pe", "matmul", banks[yb][:, boff:boff + n], Ct[:, 1, jp, :], Hb[:, 1, :], start=False, stop=(jp == 3 and (boff != 0 or ci == 4)),
                  skip_group_check=True, reads=RT(19) + RT(cb + 4), writes=rb(yb, boff, n))

            wins = [(w * TW, TW, False) for w in range(NP_ // TW)] + [(NP_, NS_, True)]
            for ut in range(4):
                I("pool", "memset", TB(0), 0.0, writes=RT(0))
                I("pool", "memset", TB(1), 0.0, writes=RT(1))
                for jp in range(4):
                    j = 4 * ut + jp
                    for g in range(2):
                        grp = 2 * j + g
                        for ri, (bsrc, csrc) in enumerate(((s5_b_re, s5_c_re), (s5_b_im, s5_c_im))):
                            P.dma("pool", stgB[g * 64:(g + 1) * 64, ri, jp, 32 * jp + 16 * g:32 * jp + 16 * g + 16],
                                  bsrc[l, grp], writes=RT(0), sem="s5wB")
                            P.dma("pool", stgC[32 * jp + 16 * g:32 * jp + 16 * g + 16, ri, jp, g * 64:(g + 1) * 64],
                                  csrc[l, grp], writes=RT(1), sem="s5wC")
                for (stg, dstT, rs_, rd_, negim) in ((stgB, Bt, RT(0), RT(18), False), (stgC, Ct, RT(1), RT(19), True)):
                    for ri in range(2):
                        bi = 7
                        pb = banks[bi][:, :].bitcast(BF16)
                        for jp in range(4):
                            I("pe", "transpose", pb[:, jp * 128:(jp + 1) * 128], stg[:, ri, jp, :], ident_b[:],
                              reads=rs_ + [r_identb], writes=rb(bi))
                        dst = dstT[:, ri].rearrange("p j n -> p (j n)")
                        if negim and ri == 1:
                            I("act", "activation", dst, pb[:, 0:512], AF.Copy, scale=-1.0, reads=rb(bi), writes=rd_)
                        else:
                            I("act", "activation", dst, pb[:, 0:512], AF.Copy, reads=rb(bi), writes=rd_)
                for pair in range(2):
                    jps = (2 * pair, 2 * pair + 1)
                    for c, jp in enumerate(jps):
                        gen_tables(9 * c, l * 16 + 4 * ut + jp)
                    for c, jp in enumerate(jps):
                        front(9 * c, 5 + c, ut, jp, wins[0][0], wins[0][1])
                    for wi_, (c0, n, smp) in enumerate(wins):
                        nxt = wins[wi_ + 1][0:2] if wi_ + 1 < len(wins) else None
                        interleave([window(9 * c, 5 + c, ut, jp, 4 * ut + jp, l * 16 + 4 * ut + jp, wi_, c0, n, smp, len(wins), nxt)
                                    for c, jp in enumerate(jps)])
                if ut == 3:
                    P.dma("sp", o_re_p[l].rearrange("(j p) -> p j", p=128), Hst[:, :, 0], reads=[r_H], sem="o_H", final=True, **NC_DMA)
                    P.dma("sp", o_im_p[l].rearrange("(j p) -> p j", p=128), Hst[:, :, 1], reads=[r_H], sem="o_H", final=True, **NC_DMA)
                for ci, (c0, n) in enumerate(TC):
                    ys, t_, t2 = T(9, n), T(10, n), T(11, n)
                    I("dve", "scalar_tensor_tensor", ys, yT[:, ut, c0:c0 + n], s5dd[:, l, ut:ut + 1], banks[YB[ci]][:, 0:n],
                      ALU.mult, ALU.add, reads=[r_yT[ci], r_par] + rb(YB[ci], 0, n), writes=RT(9))
                    I("act", "activation", t_, ys, AF.Square, reads=RT(9), writes=RT(10))
                    I("dve", "tensor_scalar", t_, t_, 0.044715, 1.0, ALU.mult, ALU.add, reads=RT(10), writes=RT(10))
                    I("pool", "tensor_tensor", t_, t_, ys, ALU.mult, reads=RT(10) + RT(9), writes=RT(10))
                    I("act", "activation", t2, t_, AF.Sigmoid, scale=GELU_C, reads=RT(10), writes=RT(11))
                    I("dve", "tensor_tensor", yT[:, ut, c0:c0 + n], ys, t2, ALU.mult, reads=RT(9) + RT(11), writes=[r_yT[ci]])
            for s_ in range(NSEQ):
                P.dma("sp", o_re_s[l, s_].rearrange("(j p) -> p j", p=128), Hsre[:, :, s_], reads=RT(21), sem="o_Hs", final=True, **NC_DMA)
                P.dma("sp", o_im_s[l, s_].rearrange("(j p) -> p j", p=128), Hsim[:, :, s_], reads=RT(21), sem="o_Hs", final=True, **NC_DMA)
            (vg,), gres, gi = ws.take([w_glu[l].rearrange("(k p) c -> p k c", p=128)])
            (vz,), zres, zi_ = ws.take([wl[:, :, S_Z:S_Z + 512]])
            resb = TB(0, 2048, 0, 2).rearrange("p (k n) -> p k n", k=4)
            for ci, (c0, n) in enumerate(TC):
                for ot in range(4):
                    b1, b2 = bank(), bank()
                    for kt in range(4):
                        I("pe", "matmul", banks[b1][:, 0:n], vg[:, kt, ot * 128:(ot + 1) * 128], yT[:, kt, c0:c0 + n],
                          start=(kt == 0), stop=(kt == 3), reads=[gres, r_yT[ci]], writes=rb(b1, 0, n))
                    proj(vz, zres, ot * 128, 128, ci, b2)
                    I("act", "activation", T(2, n), banks[b1][:, 0:n], AF.Sigmoid, bias=bglu[:, l, ot:ot + 1], reads=rb(b1, 0, n) + [r_par], writes=RT(2))
                    I("act", "activation", T(3, n), banks[b2][:, 0:n], AF.Silu, reads=rb(b2, 0, n), writes=RT(3))
                    I("dve", "tensor_tensor", T(4, n), yT[:, ot, c0:c0 + n], T(2, n), ALU.mult, reads=[r_yT[ci]] + RT(2), writes=RT(4))
                    I("dve", "tensor_tensor", resb[:, ot, 0:n], T(4, n), T(3, n), ALU.mult, reads=RT(4) + RT(3), writes=RT(0, 2))
                I("pool", "tensor_copy", yT[:, :, c0:c0 + n], resb[:, :, 0:n], reads=RT(0, 2), writes=[r_yT[ci]])
            ws.release(gi)
            ws.release(zi_)

        def branch_gated(l, kind):
            wl = w_in[l].rearrange("(k p) c -> p k c", p=128)
            hg = kind == "hgrn"
            K = 128 if hg else 64
            if hg:
                (vq,), rq, iq = ws.take([wl[:, :, C_Q:C_Q + 512]])
                (vf,), rf, if_ = ws.take([wl[:, :, C_F:C_F + 512]])
                (vv,), rv, iv = ws.take([wl[:, :, C_I:C_I + 512]])
                (vz,), rz, iz = ws.take([wl[:, :, C_Z:C_Z + 512]])
                handles = [iq, if_, iv, iz]
                gn = gn_h
                st_in, o_p, o_s = st_hg, o_hg_p, o_hg_s
            else:
                (vqk,), rq, iq = ws.take([wl[:, :, G_Q:G_Q + 512]])
                (vv,), rv, iv = ws.take([wl[:, :, G_V:G_V + 512]])
                (vz,), rz, iz = ws.take([wl[:, :, G_Z:G_Z + 528]])
                handles = [iq, iv, iz]
                gn = gn_g
                st_in, o_p, o_s = st_gla, o_gla_p, o_gla_s
            qb = TB(0, 2048, 0, 2).rearrange("p (h n) -> p h n", h=4)
            kb = TB(2, 2048, 0, 2).rearrange("p (h n) -> p h n", h=4)
            oT = AR[:, 4 * 512:8 * 512].rearrange("p (h n) -> p h n", h=4)
            vtok = [TB(8, 512, 0), TB(8, 512, 512)]
            attm = [TB(9, 64, 128 * i) for i in range(4)]
            kbTs = [TB(9, 128, 512 + 128 * i) for i in range(4)]
            Sf = T(10).rearrange("p (h n) -> p h n", h=4)
            Sb = TB(11, 512, 0).rearrange("p (h n) -> p h n", h=4)
            elast = T(11, 64, 256).rearrange("p (h n) -> p h n", h=4)
            Ss = [T(12, 128, 128 * i) for i in range(4)]
            Ssb = [TB(13, 128, 128 * i) for i in range(4)]
            tmpS = [T(18, 128, 128 * i) for i in range(4)]
            rT = TB(19, 512, 0)
            r_vtok = [Res("vtok%d" % i) for i in range(2)]
            r_attm = [Res("attm%d" % i) for i in range(4)]
            r_kbT = [Res("kbT%d" % i) for i in range(4)]
            r_Sf = [Res("Sf%d" % i) for i in range(4)]
            r_Sb = [Res("Sb%d" % i) for i in range(4)]
            r_el = Res("elast")
            r_Ss = [Res("Ss%d" % i) for i in range(4)]
            r_Ssb = [Res("Ssb%d" % i) for i in range(4)]
            r_tmp = [Res("tmpS%d" % i) for i in range(4)]
            fine = {8: r_vtok, 9: r_attm + r_kbT, 10: r_Sf, 11: r_Sb + [r_el], 12: r_Ss, 13: r_Ssb, 18: r_tmp}

            def touch():
                for tile_, toks in fine.items():
                    I("pool", "memset", T(tile_, 2), 0.0, writes=RT(tile_) + toks)
            touch()
            I("pool", "memset", T(10), 0.0, writes=r_Sf)
            I("pool", "memset", TB(11, 512, 0), 0.0, writes=r_Sb)
            BQ, BF_, BV = (0, 1), (0, 1), (2, 2)
            vcount = {"i": 0}
            ucount = {"i": 0}
            for ci, (c0, n) in enumerate(TC):
                smp = ci == 4
                csz = 8 if smp else 64
                nch = n // csz
                cmt = cm[:, 512:640] if smp else cm[:, 0:512]
                if not hg:
                    bi = bank() % 2
                    proj(vz, rz, 512, 16, ci, bi)
                    I("act", "activation", rT[0:16, 0:n], banks[bi][0:16, 0:n], AF.Copy, reads=rb(bi, 0, n), writes=RT(19))
                for hd in range(4):
                    b1, b2 = 0, 1
                    t0_, t1_, t2_, t3_ = T(14, n), T(15, n), T(16, n), T(17, n)
                    if hg:
                        proj(vq, rq, hd * 128, 128, ci, b1)
                        proj(vf, rf, hd * 128, 128, ci, b2)
                        I("act", "activation", t0_, banks[b2][:, 0:n], AF.Sigmoid, reads=rb(b2, 0, n), writes=RT(14))
                        I("dve", "tensor_scalar", t0_, t0_, oml[:, l, hd:hd + 1], lbp[:, l, hd:hd + 1], ALU.mult, ALU.add,
                          reads=RT(14) + [r_par], writes=RT(14))
                        I("act", "activation", t1_, t0_, AF.Ln, reads=RT(14), writes=RT(15))
                        I("dve", "tensor_tensor_scan", t2_, cmt, t1_, 0.0, ALU.mult, ALU.add, reads=[r_cm] + RT(15), writes=RT(16))
                        I("act", "activation", t1_, t2_, AF.Exp, reads=RT(16), writes=RT(15))
                        I("act", "activation", t3_, t2_, AF.Exp, scale=-1.0, reads=RT(16), writes=RT(17))
                        I("dve", "tensor_scalar", t0_, t0_, -1.0, 1.0, ALU.mult, ALU.add, reads=RT(14), writes=RT(14))
                        I("dve", "tensor_tensor", kb[:, hd, 0:n], t0_, t3_, ALU.mult, reads=RT(14) + RT(17), writes=RT(2, 2))
                        I("act", "activation", t2_, banks[b1][:, 0:n], AF.Silu, reads=rb(b1, 0, n), writes=RT(16))
                        I("dve", "scalar_tensor_tensor", qb[:, hd, 0:n], t2_, float(K) ** -0.5, t1_, ALU.mult, ALU.mult,
                          reads=RT(16) + RT(15), writes=RT(0, 2))
                    else:
                        proj(vqk, rq, hd * 64, 64, ci, b1)
                        proj(vqk, rq, 256 + hd * 64, 64, ci, b2)
                        b3 = 6
                        I("pe", "matmul", banks[b3][0:64, 0:n], wgk[0:16, l, hd * 64:(hd + 1) * 64], rT[0:16, 0:n], start=True, stop=True,
                          reads=[r_wgk] + RT(19), writes=rb(b3, 0, n))
                        I("act", "activation", t0_[0:64], banks[b3][0:64, 0:n], AF.Exp, scale=-1.0, bias=nbgk[:, l, hd:hd + 1],
                          reads=rb(b3, 0, n) + [r_par], writes=RT(14))
                        I("act", "activation", t1_[0:64], t0_[0:64], AF.Ln, bias=1.0, reads=RT(14), writes=RT(15))
                        I("dve", "tensor_tensor_scan", t2_[0:64], cmt[0:64], t1_[0:64], 0.0, ALU.mult, ALU.add, reads=[r_cm] + RT(15), writes=RT(16))
                        I("act", "activation", t1_[0:64], t2_[0:64], AF.Exp, scale=-1.0 / 16.0, reads=RT(16), writes=RT(15))
                        I("act", "activation", t3_[0:64], t2_[0:64], AF.Exp, scale=1.0 / 16.0, reads=RT(16), writes=RT(17))
                        I("dve", "tensor_tensor", kb[0:64, hd, 0:n], banks[b2][0:64, 0:n], t3_[0:64], ALU.mult, reads=rb(b2, 0, n) + RT(17), writes=RT(2, 2))
                        I("dve", "scalar_tensor_tensor", qb[0:64, hd, 0:n], banks[b1][0:64, 0:n], float(K) ** -0.5, t1_[0:64], ALU.mult, ALU.mult,
                          reads=rb(b1, 0, n) + RT(15), writes=RT(0, 2))
                    ev = t1_[0:K].rearrange("p (c t) -> p c t", t=csz)[:, :, csz - 1]
                    I("pool", "tensor_copy", elast[0:K, hd, 0:nch], ev, reads=RT(15), writes=[r_el])
                def unit(ch, hd, vi, u_):
                    tl = ch * csz
                    pa = hd
                    if smp:
                        su = ch * 4 + hd
                        p4 = su % 4
                        S_f, S_b = Ss[p4][0:K, :], Ssb[p4][0:K, :]
                        for ahead in ((0, 1, 2) if su == 0 else (2,)):
                            sn = su + ahead
                            if sn < 4 * nch:
                                P.dma("sp", Ss[sn % 4][0:K, :], st_in[l, sn // 4, sn % 4], writes=[r_Ss[sn % 4]], sem="sst%d" % (sn % 4))
                        I("act", "activation", S_b, S_f, AF.Copy, reads=[r_Ss[p4]], writes=[r_Ssb[p4]])
                        rSf, rSb = [r_Ss[p4]], [r_Ssb[p4]]
                    else:
                        S_f, S_b = Sf[0:K, hd, :], Sb[0:K, hd, :]
                        rSf, rSb = [r_Sf[hd]], [r_Sb[hd]]
                    q_c = qb[0:K, hd, tl:tl + csz]
                    k_c = kb[0:K, hd, tl:tl + csz]
                    ev_ = hd % 2 == 0
                    ab, tbk, obk, pk = (4, 6, 5, 7) if ev_ else (2, 3, 0, 1)
                    I("pe", "matmul", banks[ab][0:csz, 0:csz], k_c, q_c, start=True, stop=True,
                      reads=RT(0, 4), writes=rb(ab))
                    pbf = banks[tbk][:, :].bitcast(BF16)
                    I("pe", "transpose", pbf[0:csz, 0:K], k_c, ident_b[0:K, 0:K], reads=RT(2, 2) + [r_identb], writes=rb(tbk))
                    yield
                    I("dve", "tensor_tensor", attm[pa][0:csz, 0:csz], banks[ab][0:csz, 0:csz], mask64[0:csz, 0:csz], ALU.mult,
                      reads=rb(ab) + [r_mask], writes=[r_attm[pa]])
                    I("dve", "tensor_copy", kbTs[pa][0:csz, 0:K], pbf[0:csz, 0:K], reads=rb(tbk), writes=[r_kbT[pa]])
                    yield
                    I("pe", "matmul", banks[obk][:, 0:csz], S_b, q_c, start=True, stop=False,
                      reads=rSb + RT(0, 2), writes=rb(obk))
                    I("pe", "matmul", banks[obk][:, 0:csz], vtok[vi][0:csz, hd * 128:(hd + 1) * 128], attm[pa][0:csz, 0:csz],
                      start=False, stop=True, reads=[r_vtok[vi], r_attm[pa]], writes=rb(obk))
                    I("pe", "matmul", banks[pk][0:K, 0:128], kbTs[pa][0:csz, 0:K], vtok[vi][0:csz, hd * 128:(hd + 1) * 128],
                      start=True, stop=True, reads=[r_kbT[pa], r_vtok[vi]], writes=rb(pk))
                    yield
                    I("act", "activation", oT[:, hd, tl:tl + csz], banks[obk][:, 0:csz], AF.Copy, reads=rb(obk), writes=RT(4, 4))
                    tm = tmpS[pa][0:K, :]
                    I("dve", "tensor_tensor", tm, banks[pk][0:K, 0:128], S_f, ALU.add, reads=rb(pk) + rSf, writes=[r_tmp[pa]])
                    yield
                    I("act", "activation", S_f, tm, AF.Copy, scale=elast[0:K, hd, ch:ch + 1], reads=[r_tmp[pa], r_el], writes=rSf)
                    if not smp:
                        I("dve", "tensor_scalar", S_b, tm, elast[0:K, hd, ch:ch + 1], None, ALU.mult, reads=[r_tmp[pa], r_el], writes=rSb)
                    if smp:
                        P.dma("sp", o_s[l, ch, hd], S_f, reads=rSf, sem="o_ss%d" % p4, final=True)
                    elif ci == 3 and ch == nch - 1:
                        P.dma("sp", o_p[l, hd], S_f, reads=rSf, sem="o_sp", final=True)
                    yield

                for ch in range(nch):
                    t0g = c0 + ch * csz
                    vi = vcount["i"] % 2
                    vcount["i"] += 1
                    bv = BV[vi]
                    for kt in range(8):
                        I("pe", "matmul", banks[bv][0:csz, 0:512], hT[:, kt, t0g:t0g + csz], vv[:, kt, :], start=(kt == 0), stop=(kt == 7),
                          reads=[rv, r_hT[ci]], writes=rb(bv))
                    I("act", "activation", vtok[vi][0:csz, :], banks[bv][0:csz, 0:512], AF.Copy, reads=rb(bv), writes=[r_vtok[vi]])
                    u0 = ucount["i"]
                    ucount["i"] += 4
                    interleave([unit(ch, hd, vi, u0 + hd) for hd in (0, 1)])
                    interleave([unit(ch, hd, vi, u0 + hd) for hd in (2, 3)])
                for hd in range(4):
                    rstd_tile([oT[:, hd, 0:n]], RT(4, 4), n, 1.0 / 128.0, (20,), 21)
                    bz = bank() % 2 + 2
                    proj(vz, rz, hd * 128, 128, ci, bz)
                    I("act", "activation", T(20, n), banks[bz][:, 0:n], AF.Silu, reads=rb(bz, 0, n), writes=RT(20))
                    I("dve", "scalar_tensor_tensor", T(19, n), oT[:, hd, 0:n], gn[:, l:l + 1], T(21, n), ALU.mult, ALU.mult,
                      reads=RT(4, 4) + RT(21) + [r_par], writes=RT(19))
                    I("dve", "tensor_tensor", yT[:, hd, c0:c0 + n], T(19, n), T(20, n), ALU.mult, reads=RT(19) + RT(20), writes=[r_yT[ci]])
            touch()
            for h_ in handles:
                ws.release(h_)

        def merge_branch(l, b):
            wl = w_in[l].rearrange("(k p) c -> p k c", p=128)
            wbl = w_branch[l, b].rearrange("(k p) c -> p k c", p=128)
            wol = w_out[l].rearrange("(k p) c -> p k c", p=128)
            mb = TB(0, 2048, 0, 2).rearrange("p (k n) -> p k n", k=4)
            for hf in range(2):
                g0 = M_G + b * 1024 + hf * 512
                (vg,), rg, ig = ws.take([wl[:, :, g0:g0 + 512]])
                (vb,), rbw, ib = ws.take([wbl[:, :, hf * 512:(hf + 1) * 512]])
                (vo,), ro, io = ws.take([wol[:, hf * 4:(hf + 1) * 4, :]])
                for ci, (c0, n) in enumerate(TC):
                    for dt in range(4):
                        b1, b2 = bank(), bank()
                        for kt in range(4):
                            I("pe", "matmul", banks[b1][:, 0:n], vb[:, kt, dt * 128:(dt + 1) * 128], yT[:, kt, c0:c0 + n],
                              start=(kt == 0), stop=(kt == 3), reads=[rbw, r_yT[ci]], writes=rb(b1, 0, n))
                        proj(vg, rg, dt * 128, 128, ci, b2)
                        I("act", "activation", T(2 + dt % 2, n), banks[b2][:, 0:n], AF.Sigmoid, reads=rb(b2, 0, n), writes=RT(2 + dt % 2))
                        I("dve", "tensor_tensor", mb[:, dt, 0:n], banks[b1][:, 0:n], T(2 + dt % 2, n), ALU.mult,
                          reads=rb(b1, 0, n) + RT(2 + dt % 2), writes=RT(0, 2))
                    for od in range(8):
                        b3 = bank()
                        for kt in range(4):
                            I("pe", "matmul", banks[b3][:, 0:n], vo[:, kt, od * 128:(od + 1) * 128], mb[:, kt, 0:n],
                              start=(kt == 0), stop=(kt == 3), reads=[ro] + RT(0, 2), writes=rb(b3, 0, n))
                        I("dve", "tensor_tensor", xT[:, od, c0:c0 + n], banks[b3][:, 0:n], xT[:, od, c0:c0 + n], ALU.add,
                          reads=rb(b3, 0, n) + [r_xT[ci]], writes=[r_xT[ci]])
                ws.release(ig)
                ws.release(ib)
                ws.release(io)

        def final_out():
            for ci, (c0, n) in enumerate(TC):
                rstd_tile([xT[:, kt, c0:c0 + n] for kt in range(8)], [r_xT[ci]], n, 1.0 / D, (0, 1), 2)
                for sub in range(n // 128):
                    t0 = c0 + sub * 128
                    a0 = 18 + (sub % 2) * 2
                    ob = AR[:, a0 * 512:(a0 + 2) * 512]
                    rob = RT(a0, 2)
                    for h in range(2):
                        bi = bank()
                        for q in range(4):
                            kt = h * 4 + q
                            w_ = 3 + (kt % 2)
                            I("dve", "scalar_tensor_tensor", T(w_, 128), xT[:, kt, t0:t0 + 128], fnorm[:, kt:kt + 1],
                              T(2, 128, sub * 128), ALU.mult, ALU.mult, reads=[r_xT[ci], r_par] + RT(2), writes=RT(w_))
                            I("pe", "transpose", banks[bi][:, q * 128:(q + 1) * 128], T(w_, 128), ident_f[:],
                              reads=RT(w_) + [r_identf], writes=rb(bi))
                        if h == 0:
                            I("act", "activation", ob[:, 0:512], banks[bi][:, :], AF.Copy, reads=rb(bi), writes=rob)
                        else:
                            I("dve", "tensor_copy", ob[:, 512:1024], banks[bi][:, :], reads=rb(bi), writes=rob)
                    dst = o_yp[t0:t0 + 128, :] if ci < 4 else o_ys
                    P.dma("sp", dst, ob, reads=rob, sem="o_y%d" % (sub % 2), final=True)

        def emit_all():
            rr["i"] = 0
            P.mark("setup")
            setup()
            for l in range(DEPTH):
                P.mark("L%d norm" % l)
                rmsnorm_h(l)
                if stage in (1, 99):
                    P.mark("L%d A" % l)
                    branch_a(l)
                    if stage == 99:
                        P.mark("L%d mergeA" % l)
                        merge_branch(l, 0)
                if stage in (2, 99):
                    P.mark("L%d S5" % l)
                    branch_s5(l)
                    if stage == 99:
                        P.mark("L%d mergeS5" % l)
                        merge_branch(l, 1)
                if stage in (3, 99):
                    P.mark("L%d hgrn" % l)
                    branch_gated(l, "hgrn")
                    if stage == 99:
                        P.mark("L%d mergeH" % l)
                        merge_branch(l, 2)
                if stage in (4, 99):
                    P.mark("L%d gla" % l)
                    branch_gated(l, "gla")
                    if stage == 99:
                        P.mark("L%d mergeG" % l)
                        merge_branch(l, 3)
                if stage != 99:
                    break
            P.mark("final")
            final_out()
            P.mark("end")

        P.dry = True
        emit_all()
        P.dry = False
        ws.reset()
        emit_all()
        P.emit()
        if os.environ.get("MK_MARKS"):
            import json
            json.dump(P.marks, open(os.environ["MK_MARKS"], "w"))
        print("[kernel] instr per engine:", {e: len(s) for e, s in P.streams.items()},
              "sbuf left:", nc.sbuf_bytes_remaining, "sems:", len(P.sems), flush=True)
    return nc


_CONSTS = None


def _consts():
    global _CONSTS
    if _CONSTS is None:
        cm = np.ones((128, 640), np.float32)
        cm[:, 0:512:64] = 0.0
        cm[:, 512:640:8] = 0.0
        s = np.arange(64)
        _CONSTS = {
            "c_ident": np.eye(128, dtype=np.float32),
            "c_ones": np.ones((128, 128), np.float32),
            "c_mask64": (s[:, None] <= s[None, :]).astype(np.float32),
            "c_cm": cm,
            "c_tp1": np.broadcast_to(np.arange(1, 257, dtype=np.float32), (128, 256)).copy(),
        }
    return _CONSTS


_WNAMES = ["norm_w", "w_in", "conv_w", "s5_a_re", "s5_a_im", "s5_log_dt", "s5_b_re", "s5_b_im",
           "s5_c_re", "s5_c_im", "s5_d", "w_glu", "b_glu", "hgrn_lb_raw", "hgrn_norm", "w_gk", "b_gk",
           "gla_norm", "w_branch", "w_out", "final_norm"]


def kernel(**inputs):
    global LAST_RESULTS
    stage = int(os.environ.get("MK_STAGE", "99"))
    nc = build_program(stage)
    f = lambda a: np.ascontiguousarray(np.asarray(a, dtype=np.float32))
    shared = {k: f(inputs[k]) for k in _WNAMES}
    shared.update(_consts())
    x_prompt = f(inputs["x_prompt"]); x_sample = f(inputs["x_sample"])
    in_maps = []
    for c in range(8):
        sq = slice(c * NSEQ, (c + 1) * NSEQ)
        m = dict(shared)
        m["xp"] = x_prompt[c]
        m["xs"] = x_sample[sq].reshape(NS_, D)
        m["st_conv"] = f(inputs["state_conv"][:, sq])
        m["st_re"] = f(inputs["state_ssm_re"][:, sq])
        m["st_im"] = f(inputs["state_ssm_im"][:, sq])
        m["st_hg"] = f(inputs["state_hgrn"][:, sq])
        m["st_gla"] = f(inputs["state_gla"][:, sq])
        in_maps.append(m)
    res = run_bass_kernel_spmd(nc, in_maps, core_ids=list(range(8)))
    R = res.results
    LAST_RESULTS = R
    cat_p = lambda k, shp: np.stack([R[c][k].reshape(shp) for c in range(8)], axis=1)
    cat_s = lambda k, shp: np.concatenate([R[c][k].reshape(shp) for c in range(8)], axis=1)
    y_prompt = np.stack([R[c]["o_yp"] for c in range(8)], axis=0)
    y_sample = np.concatenate([R[c]["o_ys"].reshape(NSEQ, 8, D) for c in range(8)], axis=0)
    return (y_prompt, y_sample,
            cat_p("o_conv_p", (DEPTH, 2, WBR)), cat_s("o_conv_s", (DEPTH, NSEQ, 2, WBR)),
            cat_p("o_re_p", (DEPTH, 32, 64)), cat_s("o_re_s", (DEPTH, NSEQ, 32, 64)),
            cat_p("o_im_p", (DEPTH, 32, 64)), cat_s("o_im_s", (DEPTH, NSEQ, 32, 64)),
            cat_p("o_hg_p", (DEPTH, 4, 128, 128)), cat_s("o_hg_s", (DEPTH, NSEQ, 4, 128, 128)),
            cat_p("o_gla_p", (DEPTH, 4, 64, 128)), cat_s("o_gla_s", (DEPTH, NSEQ, 4, 64, 128)))
```

```python
import math
import os
from contextlib import ExitStack

import numpy as np
import concourse.bass as bass
import concourse.mybir as mybir
from concourse.bass_utils import run_bass_kernel_spmd

F32 = mybir.dt.float32
BF16 = mybir.dt.bfloat16
I32 = mybir.dt.int32
ALU = mybir.AluOpType
AF = mybir.ActivationFunctionType

D = 1024
NP_ = 2048
NS_ = 128
NT = NP_ + NS_
DEPTH = 4
WBR = 512
DIN = 10768
NSEQ = 16
EPS = 1e-6
TC = [(0, 512), (512, 512), (1024, 512), (1536, 512), (2048, 128)]
A_X, A_B, A_C, A_Z = 0, 512, 1024, 1536
S_U, S_Z = 2048, 2560
C_Q, C_F, C_I, C_Z = 3072, 3584, 4096, 4608
G_Q, G_K, G_V, G_Z, G_R = 5120, 5376, 5632, 6144, 6656
M_G = 6672
TWO_PI = 2.0 * math.pi

ENGS = ("pe", "act", "dve", "pool", "sp")


class Res:
    __slots__ = ("name", "w", "r")

    def __init__(self, name):
        self.name = name
        self.w = None
        self.r = {}


class Prog:
    def __init__(self, nc, stack):
        self.nc = nc
        self.stack = stack
        self.streams = {e: [] for e in ENGS}
        self.sems = {}
        self.count = {}
        self.known = {e: {} for e in ENGS}
        for e in ("pe", "act", "dve", "pool"):
            self._newsem("eng_" + e)
        self.final_waits = {}
        self.dry = False
        self.marks = []

    def _newsem(self, key):
        s = self.stack.enter_context(self.nc.semaphore(key))
        self.sems[key] = s
        self.count[key] = 0
        return s

    def _deps(self, eng, reads, writes, own=None):
        need = {}

        def add(tok):
            if tok is None:
                return
            k, v = tok
            if need.get(k, 0) < v:
                need[k] = v
        for r in reads:
            add(r.w)
        for w in writes:
            if not (own is not None and w.w is not None and w.w[0] == own):
                add(w.w)
            for k, v in w.r.items():
                add((k, v))
        waits = []
        kn = self.known[eng]
        for k, v in need.items():
            if k == "eng_pe" and eng == "pe":
                continue
            if kn.get(k, 0) >= v:
                continue
            kn[k] = v
            waits.append((k, v))
        return waits

    def _commit(self, tok, reads, writes):
        k, v = tok
        for r in reads:
            if r.r.get(k, 0) < v:
                r.r[k] = v
        for w in writes:
            w.w = tok
            w.r = {}

    def op(self, eng, fn, reads=(), writes=()):
        if self.dry:
            return
        waits = self._deps(eng, reads, writes)
        k = "eng_" + eng
        self.count[k] += 1
        tok = (k, self.count[k])
        self.streams[eng].append((waits, fn, (k, 1)))
        self._commit(tok, reads, writes)

    def mark(self, name):
        if not self.dry:
            self.marks.append((name, {e: len(st_) for e, st_ in self.streams.items()}))

    def I(self, eng, name, *args, reads=(), writes=(), **kw):
        self.op(eng, (name, args, kw), reads, writes)

    def dma(self, q, out, in_, reads=(), writes=(), sem="g", final=False, **kw):
        if self.dry:
            return
        k = "dma_" + sem
        waits = self._deps(q, reads, writes, own=k)
        if k not in self.sems:
            self._newsem(k)
        self.count[k] += 16
        tok = (k, self.count[k])
        self.streams[q].append(
            (waits, lambda e, o=out, i=in_, kw=kw: e.dma_start(out=o, in_=i, **kw), (k, 16)))
        self._commit(tok, reads, writes)
        if final:
            self.final_waits[k] = self.count[k]

    def emit(self):
        nc = self.nc
        fw = list(self.final_waits.items())
        with nc.Block() as block:
            def replay(e, name, extra=()):
                for waits, fn, inc in self.streams[name]:
                    for k, v in waits:
                        e.wait_ge(self.sems[k], v)
                    if isinstance(fn, tuple):
                        ins = getattr(e, fn[0])(*fn[1], **fn[2])
                    else:
                        ins = fn(e)
                    ins.then_inc(self.sems[inc[0]], inc[1])
                for k, v in extra:
                    e.wait_ge(self.sems[k], v)

            @block.tensor
            def _(e):
                replay(e, "pe")

            @block.scalar
            def _(e):
                replay(e, "act")

            @block.vector
            def _(e):
                replay(e, "dve")

            @block.gpsimd
            def _(e):
                replay(e, "pool")

            @block.sync
            def _(e):
                replay(e, "sp", fw)


class WS:
    AHEAD = 2

    def __init__(self, P, slots, slot_elems):
        self.P = P
        self.slots = slots
        self.res = [Res("wslot%d" % i) for i in range(len(slots))]
        self.slot_elems = slot_elems
        self.plan = []
        self.reset()

    def reset(self):
        self.free = list(range(len(self.slots)))
        self.issued = 0
        self.taken = 0
        self.slot_of = {}

    def _views(self, slot, parts):
        t = self.slots[slot]
        off = 0
        views = []
        for ap in parts:
            shp = list(ap.shape)
            n = 1
            for s_ in shp[1:]:
                n *= s_
            v = t[0:shp[0], off:off + n]
            if len(shp) == 3:
                v = v.rearrange("p (a n) -> p a n", a=shp[1])
            views.append(v)
            off += n
        assert off <= self.slot_elems, off
        return views

    def _pump(self):
        while (self.issued < len(self.plan) and self.free
               and self.issued < self.taken + self.AHEAD):
            i = self.issued
            slot = self.free.pop(0)
            self.slot_of[i] = slot
            parts = self.plan[i]
            for ap, v in zip(parts, self._views(slot, parts)):
                self.P.dma("pool", v, ap, writes=[self.res[slot]], sem="w%d" % slot)
            self.issued += 1

    def take(self, parts):
        if self.P.dry:
            self.plan.append(parts)
            return self._views(0, parts), self.res[0], -1
        i = self.taken
        self.taken += 1
        self._pump()
        if i not in self.slot_of:
            raise RuntimeError("weight stream: no free slot for block %d" % i)
        slot = self.slot_of[i]
        return self._views(slot, self.plan[i]), self.res[slot], i

    def release(self, i):
        if self.P.dry:
            return
        self.free.append(self.slot_of[i])
        self._pump()


DEBUG = bool(int(os.environ.get("MK_DEBUG", "0")))
LAST_RESULTS = None
NWK = 22
SLOT_E = 4224
NSLOT = 4
GELU_C = 1.5957691216057308
SIN_SCALE = TWO_PI * (1.0 - 2e-6)
TW = 256


def build_program(stage=99):
    nc = bass.Bass("TRN2", target_bir_lowering=False)

    def din(name, shape, dt=F32):
        return nc.dram_tensor(name, list(shape), dt, kind="ExternalInput").ap()

    def dout(name, shape):
        return nc.dram_tensor(name, list(shape), F32, kind="ExternalOutput").ap()

    xp = din("xp", [NP_, D])
    xs = din("xs", [NS_, D])
    st_conv = din("st_conv", [DEPTH, NSEQ, 2, WBR])
    st_re = din("st_re", [DEPTH, NSEQ, 32, 64])
    st_im = din("st_im", [DEPTH, NSEQ, 32, 64])
    st_hg = din("st_hg", [DEPTH, NSEQ, 4, 128, 128])
    st_gla = din("st_gla", [DEPTH, NSEQ, 4, 64, 128])
    norm_w = din("norm_w", [DEPTH, D])
    w_in = din("w_in", [DEPTH, D, DIN])
    conv_w = din("conv_w", [DEPTH, 3, WBR])
    s5_a_re = din("s5_a_re", [DEPTH, 32, 64])
    s5_a_im = din("s5_a_im", [DEPTH, 32, 64])
    s5_log_dt = din("s5_log_dt", [DEPTH, 32])
    s5_b_re = din("s5_b_re", [DEPTH, 32, 64, 16])
    s5_b_im = din("s5_b_im", [DEPTH, 32, 64, 16])
    s5_c_re = din("s5_c_re", [DEPTH, 32, 16, 64])
    s5_c_im = din("s5_c_im", [DEPTH, 32, 16, 64])
    s5_d = din("s5_d", [DEPTH, WBR])
    w_glu = din("w_glu", [DEPTH, WBR, WBR])
    b_glu = din("b_glu", [DEPTH, WBR])
    hgrn_lb_raw = din("hgrn_lb_raw", [DEPTH, WBR])
    hgrn_norm = din("hgrn_norm", [DEPTH, 128])
    w_gk = din("w_gk", [DEPTH, 16, 256])
    b_gk = din("b_gk", [DEPTH, 256])
    gla_norm = din("gla_norm", [DEPTH, 128])
    w_branch = din("w_branch", [DEPTH, 4, WBR, D])
    w_out = din("w_out", [DEPTH, D, D])
    final_norm = din("final_norm", [D])
    c_ident = din("c_ident", [128, 128])
    c_ones = din("c_ones", [128, 128])
    c_mask64 = din("c_mask64", [64, 64])
    c_cm = din("c_cm", [128, 640])
    c_tp1 = din("c_tp1", [128, 256])

    o_yp = dout("o_yp", [NP_, D])
    o_ys = dout("o_ys", [NS_, D])
    o_conv_p = dout("o_conv_p", [DEPTH, 2, WBR])
    o_conv_s = dout("o_conv_s", [DEPTH, NSEQ, 2, WBR])
    o_re_p = dout("o_re_p", [DEPTH, 2048])
    o_re_s = dout("o_re_s", [DEPTH, NSEQ, 2048])
    o_im_p = dout("o_im_p", [DEPTH, 2048])
    o_im_s = dout("o_im_s", [DEPTH, NSEQ, 2048])
    o_hg_p = dout("o_hg_p", [DEPTH, 4, 128, 128])
    o_hg_s = dout("o_hg_s", [DEPTH, NSEQ, 4, 128, 128])
    o_gla_p = dout("o_gla_p", [DEPTH, 4, 64, 128])
    o_gla_s = dout("o_gla_s", [DEPTH, NSEQ, 4, 64, 128])

    st = ExitStack()
    with st:
        P = Prog(nc, st)
        I = P.I

        def sb(name, shape, dt=F32):
            return st.enter_context(nc.sbuf_tensor(name, list(shape), dt))

        xT = sb("xT", [128, 8, NT]); r_xT = [Res("xT%d" % i) for i in range(len(TC))]
        hT = sb("hT", [128, 8, NT], BF16); r_hT = [Res("hT%d" % i) for i in range(len(TC))]
        yT = sb("yT", [128, 4, NT], BF16); r_yT = [Res("yT%d" % i) for i in range(len(TC))]
        wslots = [sb("wslot%d" % i, [128, SLOT_E], BF16) for i in range(NSLOT)]
        ws = WS(P, wslots, SLOT_E)
        banks = [st.enter_context(nc.psum_tensor("bank%d" % i, [128, 512], F32)) for i in range(8)]
        r_breg = [[Res("bank%d_%d" % (i, q)) for q in range(8)] for i in range(8)]

        def rb(i, c0=0, n=512):
            return r_breg[i][c0 // 64:(c0 + n + 63) // 64]

        rr = {"i": 0}

        def bank():
            i = rr["i"] % 8
            rr["i"] += 1
            return i

        AR = sb("AR", [128, NWK * 512])
        r_ar = [Res("ar%d" % i) for i in range(NWK)]

        def T(i, n=512, off=0):
            return AR[:, i * 512 + off:i * 512 + off + n]

        def TB(i, n=1024, off=0, nt=1):
            return AR[:, i * 512:(i + nt) * 512].bitcast(BF16)[:, off:off + n]

        def RT(i, nt=1):
            return r_ar[i:i + nt]

        ident_f = sb("ident_f", [128, 128]); r_identf = Res("identf")
        ident_b = sb("ident_b", [128, 128], BF16); r_identb = Res("identb")
        ones_b = sb("ones_b", [128, 128], BF16); r_ones = Res("ones")
        mask64 = sb("mask64", [64, 64]); r_mask = Res("mask64")
        cm = sb("cm", [128, 640]); r_cm = Res("cm")
        tp1 = sb("tp1", [128, 256]); r_tp1 = Res("tp1")
        normw = sb("normw", [128, DEPTH, 8]); r_par = Res("params")
        fnorm = sb("fnorm", [128, 8])
        cw = sb("cw", [128, DEPTH, 3, 4])
        s5r = sb("s5r", [128, 64]); s5thn = sb("s5thn", [128, 64]); s5zr = sb("s5zr", [128, 64]); s5zi = sb("s5zi", [128, 64])
        s5dd = sb("s5dd", [128, DEPTH, 4])
        bglu = sb("bglu", [128, DEPTH, 4])
        lbp = sb("lbp", [128, DEPTH, 4]); oml = sb("oml", [128, DEPTH, 4])
        gn_h = sb("gn_h", [128, DEPTH]); gn_g = sb("gn_g", [128, DEPTH])
        nbgk = sb("nbgk", [64, DEPTH, 4])
        wgk = sb("wgk", [16, DEPTH, 256], BF16); r_wgk = Res("wgk")
        Hst = sb("Hst", [128, 16, 2]); r_H = Res("H")
        epsb = sb("epsb", [128, 1])

        NC_DMA = dict(allow_slow_non_contiguous=True)
        DBG = {}

        def dbg(name, ap, res):
            if P.dry or name in DBG or not DEBUG:
                return
            shp = list(ap.shape)
            DBG[name] = shp
            o = nc.dram_tensor("dbg_" + name, shp, F32, kind="ExternalOutput").ap()
            P.dma("sp", o, ap, reads=list(res), sem="dbg", final=True)

        def interleave(gens):
            gens = list(gens)
            while gens:
                for g in list(gens):
                    try:
                        next(g)
                    except StopIteration:
                        gens.remove(g)

        def setup():
            P.dma("sp", ident_f[:], c_ident, writes=[r_identf], sem="c8")
            P.dma("pool", ident_b[:], c_ident, writes=[r_identb], sem="c9")
            P.dma("pool", ones_b[:], c_ones, writes=[r_ones], sem="c10")
            P.dma("sp", mask64[:], c_mask64, writes=[r_mask], sem="c5")
            P.dma("sp", cm[:], c_cm, writes=[r_cm], sem="c6")
            P.dma("sp", tp1[:], c_tp1, writes=[r_tp1], sem="c7")
            I("pool", "memset", epsb[:], EPS, writes=[r_par])
            pd = lambda o, i_: P.dma("sp", o, i_, writes=[r_par], sem="c0", **NC_DMA)
            pd(normw[:], norm_w.rearrange("l (k p) -> p l k", p=128))
            pd(fnorm[:], final_norm.rearrange("(k p) -> p k", p=128))
            pd(s5dd[:], s5_d.rearrange("l (k p) -> p l k", p=128))
            pd(bglu[:], b_glu.rearrange("l (k p) -> p l k", p=128))
            pd(gn_h[:], hgrn_norm.rearrange("l p -> p l"))
            pd(gn_g[:], gla_norm.rearrange("l p -> p l"))
            pd(lbp[:], hgrn_lb_raw.rearrange("l (k p) -> p l k", p=128))
            pd(nbgk[:], b_gk.rearrange("l (k p) -> p l k", p=64))
            for l in range(DEPTH):
                pd(cw[:, l], conv_w[l].rearrange("j (t p) -> p j t", p=128))
                P.dma("pool", wgk[:, l, :], w_gk[l], writes=[r_wgk], sem="c1")
            ex = T(0, 16).rearrange("p (l k) -> p l k", k=4)
            I("act", "activation", ex, lbp[:], AF.Exp, reads=[r_par], writes=RT(0))
            sm = T(1, 4)
            I("dve", "tensor_tensor", sm, ex[:, 0, :], ex[:, 1, :], ALU.add, reads=RT(0), writes=RT(1))
            I("dve", "tensor_tensor", sm, sm, ex[:, 2, :], ALU.add, reads=RT(0) + RT(1), writes=RT(1))
            I("dve", "tensor_tensor", sm, sm, ex[:, 3, :], ALU.add, reads=RT(0) + RT(1), writes=RT(1))
            I("dve", "reciprocal", sm, sm, reads=RT(1), writes=RT(1))
            for l in range(1, DEPTH):
                I("dve", "tensor_tensor", ex[:, l, :], ex[:, l, :], sm, ALU.mult, reads=RT(0) + RT(1), writes=RT(0))
            I("dve", "memset", lbp[:, 0, :], 0.0, reads=RT(0), writes=[r_par])
            I("dve", "tensor_copy", lbp[:, 1, :], ex[:, 1, :], reads=RT(0), writes=[r_par])
            I("dve", "tensor_tensor", lbp[:, 2, :], lbp[:, 1, :], ex[:, 2, :], ALU.add, reads=RT(0) + [r_par], writes=[r_par])
            I("dve", "tensor_tensor", lbp[:, 3, :], lbp[:, 2, :], ex[:, 3, :], ALU.add, reads=RT(0) + [r_par], writes=[r_par])
            I("dve", "tensor_scalar", oml[:], lbp[:], -1.0, 1.0, ALU.mult, ALU.add, reads=[r_par], writes=[r_par])
            I("dve", "tensor_scalar", nbgk[:], nbgk[:], -1.0, None, ALU.mult, reads=[r_par], writes=[r_par])
            ar_, ai_, ldt = T(2, 64), T(3, 64), T(4, 64)
            for l in range(DEPTH):
                for g in range(2):
                    ps_ = slice(g * 64, (g + 1) * 64)
                    cs_ = slice(l * 16, (l + 1) * 16)
                    P.dma("sp", AR[ps_, 2 * 512 + l * 16:2 * 512 + (l + 1) * 16],
                          s5_a_re[l].rearrange("(j g) n -> g n j", g=2)[g], writes=RT(2), sem="c2", **NC_DMA)
                    P.dma("sp", AR[ps_, 3 * 512 + l * 16:3 * 512 + (l + 1) * 16],
                          s5_a_im[l].rearrange("(j g) n -> g n j", g=2)[g], writes=RT(3), sem="c3", **NC_DMA)
                    P.dma("sp", AR[ps_, 4 * 512 + l * 16:4 * 512 + (l + 1) * 16],
                          s5_log_dt[l].rearrange("(j g) -> g j", g=2)[g:g + 1, :].broadcast_to([64, 16]),
                          writes=RT(4), sem="c4", **NC_DMA)
            dt_, th, us, kf, fs, cs_t, sn_t = T(5, 64), T(6, 64), T(7, 64), T(8, 64), T(9, 64), T(10, 64), T(11, 64)
            ki = T(12, 64).bitcast(I32)
            I("act", "activation", dt_, ldt, AF.Exp, reads=RT(4), writes=RT(5))
            I("dve", "tensor_tensor", th, dt_, ai_, ALU.mult, reads=RT(5) + RT(3), writes=RT(6))
            I("dve", "tensor_scalar", s5thn[:], th, 1.0 / TWO_PI, None, ALU.mult, reads=RT(6), writes=[r_par])
            for (shift, dst, dt_i) in ((64.0, sn_t, 11), (64.25, cs_t, 10)):
                I("dve", "tensor_scalar", us, s5thn[:], shift, None, ALU.add, reads=[r_par], writes=RT(7))
                I("dve", "tensor_copy", ki, us, reads=RT(7), writes=RT(12))
                I("dve", "tensor_copy", kf, ki, reads=RT(12), writes=RT(8))
                I("dve", "tensor_tensor", fs, us, kf, ALU.subtract, reads=RT(7) + RT(8), writes=RT(9))
                I("act", "activation", dst, fs, AF.Sin, scale=SIN_SCALE, reads=RT(9), writes=RT(dt_i))
            mg = s5r[:]
            I("dve", "tensor_tensor", us, dt_, ar_, ALU.mult, reads=RT(5) + RT(2), writes=RT(7))
            I("act", "activation", mg, us, AF.Exp, reads=RT(7), writes=[r_par])
            abr, abi = T(13, 64), T(14, 64)
            I("dve", "tensor_tensor", abr, mg, cs_t, ALU.mult, reads=[r_par] + RT(10), writes=RT(13))
            I("dve", "tensor_tensor", abi, mg, sn_t, ALU.mult, reads=[r_par] + RT(11), writes=RT(14))
            den, t1_, t2_ = T(15, 64), T(16, 64), T(17, 64)
            I("dve", "tensor_tensor", den, ar_, ar_, ALU.mult, reads=RT(2), writes=RT(15))
            I("dve", "tensor_tensor", t1_, ai_, ai_, ALU.mult, reads=RT(3), writes=RT(16))
            I("dve", "tensor_tensor", den, den, t1_, ALU.add, reads=RT(15) + RT(16), writes=RT(15))
            I("dve", "reciprocal", den, den, reads=RT(15), writes=RT(15))
            I("dve", "tensor_scalar", abr, abr, -1.0, None, ALU.add, reads=RT(13), writes=RT(13))
            I("dve", "tensor_tensor", t1_, abr, ar_, ALU.mult, reads=RT(13) + RT(2), writes=RT(16))
            I("dve", "tensor_tensor", t2_, abi, ai_, ALU.mult, reads=RT(14) + RT(3), writes=RT(17))
            I("dve", "tensor_tensor", t1_, t1_, t2_, ALU.add, reads=RT(16) + RT(17), writes=RT(16))
            I("dve", "tensor_tensor", s5zr[:], t1_, den, ALU.mult, reads=RT(16) + RT(15), writes=[r_par])
            I("dve", "tensor_tensor", t1_, abi, ar_, ALU.mult, reads=RT(14) + RT(2), writes=RT(16))
            I("dve", "tensor_tensor", t2_, abr, ai_, ALU.mult, reads=RT(13) + RT(3), writes=RT(17))
            I("dve", "tensor_tensor", t1_, t1_, t2_, ALU.subtract, reads=RT(16) + RT(17), writes=RT(16))
            I("dve", "tensor_tensor", s5zi[:], t1_, den, ALU.mult, reads=RT(16) + RT(15), writes=[r_par])

            for tt in range(17):
                src = xp[tt * 128:(tt + 1) * 128, :] if tt < 16 else xs
                a0 = 18 + (tt % 2) * 2
                xb_ = AR[:, a0 * 512:(a0 + 2) * 512]
                rxb = RT(a0, 2)
                P.dma("sp", xb_, src, writes=rxb, sem="xin%d" % (tt % 2))
                ci = tt // 4
                for h in range(2):
                    bi = bank()
                    for q in range(4):
                        ft = h * 4 + q
                        I("pe", "transpose", banks[bi][:, q * 128:(q + 1) * 128], xb_[:, ft * 128:(ft + 1) * 128],
                          ident_f[:], reads=rxb + [r_identf], writes=rb(bi))
                    dst = xT[:, h * 4:h * 4 + 4, tt * 128:(tt + 1) * 128]
                    srcv = banks[bi][:, :].rearrange("p (a n) -> p a n", a=4)
                    if h == 0:
                        I("act", "activation", dst, srcv, AF.Copy, reads=rb(bi), writes=[r_xT[ci]])
                    else:
                        I("dve", "tensor_copy", dst, srcv, reads=rb(bi), writes=[r_xT[ci]])

        def rstd_tile(src_list, src_res, n, scale, sq_tiles, out_tile, ones_k=128):
            bi = bank()
            nk = len(src_list)
            for kt, src in enumerate(src_list):
                w_ = sq_tiles[kt % len(sq_tiles)]
                I("act", "activation", TB(w_, n), src, AF.Square, reads=src_res, writes=RT(w_))
                I("pe", "matmul", banks[bi][:, 0:n], ones_b[0:ones_k, :], TB(w_, n)[0:ones_k, :], start=(kt == 0), stop=(kt == nk - 1),
                  reads=RT(w_) + [r_ones], writes=rb(bi, 0, n))
            o = T(out_tile, n)
            I("act", "activation", o, banks[bi][:, 0:n], AF.Ln, scale=scale, bias=epsb[:, 0:1], reads=rb(bi, 0, n) + [r_par], writes=RT(out_tile))
            I("act", "activation", o, o, AF.Exp, scale=-0.5, reads=RT(out_tile), writes=RT(out_tile))

        def rmsnorm_h(l):
            for ci, (c0, n) in enumerate(TC):
                rstd_tile([xT[:, kt, c0:c0 + n] for kt in range(8)], [r_xT[ci]], n, 1.0 / D, (0, 1), 2)
                for kt in range(8):
                    I("dve", "scalar_tensor_tensor", hT[:, kt, c0:c0 + n], xT[:, kt, c0:c0 + n], normw[:, l, kt:kt + 1],
                      T(2, n), ALU.mult, ALU.mult, reads=[r_xT[ci], r_par] + RT(2), writes=[r_hT[ci]])

        def proj(wv, wres, col0, m, ci, bi, boff=0):
            c0, n = TC[ci]
            for kt in range(8):
                I("pe", "matmul", banks[bi][0:m, boff:boff + n], wv[:, kt, col0:col0 + m], hT[:, kt, c0:c0 + n],
                  start=(kt == 0), stop=(kt == 7), reads=[wres, r_hT[ci]], writes=rb(bi, boff, n))

        def branch_a(l):
            wl = w_in[l].rearrange("(k p) c -> p k c", p=128)
            EXT0 = 0
            ext = AR[:, 0:2 + NP_]
            r_ext = RT(0, 5)
            ext_s = T(5, 160).rearrange("p (s t) -> p s t", t=10)
            r_exts = RT(5)
            I("pool", "memset", ext[:, 0:2], 0.0, writes=r_ext)
            for jt in range(4):
                parts = [wl[:, :, base + jt * 128: base + (jt + 1) * 128] for base in (A_X, A_B, A_C, A_Z)]
                (vx, vb, vc, vz), wres, wi = ws.take(parts)
                for j in range(2):
                    P.dma("sp", ext_s[:, :, j], st_conv[l, :, j, jt * 128:(jt + 1) * 128].rearrange("s c -> c s"),
                          writes=r_exts, sem="cst", **NC_DMA)
                for ci, (c0, n) in enumerate(TC):
                    smp = ci == 4
                    bx, bc, bz, bb = bank(), bank(), bank(), bank()
                    proj(vx, wres, 0, 128, ci, bx)
                    proj(vc, wres, 0, 128, ci, bc)
                    proj(vz, wres, 0, 128, ci, bz)
                    proj(vb, wres, 0, 128, ci, bb)
                    I("act", "activation", T(6, n), banks[bx][:, 0:n], AF.Copy, reads=rb(bx, 0, n), writes=RT(6))
                    v3 = lambda ap: ap.rearrange("p (s t) -> p s t", t=8)
                    if not smp:
                        I("dve", "tensor_tensor", ext[:, 2 + c0:2 + c0 + n], banks[bc][:, 0:n], T(6, n), ALU.mult,
                          reads=rb(bc, 0, n) + RT(6), writes=r_ext)
                    else:
                        I("dve", "tensor_tensor", ext_s[:, :, 2:10], v3(banks[bc][:, 0:128]), v3(T(6, 128)), ALU.mult,
                          reads=rb(bc, 0, n) + RT(6), writes=r_exts)
                    I("act", "activation", T(7, n), banks[bz][:, 0:n], AF.Silu, reads=rb(bz, 0, n), writes=RT(7))
                    I("dve", "tensor_tensor", T(8, n), banks[bb][:, 0:n], T(7, n), ALU.mult, reads=rb(bb, 0, n) + RT(7), writes=RT(8))
                    if not smp:
                        e0 = lambda k: ext[:, c0 + k:c0 + k + n]
                        acc = T(9, n)
                        rsrc = r_ext
                    else:
                        e0 = lambda k: ext_s[:, :, k:k + 8]
                        acc = v3(T(9, 128))
                        rsrc = r_exts
                    I("act", "activation", acc, e0(0), AF.Copy, scale=cw[:, l, 0, jt:jt + 1], reads=rsrc + [r_par], writes=RT(9))
                    for k in (1, 2):
                        I("dve", "scalar_tensor_tensor", acc, e0(k), cw[:, l, k, jt:jt + 1], acc, ALU.mult, ALU.add,
                          reads=rsrc + [r_par] + RT(9), writes=RT(9))
                    I("dve", "tensor_tensor", yT[:, jt, c0:c0 + n], T(9, n), T(8, n), ALU.mult, reads=RT(9) + RT(8), writes=[r_yT[ci]])
                    if ci == 3:
                        P.dma("sp", o_conv_p[l].rearrange("j c -> c j")[jt * 128:(jt + 1) * 128],
                              ext[:, NP_:NP_ + 2], reads=r_ext, sem="o_ext", final=True, **NC_DMA)
                    if ci == 4:
                        for j in range(2):
                            P.dma("sp", o_conv_s[l, :, j, jt * 128:(jt + 1) * 128].rearrange("s c -> c s"),
                                  ext_s[:, :, 8 + j], reads=r_exts, sem="o_exts", final=True, **NC_DMA)
                ws.release(wi)

        def branch_s5(l):
            wl = w_in[l].rearrange("(k p) c -> p k c", p=128)
            (vu,), wres, wi = ws.take([wl[:, :, S_U:S_U + 512]])
            for ci, (c0, n) in enumerate(TC):
                for ut in range(4):
                    bi = bank()
                    proj(vu, wres, ut * 128, 128, ci, bi)
                    if ut % 2 == 0:
                        I("act", "activation", yT[:, ut, c0:c0 + n], banks[bi][:, 0:n], AF.Copy, reads=rb(bi, 0, n), writes=[r_yT[ci]])
                    else:
                        I("dve", "tensor_copy", yT[:, ut, c0:c0 + n], banks[bi][:, 0:n], reads=rb(bi, 0, n), writes=[r_yT[ci]])
            ws.release(wi)
            stgB = TB(0).rearrange("p (r j n) -> p r j n", r=2, j=4)
            stgC = TB(1).rearrange("p (r j n) -> p r j n", r=2, j=4)
            Bt = TB(18).rearrange("p (r j n) -> p r j n", r=2, j=4)
            Ct = TB(19).rearrange("p (r j n) -> p r j n", r=2, j=4)
            h0re = T(20, 256).rearrange("p (j s) -> p j s", s=16)
            h0im = T(20, 256, 256).rearrange("p (j s) -> p j s", s=16)
            Hsre = T(21, 256).rearrange("p (j s) -> p j s", s=16)
            Hsim = T(21, 256, 256).rearrange("p (j s) -> p j s", s=16)
            for s_ in range(NSEQ):
                P.dma("sp", h0re[:, :, s_], st_re[l, s_].rearrange("(j g) n -> (g n) j", g=2), writes=RT(20), sem="s5st", **NC_DMA)
                P.dma("sp", h0im[:, :, s_], st_im[l, s_].rearrange("(j g) n -> (g n) j", g=2), writes=RT(20), sem="s5st", **NC_DMA)
            YB = [0, 1, 2, 3, 4]
            h2 = lambda ap: ap.rearrange("p (h x) -> p h x", h=2)

            def gen_tables(cb, lj):
                thn_c, r_c, zr_c, zi_c = s5thn[:, lj:lj + 1], s5r[:, lj:lj + 1], s5zr[:, lj:lj + 1], s5zi[:, lj:lj + 1]
                us, kf = T(cb + 0, TW), T(cb + 0, TW, TW)
                ki, fs = T(cb + 1, TW).bitcast(I32), T(cb + 1, TW, TW)
                TR, EC = T(cb + 5, TW), T(cb + 5, TW, TW)
                TI2, ES2 = T(cb + 6), T(cb + 7)
                rt, rt_s = T(cb + 8, TW), T(cb + 8, 128, TW)
                Es = ES2[:, TW:2 * TW]
                for (shift, dst, dtile) in ((64.0, Es, cb + 7), (64.25, EC, cb + 5)):
                    I("dve", "tensor_scalar", us, tp1[:, 0:TW], thn_c, shift, ALU.mult, ALU.add, reads=[r_tp1, r_par], writes=RT(cb + 0))
                    I("dve", "tensor_copy", ki, us, reads=RT(cb + 0), writes=RT(cb + 1))
                    I("dve", "tensor_copy", kf, ki, reads=RT(cb + 1), writes=RT(cb + 0))
                    I("dve", "tensor_tensor", fs, us, kf, ALU.subtract, reads=RT(cb + 0), writes=RT(cb + 1))
                    I("act", "activation", dst, fs, AF.Sin, scale=SIN_SCALE, reads=RT(cb + 1), writes=RT(dtile))
                tA, tB_ = T(cb + 3, TW), T(cb + 3, TW, TW)
                I("act", "activation", tA, Es, AF.Copy, scale=zi_c, reads=RT(cb + 7) + [r_par], writes=RT(cb + 3))
                I("dve", "scalar_tensor_tensor", TR, EC, zr_c, tA, ALU.mult, ALU.add, reads=RT(cb + 5) + RT(cb + 3) + [r_par], writes=RT(cb + 5))
                I("act", "activation", tB_, Es, AF.Copy, scale=zr_c, reads=RT(cb + 7) + [r_par], writes=RT(cb + 3))
                I("dve", "scalar_tensor_tensor", TI2[:, TW:2 * TW], EC, zi_c, tB_, ALU.mult, ALU.subtract,
                  reads=RT(cb + 5) + RT(cb + 3) + [r_par], writes=RT(cb + 6))
                I("act", "activation", TI2[:, 0:TW], TI2[:, TW:2 * TW], AF.Copy, scale=-1.0, reads=RT(cb + 6), writes=RT(cb + 6))
                I("act", "activation", ES2[:, 0:TW], Es, AF.Copy, scale=-1.0, reads=RT(cb + 7), writes=RT(cb + 7))
                I("dve", "tensor_scalar", rt, tp1[:, 0:TW], 0.0, r_c, ALU.mult, ALU.add, reads=[r_tp1, r_par], writes=RT(cb + 8))
                I("dve", "tensor_scalar", rt_s, cm[:, 512:640], r_c, None, ALU.mult, reads=[r_cm, r_par], writes=RT(cb + 8))

            def front(cb, bk, ut, jp, c0, n):
                ci = c0 // 512
                u_rhs = yT[:, ut, c0:c0 + n]
                I("pe", "matmul", banks[bk][:, 0:n], Bt[:, 0, jp, :], u_rhs, start=True, stop=True,
                  reads=RT(18) + [r_yT[ci]], writes=rb(bk, 0, n))
                I("pe", "matmul", banks[bk][:, 256:256 + n], Bt[:, 1, jp, :], u_rhs, start=True, stop=True,
                  reads=RT(18) + [r_yT[ci]], writes=rb(bk, 256, n))
                I("act", "activation", h2(T(cb + 0))[:, :, 0:n], h2(banks[bk][:, :])[:, :, 0:n], AF.Copy,
                  reads=rb(bk, 0, n) + rb(bk, 256, n), writes=RT(cb + 0))

            def window(cb, bk, ut, jp, j, lj, wi_, c0, n, smp, nwin, nxt=None):
                r_c = s5r[:, lj:lj + 1]
                ci = c0 // 512
                boff = c0 % 512
                TR, EC = T(cb + 5, TW), T(cb + 5, TW, TW)
                TI2, ES2 = h2(T(cb + 6)), h2(T(cb + 7))
                rt, rt_s = T(cb + 8, TW), T(cb + 8, 128, TW)
                Xp = h2(banks[bk][:, :])[:, :, 0:n]
                Xs = h2(T(cb + 0))[:, :, 0:n]
                A = h2(T(cb + 1))[:, :, 0:n]
                Bm = h2(T(cb + 2))[:, :, 0:n]
                G = h2(T(cb + 3))[:, :, 0:n]
                A2 = A
                Hb = h2(TB(cb + 4, 512))[:, :, 0:n]
                f3 = lambda ap: ap.rearrange("p h (s t) -> p h s t", t=8)
                if not smp:
                    bc2 = lambda tab: tab[:, 0:n].unsqueeze(1).broadcast_to([128, 2, n])
                    tv = lambda ap: ap
                    sv = lambda ap: ap
                    rtab = rt[:, 0:n]
                else:
                    bc2 = lambda tab: tab[:, 0:8].unsqueeze(1).unsqueeze(1).broadcast_to([128, 2, NSEQ, 8])
                    tv = lambda ap: ap[:, 0:8].unsqueeze(1).broadcast_to([128, NSEQ, 8])
                    sv = lambda ap: ap.rearrange("p (s t) -> p s t", t=8)
                    rtab = rt_s
                if not smp:
                    I("dve", "tensor_tensor", A, Xs, bc2(TR), ALU.mult, reads=RT(cb + 0) + RT(cb + 5), writes=RT(cb + 1))
                else:
                    I("dve", "tensor_tensor", f3(A), f3(Xs), bc2(TR), ALU.mult, reads=RT(cb + 0) + RT(cb + 5), writes=RT(cb + 1))
                yield
                I("dve", "tensor_tensor", sv(Bm[:, 0, :]), sv(Xs[:, 1, :]), tv(TI2[:, 0, :]), ALU.mult, reads=RT(cb + 0) + RT(cb + 6), writes=RT(cb + 2))
                I("dve", "tensor_tensor", sv(Bm[:, 1, :]), sv(Xs[:, 0, :]), tv(TI2[:, 1, :]), ALU.mult, reads=RT(cb + 0) + RT(cb + 6), writes=RT(cb + 2))
                yield
                if nxt is not None:
                    front(cb, bk, ut, jp, nxt[0], nxt[1])
                    yield
                I("dve", "tensor_tensor", A, A, Bm, ALU.add, reads=RT(cb + 1) + RT(cb + 2), writes=RT(cb + 1))
                yield
                if smp:
                    w0r = sv(A[:, 0, :])[:, :, 0]
                    w0i = sv(A[:, 1, :])[:, :, 0]
                    I("dve", "scalar_tensor_tensor", w0r, h0re[:, j, :], r_c, w0r, ALU.mult, ALU.add,
                      reads=RT(20) + RT(cb + 1) + [r_par], writes=RT(cb + 1))
                    I("dve", "scalar_tensor_tensor", w0i, h0im[:, j, :], r_c, w0i, ALU.mult, ALU.add,
                      reads=RT(20) + RT(cb + 1) + [r_par], writes=RT(cb + 1))
                init_re = 0.0 if (wi_ == 0 or smp) else Hst[:, j, 0:1]
                init_im = 0.0 if (wi_ == 0 or smp) else Hst[:, j, 1:2]
                I("dve", "tensor_tensor_scan", G[:, 0, :], rtab, A[:, 0, :], init_re, ALU.mult, ALU.add,
                  reads=RT(cb + 8) + RT(cb + 1) + [r_H], writes=RT(cb + 3))
                I("dve", "tensor_tensor_scan", G[:, 1, :], rtab, A[:, 1, :], init_im, ALU.mult, ALU.add,
                  reads=RT(cb + 8) + RT(cb + 1) + [r_H], writes=RT(cb + 3))
                yield
                if not smp:
                    I("dve", "tensor_tensor", A2, G, bc2(EC), ALU.mult, reads=RT(cb + 3) + RT(cb + 5), writes=RT(cb + 1))
                else:
                    I("dve", "tensor_tensor", f3(A2), f3(G), bc2(EC), ALU.mult, reads=RT(cb + 3) + RT(cb + 5), writes=RT(cb + 1))
                I("dve", "tensor_tensor", sv(Bm[:, 0, :]), sv(G[:, 1, :]), tv(ES2[:, 0, :]), ALU.mult, reads=RT(cb + 3) + RT(cb + 7), writes=RT(cb + 2))
                I("dve", "tensor_tensor", sv(Bm[:, 1, :]), sv(G[:, 0, :]), tv(ES2[:, 1, :]), ALU.mult, reads=RT(cb + 3) + RT(cb + 7), writes=RT(cb + 2))
                yield
                I("dve", "tensor_tensor", Hb, A2, Bm, ALU.add, reads=RT(cb + 1) + RT(cb + 2), writes=RT(cb + 4))
                yield
                if not smp:
                    I("dve", "tensor_tensor", Hst[:, j, :], A2[:, :, n - 1], Bm[:, :, n - 1], ALU.add, reads=RT(cb + 1) + RT(cb + 2), writes=[r_H])
                else:
                    I("dve", "tensor_tensor", Hsre[:, j, :], sv(A2[:, 0, :])[:, :, 7], sv(Bm[:, 0, :])[:, :, 7], ALU.add,
                      reads=RT(cb + 1) + RT(cb + 2), writes=RT(21))
                    I("dve", "tensor_tensor", Hsim[:, j, :], sv(A2[:, 1, :])[:, :, 7], sv(Bm[:, 1, :])[:, :, 7], ALU.add,
                      reads=RT(cb + 1) + RT(cb + 2), writes=RT(21))
                yb = YB[ci]
                I("pe", "matmul", banks[yb][:, boff:boff + n], Ct[:, 0, jp, :], Hb[:, 0, :], start=(jp == 0 and boff == 0), stop=False,
                  skip_group_check=True, reads=RT(19) + RT(cb + 4), writes=rb(yb, boff, n))
                I("pe", "matmul", banks[yb][:, boff:boff + n], Ct[:, 1, jp, :], Hb[:, 1, :], start=False, stop=(jp == 3 and (boff != 0 or ci == 4)),
                  skip_group_check=True, reads=RT(19) + RT(cb + 4), writes=rb(yb, boff, n))

            wins = [(w * TW, TW, False) for w in range(NP_ // TW)] + [(NP_, NS_, True)]
            for ut in range(4):
                I("pool", "memset", TB(0), 0.0, writes=RT(0))
                I("pool", "memset", TB(1), 0.0, writes=RT(1))
                for jp in range(4):
                    j = 4 * ut + jp
                    for g in range(2):
                        grp = 2 * j + g
                        for ri, (bsrc, csrc) in enumerate(((s5_b_re, s5_c_re), (s5_b_im, s5_c_im))):
                            P.dma("pool", stgB[g * 64:(g + 1) * 64, ri, jp, 32 * jp + 16 * g:32 * jp + 16 * g + 16],
                                  bsrc[l, grp], writes=RT(0), sem="s5wB")
                            P.dma("pool", stgC[32 * jp + 16 * g:32 * jp + 16 * g + 16, ri, jp, g * 64:(g + 1) * 64],
                                  csrc[l, grp], writes=RT(1), sem="s5wC")
                for (stg, dstT, rs_, rd_, negim) in ((stgB, Bt, RT(0), RT(18), False), (stgC, Ct, RT(1), RT(19), True)):
                    for ri in range(2):
                        bi = 7
                        pb = banks[bi][:, :].bitcast(BF16)
                        for jp in range(4):
                            I("pe", "transpose", pb[:, jp * 128:(jp + 1) * 128], stg[:, ri, jp, :], ident_b[:],
                              reads=rs_ + [r_identb], writes=rb(bi))
                        dst = dstT[:, ri].rearrange("p j n -> p (j n)")
                        if negim and ri == 1:
                            I("act", "activation", dst, pb[:, 0:512], AF.Copy, scale=-1.0, reads=rb(bi), writes=rd_)
                        else:
                            I("act", "activation", dst, pb[:, 0:512], AF.Copy, reads=rb(bi), writes=rd_)
                for pair in range(2):
                    jps = (2 * pair, 2 * pair + 1)
                    for c, jp in enumerate(jps):
                        gen_tables(9 * c, l * 16 + 4 * ut + jp)
                    for c, jp in enumerate(jps):
                        front(9 * c, 5 + c, ut, jp, wins[0][0], wins[0][1])
                    for wi_, (c0, n, smp) in enumerate(wins):
                        nxt = wins[wi_ + 1][0:2] if wi_ + 1 < len(wins) else None
                        interleave([window(9 * c, 5 + c, ut, jp, 4 * ut + jp, l * 16 + 4 * ut + jp, wi_, c0, n, smp, len(wins), nxt)
                                    for c, jp in enumerate(jps)])
                if ut == 3:
                    P.dma("sp", o_re_p[l].rearrange("(j p) -> p j", p=128), Hst[:, :, 0], reads=[r_H], sem="o_H", final=True, **NC_DMA)
                    P.dma("sp", o_im_p[l].rearrange("(j p) -> p j", p=128), Hst[:, :, 1], reads=[r_H], sem="o_H", final=True, **NC_DMA)
                for ci, (c0, n) in enumerate(TC):
                    ys, t_, t2 = T(9, n), T(10, n), T(11, n)
                    I("dve", "scalar_tensor_tensor", ys, yT[:, ut, c0:c0 + n], s5dd[:, l, ut:ut + 1], banks[YB[ci]][:, 0:n],
                      ALU.mult, ALU.add, reads=[r_yT[ci], r_par] + rb(YB[ci], 0, n), writes=RT(9))
                    I("act", "activation", t_, ys, AF.Square, reads=RT(9), writes=RT(10))
                    I("dve", "tensor_scalar", t_, t_, 0.044715, 1.0, ALU.mult, ALU.add, reads=RT(10), writes=RT(10))
                    I("pool", "tensor_tensor", t_, t_, ys, ALU.mult, reads=RT(10) + RT(9), writes=RT(10))
                    I("act", "activation", t2, t_, AF.Sigmoid, scale=GELU_C, reads=RT(10), writes=RT(11))
                    I("dve", "tensor_tensor", yT[:, ut, c0:c0 + n], ys, t2, ALU.mult, reads=RT(9) + RT(11), writes=[r_yT[ci]])
            for s_ in range(NSEQ):
                P.dma("sp", o_re_s[l, s_].rearrange("(j p) -> p j", p=128), Hsre[:, :, s_], reads=RT(21), sem="o_Hs", final=True, **NC_DMA)
                P.dma("sp", o_im_s[l, s_].rearrange("(j p) -> p j", p=128), Hsim[:, :, s_], reads=RT(21), sem="o_Hs", final=True, **NC_DMA)
            (vg,), gres, gi = ws.take([w_glu[l].rearrange("(k p) c -> p k c", p=128)])
            (vz,), zres, zi_ = ws.take([wl[:, :, S_Z:S_Z + 512]])
            resb = TB(0, 2048, 0, 2).rearrange("p (k n) -> p k n", k=4)
            for ci, (c0, n) in enumerate(TC):
                for ot in range(4):
                    b1, b2 = bank(), bank()
                    for kt in range(4):
                        I("pe", "matmul", banks[b1][:, 0:n], vg[:, kt, ot * 128:(ot + 1) * 128], yT[:, kt, c0:c0 + n],
                          start=(kt == 0), stop=(kt == 3), reads=[gres, r_yT[ci]], writes=rb(b1, 0, n))
                    proj(vz, zres, ot * 128, 128, ci, b2)
                    I("act", "activation", T(2, n), banks[b1][:, 0:n], AF.Sigmoid, bias=bglu[:, l, ot:ot + 1], reads=rb(b1, 0, n) + [r_par], writes=RT(2))
                    I("act", "activation", T(3, n), banks[b2][:, 0:n], AF.Silu, reads=rb(b2, 0, n), writes=RT(3))
                    I("dve", "tensor_tensor", T(4, n), yT[:, ot, c0:c0 + n], T(2, n), ALU.mult, reads=[r_yT[ci]] + RT(2), writes=RT(4))
                    I("dve", "tensor_tensor", resb[:, ot, 0:n], T(4, n), T(3, n), ALU.mult, reads=RT(4) + RT(3), writes=RT(0, 2))
                I("pool", "tensor_copy", yT[:, :, c0:c0 + n], resb[:, :, 0:n], reads=RT(0, 2), writes=[r_yT[ci]])
            ws.release(gi)
            ws.release(zi_)

        def branch_gated(l, kind):
            wl = w_in[l].rearrange("(k p) c -> p k c", p=128)
            hg = kind == "hgrn"
            K = 128 if hg else 64
            if hg:
                (vq,), rq, iq = ws.take([wl[:, :, C_Q:C_Q + 512]])
                (vf,), rf, if_ = ws.take([wl[:, :, C_F:C_F + 512]])
                (vv,), rv, iv = ws.take([wl[:, :, C_I:C_I + 512]])
                (vz,), rz, iz = ws.take([wl[:, :, C_Z:C_Z + 512]])
                handles = [iq, if_, iv, iz]
                gn = gn_h
                st_in, o_p, o_s = st_hg, o_hg_p, o_hg_s
            else:
                (vqk,), rq, iq = ws.take([wl[:, :, G_Q:G_Q + 512]])
                (vv,), rv, iv = ws.take([wl[:, :, G_V:G_V + 512]])
                (vz,), rz, iz = ws.take([wl[:, :, G_Z:G_Z + 528]])
                handles = [iq, iv, iz]
                gn = gn_g
                st_in, o_p, o_s = st_gla, o_gla_p, o_gla_s
            qb = TB(0, 2048, 0, 2).rearrange("p (h n) -> p h n", h=4)
            kb = TB(2, 2048, 0, 2).rearrange("p (h n) -> p h n", h=4)
            oT = AR[:, 4 * 512:8 * 512].rearrange("p (h n) -> p h n", h=4)
            vtok = [TB(8, 512, 0), TB(8, 512, 512)]
            attm = [TB(9, 64, 128 * i) for i in range(4)]
            kbTs = [TB(9, 128, 512 + 128 * i) for i in range(4)]
            Sf = T(10).rearrange("p (h n) -> p h n", h=4)
            Sb = TB(11, 512, 0).rearrange("p (h n) -> p h n", h=4)
            elast = T(11, 64, 256).rearrange("p (h n) -> p h n", h=4)
            Ss = [T(12, 128, 128 * i) for i in range(4)]
            Ssb = [TB(13, 128, 128 * i) for i in range(4)]
            tmpS = [T(18, 128, 128 * i) for i in range(4)]
            rT = TB(19, 512, 0)
            r_vtok = [Res("vtok%d" % i) for i in range(2)]
            r_attm = [Res("attm%d" % i) for i in range(4)]
            r_kbT = [Res("kbT%d" % i) for i in range(4)]
            r_Sf = [Res("Sf%d" % i) for i in range(4)]
            r_Sb = [Res("Sb%d" % i) for i in range(4)]
            r_el = Res("elast")
            r_Ss = [Res("Ss%d" % i) for i in range(4)]
            r_Ssb = [Res("Ssb%d" % i) for i in range(4)]
            r_tmp = [Res("tmpS%d" % i) for i in range(4)]
            fine = {8: r_vtok, 9: r_attm + r_kbT, 10: r_Sf, 11: r_Sb + [r_el], 12: r_Ss, 13: r_Ssb, 18: r_tmp}

            def touch():
                for tile_, toks in fine.items():
                    I("pool", "memset", T(tile_, 2), 0.0, writes=RT(tile_) + toks)
            touch()
            I("pool", "memset", T(10), 0.0, writes=r_Sf)
            I("pool", "memset", TB(11, 512, 0), 0.0, writes=r_Sb)
            BQ, BF_, BV = (0, 1), (0, 1), (2, 2)
            vcount = {"i": 0}
            ucount = {"i": 0}
            for ci, (c0, n) in enumerate(TC):
                smp = ci == 4
                csz = 8 if smp else 64
                nch = n // csz
                cmt = cm[:, 512:640] if smp else cm[:, 0:512]
                if not hg:
                    bi = bank() % 2
                    proj(vz, rz, 512, 16, ci, bi)
                    I("act", "activation", rT[0:16, 0:n], banks[bi][0:16, 0:n], AF.Copy, reads=rb(bi, 0, n), writes=RT(19))
                for hd in range(4):
                    b1, b2 = 0, 1
                    t0_, t1_, t2_, t3_ = T(14, n), T(15, n), T(16, n), T(17, n)
                    if hg:
                        proj(vq, rq, hd * 128, 128, ci, b1)
                        proj(vf, rf, hd * 128, 128, ci, b2)
                        I("act", "activation", t0_, banks[b2][:, 0:n], AF.Sigmoid, reads=rb(b2, 0, n), writes=RT(14))
                        I("dve", "tensor_scalar", t0_, t0_, oml[:, l, hd:hd + 1], lbp[:, l, hd:hd + 1], ALU.mult, ALU.add,
                          reads=RT(14) + [r_par], writes=RT(14))
                        I("act", "activation", t1_, t0_, AF.Ln, reads=RT(14), writes=RT(15))
                        I("dve", "tensor_tensor_scan", t2_, cmt, t1_, 0.0, ALU.mult, ALU.add, reads=[r_cm] + RT(15), writes=RT(16))
                        I("act", "activation", t1_, t2_, AF.Exp, reads=RT(16), writes=RT(15))
                        I("act", "activation", t3_, t2_, AF.Exp, scale=-1.0, reads=RT(16), writes=RT(17))
                        I("dve", "tensor_scalar", t0_, t0_, -1.0, 1.0, ALU.mult, ALU.add, reads=RT(14), writes=RT(14))
                        I("dve", "tensor_tensor", kb[:, hd, 0:n], t0_, t3_, ALU.mult, reads=RT(14) + RT(17), writes=RT(2, 2))
                        I("act", "activation", t2_, banks[b1][:, 0:n], AF.Silu, reads=rb(b1, 0, n), writes=RT(16))
                        I("dve", "scalar_tensor_tensor", qb[:, hd, 0:n], t2_, float(K) ** -0.5, t1_, ALU.mult, ALU.mult,
                          reads=RT(16) + RT(15), writes=RT(0, 2))
                    else:
                        proj(vqk, rq, hd * 64, 64, ci, b1)
                        proj(vqk, rq, 256 + hd * 64, 64, ci, b2)
                        b3 = 6
                        I("pe", "matmul", banks[b3][0:64, 0:n], wgk[0:16, l, hd * 64:(hd + 1) * 64], rT[0:16, 0:n], start=True, stop=True,
                          reads=[r_wgk] + RT(19), writes=rb(b3, 0, n))
                        I("act", "activation", t0_[0:64], banks[b3][0:64, 0:n], AF.Exp, scale=-1.0, bias=nbgk[:, l, hd:hd + 1],
                          reads=rb(b3, 0, n) + [r_par], writes=RT(14))
                        I("act", "activation", t1_[0:64], t0_[0:64], AF.Ln, bias=1.0, reads=RT(14), writes=RT(15))
                        I("dve", "tensor_tensor_scan", t2_[0:64], cmt[0:64], t1_[0:64], 0.0, ALU.mult, ALU.add, reads=[r_cm] + RT(15), writes=RT(16))
                        I("act", "activation", t1_[0:64], t2_[0:64], AF.Exp, scale=-1.0 / 16.0, reads=RT(16), writes=RT(15))
                        I("act", "activation", t3_[0:64], t2_[0:64], AF.Exp, scale=1.0 / 16.0, reads=RT(16), writes=RT(17))
                        I("dve", "tensor_tensor", kb[0:64, hd, 0:n], banks[b2][0:64, 0:n], t3_[0:64], ALU.mult, reads=rb(b2, 0, n) + RT(17), writes=RT(2, 2))
                        I("dve", "scalar_tensor_tensor", qb[0:64, hd, 0:n], banks[b1][0:64, 0:n], float(K) ** -0.5, t1_[0:64], ALU.mult, ALU.mult,
                          reads=rb(b1, 0, n) + RT(15), writes=RT(0, 2))
                    ev = t1_[0:K].rearrange("p (c t) -> p c t", t=csz)[:, :, csz - 1]
                    I("pool", "tensor_copy", elast[0:K, hd, 0:nch], ev, reads=RT(15), writes=[r_el])
                def unit(ch, hd, vi, u_):
                    tl = ch * csz
                    pa = hd
                    if smp:
                        su = ch * 4 + hd
                        p4 = su % 4
                        S_f, S_b = Ss[p4][0:K, :], Ssb[p4][0:K, :]
                        for ahead in ((0, 1, 2) if su == 0 else (2,)):
                            sn = su + ahead
                            if sn < 4 * nch:
                                P.dma("sp", Ss[sn % 4][0:K, :], st_in[l, sn // 4, sn % 4], writes=[r_Ss[sn % 4]], sem="sst%d" % (sn % 4))
                        I("act", "activation", S_b, S_f, AF.Copy, reads=[r_Ss[p4]], writes=[r_Ssb[p4]])
                        rSf, rSb = [r_Ss[p4]], [r_Ssb[p4]]
                    else:
                        S_f, S_b = Sf[0:K, hd, :], Sb[0:K, hd, :]
                        rSf, rSb = [r_Sf[hd]], [r_Sb[hd]]
                    q_c = qb[0:K, hd, tl:tl + csz]
                    k_c = kb[0:K, hd, tl:tl + csz]
                    ev_ = hd % 2 == 0
                    ab, tbk, obk, pk = (4, 6, 5, 7) if ev_ else (2, 3, 0, 1)
                    I("pe", "matmul", banks[ab][0:csz, 0:csz], k_c, q_c, start=True, stop=True,
                      reads=RT(0, 4), writes=rb(ab))
                    pbf = banks[tbk][:, :].bitcast(BF16)
                    I("pe", "transpose", pbf[0:csz, 0:K], k_c, ident_b[0:K, 0:K], reads=RT(2, 2) + [r_identb], writes=rb(tbk))
                    yield
                    I("dve", "tensor_tensor", attm[pa][0:csz, 0:csz], banks[ab][0:csz, 0:csz], mask64[0:csz, 0:csz], ALU.mult,
                      reads=rb(ab) + [r_mask], writes=[r_attm[pa]])
                    I("act", "activation", kbTs[pa][0:csz, 0:K], pbf[0:csz, 0:K], AF.Copy, reads=rb(tbk), writes=[r_kbT[pa]])
                    yield
                    I("pe", "matmul", banks[obk][:, 0:csz], S_b, q_c, start=True, stop=False,
                      reads=rSb + RT(0, 2), writes=rb(obk))
                    I("pe", "matmul", banks[obk][:, 0:csz], vtok[vi][0:csz, hd * 128:(hd + 1) * 128], attm[pa][0:csz, 0:csz],
                      start=False, stop=True, reads=[r_vtok[vi], r_attm[pa]], writes=rb(obk))
                    I("pe", "matmul", banks[pk][0:K, 0:128], kbTs[pa][0:csz, 0:K], vtok[vi][0:csz, hd * 128:(hd + 1) * 128],
                      start=True, stop=True, reads=[r_kbT[pa], r_vtok[vi]], writes=rb(pk))
                    yield
                    I("act", "activation", oT[:, hd, tl:tl + csz], banks[obk][:, 0:csz], AF.Copy, reads=rb(obk), writes=RT(4, 4))
                    tm = tmpS[pa][0:K, :]
                    I("dve", "tensor_tensor", tm, banks[pk][0:K, 0:128], S_f, ALU.add, reads=rb(pk) + rSf, writes=[r_tmp[pa]])
                    yield
                    I("act", "activation", S_f, tm, AF.Copy, scale=elast[0:K, hd, ch:ch + 1], reads=[r_tmp[pa], r_el], writes=rSf)
                    if not smp:
                        I("dve", "tensor_scalar", S_b, tm, elast[0:K, hd, ch:ch + 1], None, ALU.mult, reads=[r_tmp[pa], r_el], writes=rSb)
                    if smp:
                        P.dma("sp", o_s[l, ch, hd], S_f, reads=rSf, sem="o_ss%d" % p4, final=True)
                    elif ci == 3 and ch == nch - 1:
                        P.dma("sp", o_p[l, hd], S_f, reads=rSf, sem="o_sp", final=True)
                    yield

                for ch in range(nch):
                    t0g = c0 + ch * csz
                    vi = vcount["i"] % 2
                    vcount["i"] += 1
                    bv = BV[vi]
                    for kt in range(8):
                        I("pe", "matmul", banks[bv][0:csz, 0:512], hT[:, kt, t0g:t0g + csz], vv[:, kt, :], start=(kt == 0), stop=(kt == 7),
                          reads=[rv, r_hT[ci]], writes=rb(bv))
                    I("act", "activation", vtok[vi][0:csz, :], banks[bv][0:csz, 0:512], AF.Copy, reads=rb(bv), writes=[r_vtok[vi]])
                    u0 = ucount["i"]
                    ucount["i"] += 4
                    interleave([unit(ch, hd, vi, u0 + hd) for hd in (0, 1)])
                    interleave([unit(ch, hd, vi, u0 + hd) for hd in (2, 3)])
                for hd in range(4):
                    rstd_tile([oT[:, hd, 0:n]], RT(4, 4), n, 1.0 / 128.0, (20,), 21)
                    bz = bank() % 2 + 2
                    proj(vz, rz, hd * 128, 128, ci, bz)
                    I("act", "activation", T(20, n), banks[bz][:, 0:n], AF.Silu, reads=rb(bz, 0, n), writes=RT(20))
                    I("dve", "scalar_tensor_tensor", T(19, n), oT[:, hd, 0:n], gn[:, l:l + 1], T(21, n), ALU.mult, ALU.mult,
                      reads=RT(4, 4) + RT(21) + [r_par], writes=RT(19))
                    I("dve", "tensor_tensor", yT[:, hd, c0:c0 + n], T(19, n), T(20, n), ALU.mult, reads=RT(19) + RT(20), writes=[r_yT[ci]])
            touch()
            for h_ in handles:
                ws.release(h_)

        def merge_branch(l, b):
            wl = w_in[l].rearrange("(k p) c -> p k c", p=128)
            wbl = w_branch[l, b].rearrange("(k p) c -> p k c", p=128)
            wol = w_out[l].rearrange("(k p) c -> p k c", p=128)
            mb = TB(0, 2048, 0, 2).rearrange("p (k n) -> p k n", k=4)
            for hf in range(2):
                g0 = M_G + b * 1024 + hf * 512
                (vg,), rg, ig = ws.take([wl[:, :, g0:g0 + 512]])
                (vb,), rbw, ib = ws.take([wbl[:, :, hf * 512:(hf + 1) * 512]])
                (vo,), ro, io = ws.take([wol[:, hf * 4:(hf + 1) * 4, :]])
                for ci, (c0, n) in enumerate(TC):
                    for dt in range(4):
                        b1, b2 = bank(), bank()
                        for kt in range(4):
                            I("pe", "matmul", banks[b1][:, 0:n], vb[:, kt, dt * 128:(dt + 1) * 128], yT[:, kt, c0:c0 + n],
                              start=(kt == 0), stop=(kt == 3), reads=[rbw, r_yT[ci]], writes=rb(b1, 0, n))
                        proj(vg, rg, dt * 128, 128, ci, b2)
                        I("act", "activation", T(2 + dt % 2, n), banks[b2][:, 0:n], AF.Sigmoid, reads=rb(b2, 0, n), writes=RT(2 + dt % 2))
                        I("dve", "tensor_tensor", mb[:, dt, 0:n], banks[b1][:, 0:n], T(2 + dt % 2, n), ALU.mult,
                          reads=rb(b1, 0, n) + RT(2 + dt % 2), writes=RT(0, 2))
                    for od in range(8):
                        b3 = bank()
                        for kt in range(4):
                            I("pe", "matmul", banks[b3][:, 0:n], vo[:, kt, od * 128:(od + 1) * 128], mb[:, kt, 0:n],
                              start=(kt == 0), stop=(kt == 3), reads=[ro] + RT(0, 2), writes=rb(b3, 0, n))
                        I("dve", "tensor_tensor", xT[:, od, c0:c0 + n], banks[b3][:, 0:n], xT[:, od, c0:c0 + n], ALU.add,
                          reads=rb(b3, 0, n) + [r_xT[ci]], writes=[r_xT[ci]])
                ws.release(ig)
                ws.release(ib)
                ws.release(io)

        def final_out():
            for ci, (c0, n) in enumerate(TC):
                rstd_tile([xT[:, kt, c0:c0 + n] for kt in range(8)], [r_xT[ci]], n, 1.0 / D, (0, 1), 2)
                for sub in range(n // 128):
                    t0 = c0 + sub * 128
                    a0 = 18 + (sub % 2) * 2
                    ob = AR[:, a0 * 512:(a0 + 2) * 512]
                    rob = RT(a0, 2)
                    for h in range(2):
                        bi = bank()
                        for q in range(4):
                            kt = h * 4 + q
                            w_ = 3 + (kt % 2)
                            I("dve", "scalar_tensor_tensor", T(w_, 128), xT[:, kt, t0:t0 + 128], fnorm[:, kt:kt + 1],
                              T(2, 128, sub * 128), ALU.mult, ALU.mult, reads=[r_xT[ci], r_par] + RT(2), writes=RT(w_))
                            I("pe", "transpose", banks[bi][:, q * 128:(q + 1) * 128], T(w_, 128), ident_f[:],
                              reads=RT(w_) + [r_identf], writes=rb(bi))
                        if h == 0:
                            I("act", "activation", ob[:, 0:512], banks[bi][:, :], AF.Copy, reads=rb(bi), writes=rob)
                        else:
                            I("dve", "tensor_copy", ob[:, 512:1024], banks[bi][:, :], reads=rb(bi), writes=rob)
                    dst = o_yp[t0:t0 + 128, :] if ci < 4 else o_ys
                    P.dma("sp", dst, ob, reads=rob, sem="o_y%d" % (sub % 2), final=True)

        def emit_all():
            rr["i"] = 0
            P.mark("setup")
            setup()
            for l in range(DEPTH):
                P.mark("L%d norm" % l)
                rmsnorm_h(l)
                if stage in (1, 99):
                    P.mark("L%d A" % l)
                    branch_a(l)
                    if stage == 99:
                        P.mark("L%d mergeA" % l)
                        merge_branch(l, 0)
                if stage in (2, 99):
                    P.mark("L%d S5" % l)
                    branch_s5(l)
                    if stage == 99:
                        P.mark("L%d mergeS5" % l)
                        merge_branch(l, 1)
                if stage in (3, 99):
                    P.mark("L%d hgrn" % l)
                    branch_gated(l, "hgrn")
                    if stage == 99:
                        P.mark("L%d mergeH" % l)
                        merge_branch(l, 2)
                if stage in (4, 99):
                    P.mark("L%d gla" % l)
                    branch_gated(l, "gla")
                    if stage == 99:
                        P.mark("L%d mergeG" % l)
                        merge_branch(l, 3)
                if stage != 99:
                    break
            P.mark("final")
            final_out()
            P.mark("end")

        P.dry = True
        emit_all()
        P.dry = False
        ws.reset()
        emit_all()
        P.emit()
        if os.environ.get("MK_MARKS"):
            import json
            json.dump(P.marks, open(os.environ["MK_MARKS"], "w"))
        print("[kernel] instr per engine:", {e: len(s) for e, s in P.streams.items()},
              "sbuf left:", nc.sbuf_bytes_remaining, "sems:", len(P.sems), flush=True)
    return nc


_CONSTS = None


def _consts():
    global _CONSTS
    if _CONSTS is None:
        cm = np.ones((128, 640), np.float32)
        cm[:, 0:512:64] = 0.0
        cm[:, 512:640:8] = 0.0
        s = np.arange(64)
        _CONSTS = {
            "c_ident": np.eye(128, dtype=np.float32),
            "c_ones": np.ones((128, 128), np.float32),
            "c_mask64": (s[:, None] <= s[None, :]).astype(np.float32),
            "c_cm": cm,
            "c_tp1": np.broadcast_to(np.arange(1, 257, dtype=np.float32), (128, 256)).copy(),
        }
    return _CONSTS


_WNAMES = ["norm_w", "w_in", "conv_w", "s5_a_re", "s5_a_im", "s5_log_dt", "s5_b_re", "s5_b_im",
           "s5_c_re", "s5_c_im", "s5_d", "w_glu", "b_glu", "hgrn_lb_raw", "hgrn_norm", "w_gk", "b_gk",
           "gla_norm", "w_branch", "w_out", "final_norm"]


def kernel(**inputs):
    global LAST_RESULTS
    stage = int(os.environ.get("MK_STAGE", "99"))
    nc = build_program(stage)
    f = lambda a: np.ascontiguousarray(np.asarray(a, dtype=np.float32))
    shared = {k: f(inputs[k]) for k in _WNAMES}
    shared.update(_consts())
    x_prompt = f(inputs["x_prompt"]); x_sample = f(inputs["x_sample"])
    in_maps = []
    for c in range(8):
        sq = slice(c * NSEQ, (c + 1) * NSEQ)
        m = dict(shared)
        m["xp"] = x_prompt[c]
        m["xs"] = x_sample[sq].reshape(NS_, D)
        m["st_conv"] = f(inputs["state_conv"][:, sq])
        m["st_re"] = f(inputs["state_ssm_re"][:, sq])
        m["st_im"] = f(inputs["state_ssm_im"][:, sq])
        m["st_hg"] = f(inputs["state_hgrn"][:, sq])
        m["st_gla"] = f(inputs["state_gla"][:, sq])
        in_maps.append(m)
    res = run_bass_kernel_spmd(nc, in_maps, core_ids=list(range(8)))
    R = res.results
    LAST_RESULTS = R
    cat_p = lambda k, shp: np.stack([R[c][k].reshape(shp) for c in range(8)], axis=1)
    cat_s = lambda k, shp: np.concatenate([R[c][k].reshape(shp) for c in range(8)], axis=1)
    y_prompt = np.stack([R[c]["o_yp"] for c in range(8)], axis=0)
    y_sample = np.concatenate([R[c]["o_ys"].reshape(NSEQ, 8, D) for c in range(8)], axis=0)
    return (y_prompt, y_sample,
            cat_p("o_conv_p", (DEPTH, 2, WBR)), cat_s("o_conv_s", (DEPTH, NSEQ, 2, WBR)),
            cat_p("o_re_p", (DEPTH, 32, 64)), cat_s("o_re_s", (DEPTH, NSEQ, 32, 64)),
            cat_p("o_im_p", (DEPTH, 32, 64)), cat_s("o_im_s", (DEPTH, NSEQ, 32, 64)),
            cat_p("o_hg_p", (DEPTH, 4, 128, 128)), cat_s("o_hg_s", (DEPTH, NSEQ, 4, 128, 128)),
            cat_p("o_gla_p", (DEPTH, 4, 64, 128)), cat_s("o_gla_s", (DEPTH, NSEQ, 4, 64, 128)))
```
